# Optimizing a Trainium2 kernel written in Bass

```python
import math
import jax, jax.numpy as jnp
from jax import lax
import numpy as np

D_MODEL = 4096
BATCH = 4
SEQ = 2048
DEPTH = 2
DEC_BATCH = 32
DEC_SEQ = 1
PAST_LEN = 16384
PAGE_SIZE = 128

N_A_LAYERS = DEPTH // 2
N_B_LAYERS = DEPTH - N_A_LAYERS
M_HEADS = 8
M_QK_DIM = D_MODEL // (2 * M_HEADS)
M_V_DIM = D_MODEL // M_HEADS
M_CHUNK = 64
GATE_CAP = 15.0
HEAD_DIM = 64
A_HEADS = D_MODEL // HEAD_DIM
KV_HEADS = 8
GROUP = A_HEADS // KV_HEADS
WINDOW = 128
ROPE_THETA = 10000.0
D_FF = ((8 * D_MODEL + 3 * 256 - 1) // (3 * 256)) * 256
CONV_W = 3
EPS = 1e-6

kernel_name = "yoco_mlstm_swa_sink_convffn_step"


def rmsnorm(x, g):
    xf = x.astype(jnp.float32)
    y = xf * lax.rsqrt(jnp.mean(xf * xf, axis=-1, keepdims=True) + EPS)
    return (y * g.astype(jnp.float32)).astype(x.dtype)


def modulate(x, g, shift, scale):
    return rmsnorm(x, g) * (1 + scale[:, None, :]) + shift[:, None, :]


def rope(x, pos):
    half = x.shape[-1] // 2
    freq = ROPE_THETA ** (-jnp.arange(half, dtype=jnp.float32) / half)
    ang = pos.astype(jnp.float32)[:, None] * freq[None, :]
    cos = jnp.cos(ang)[None, :, None, :]
    sin = jnp.sin(ang)[None, :, None, :]
    xf = x.astype(jnp.float32)
    x1, x2 = xf[..., :half], xf[..., half:]
    return jnp.concatenate([x1 * cos - x2 * sin, x2 * cos + x1 * sin], axis=-1).astype(x.dtype)


def softcap(z):
    return GATE_CAP * jnp.tanh(z / GATE_CAP)


def mlstm_recurrence(q, k, v, ig, lf, C0, n0, m0):
    B, T, H, dk = q.shape
    L = M_CHUNK if T % M_CHUNK == 0 else T
    nc = T // L

    def to_chunks(a):
        a = a.reshape((B, nc, L) + a.shape[2:])
        return jnp.swapaxes(jnp.moveaxis(a, 1, 0), 2, 3)

    causal = jnp.tril(jnp.ones((L, L), dtype=bool))

    def step(carry, inp):
        C, n, m = carry
        qc, kc, vc, ic, fc = inp
        b = jnp.cumsum(fc, axis=-1)
        d = b[..., :, None] - b[..., None, :] + ic[..., None, :]
        d = jnp.where(causal, d, -jnp.inf)
        inter = b + m[..., None]
        mt = jnp.maximum(inter, jnp.max(d, axis=-1))
        w = jnp.exp(d - mt[..., None])
        si = jnp.exp(inter - mt)
        qk = jnp.einsum('bhtd,bhsd->bhts', qc, kc) * w
        num = jnp.einsum('bhts,bhsv->bhtv', qk, vc) + si[..., None] * jnp.einsum('bhtd,bhdv->bhtv', qc, C)
        den = jnp.sum(qk, axis=-1) + si * jnp.einsum('bhtd,bhd->bht', qc, n)
        h = num / jnp.maximum(jnp.abs(den), jnp.exp(-mt))[..., None]
        gl = b[..., -1:] - b + ic
        m_new = jnp.maximum(b[..., -1] + m, jnp.max(gl, axis=-1))
        wl = jnp.exp(gl - m_new[..., None])
        sd = jnp.exp(b[..., -1] + m - m_new)
        C_new = sd[..., None, None] * C + jnp.einsum('bhs,bhsd,bhsv->bhdv', wl, kc, vc)
        n_new = sd[..., None] * n + jnp.einsum('bhs,bhsd->bhd', wl, kc)
        return (C_new, n_new, m_new), h

    (C, n, m), hs = lax.scan(step, (C0, n0, m0),
                             (to_chunks(q), to_chunks(k), to_chunks(v), to_chunks(ig), to_chunks(lf)))
    hs = jnp.swapaxes(jnp.moveaxis(hs, 0, 1), 2, 3).reshape(B, T, H, v.shape[-1])
    return hs, C, n, m


def mlstm_mixer(h, w_in, b_i, b_f, g_head, w_out, C0, n0, m0):
    B, T, _ = h.shape
    qd = M_HEADS * M_QK_DIM
    vd = M_HEADS * M_V_DIM
    z = h @ w_in
    q, k, v, o, gi, gf = jnp.split(z, [qd, 2 * qd, 2 * qd + vd, 2 * qd + 2 * vd, 2 * qd + 2 * vd + M_HEADS], axis=-1)
    q = q.reshape(B, T, M_HEADS, M_QK_DIM).astype(jnp.float32) * (M_QK_DIM ** -0.5)
    k = k.reshape(B, T, M_HEADS, M_QK_DIM).astype(jnp.float32)
    v = v.reshape(B, T, M_HEADS, M_V_DIM).astype(jnp.float32)
    ig = softcap(gi.astype(jnp.float32) + b_i.astype(jnp.float32))
    lf = jax.nn.log_sigmoid(softcap(gf.astype(jnp.float32) + b_f.astype(jnp.float32)))
    hs, C, n, m = mlstm_recurrence(q, k, v, ig, lf, C0.astype(jnp.float32), n0.astype(jnp.float32), m0.astype(jnp.float32))
    hs = hs * lax.rsqrt(jnp.mean(hs * hs, axis=-1, keepdims=True) + EPS)
    hs = hs.reshape(B, T, D_MODEL) * g_head.astype(jnp.float32) * jax.nn.sigmoid(o.astype(jnp.float32))
    return hs.astype(h.dtype) @ w_out, (C, n, m)


def shared_kv(x, cs, pos, w_ada_kv, g_kv, w_kv, b_kv):
    B, T, _ = x.shape
    sh, sc = jnp.split(cs @ w_ada_kv, 2, axis=-1)
    hn = modulate(x, g_kv, sh, sc)
    k, v = jnp.split(hn @ w_kv + b_kv, 2, axis=-1)
    k = rope(k.reshape(B, T, KV_HEADS, HEAD_DIM), pos)
    v = v.reshape(B, T, KV_HEADS, HEAD_DIM)
    return k, v


def sink_attention(q, k, v, valid, sinks):
    s = jnp.einsum('bnqhgd,bnshd->bnhgqs', q, k, preferred_element_type=jnp.float32) * (HEAD_DIM ** -0.5)
    s = jnp.where(valid[None, :, None, None], s, -jnp.inf)
    sk = sinks.astype(jnp.float32).reshape(KV_HEADS, GROUP)[None, None, :, :, None, None]
    mx = jnp.maximum(jnp.max(s, axis=-1, keepdims=True), sk)
    p = jnp.exp(s - mx)
    p = p / (jnp.sum(p, axis=-1, keepdims=True) + jnp.exp(sk - mx))
    return jnp.einsum('bnhgqs,bnshd->bnqhgd', p.astype(v.dtype), v)


def window_attention(q, k_new, v_new, kbuf, vbuf, sinks):
    B, T, H, hd = q.shape
    if kbuf is None:
        nb = T // WINDOW
        qb = q.reshape(B, nb, WINDOW, KV_HEADS, GROUP, hd)
        kc = k_new.reshape(B, nb, WINDOW, KV_HEADS, hd)
        vc = v_new.reshape(B, nb, WINDOW, KV_HEADS, hd)
        pad = ((0, 0), (1, 0), (0, 0), (0, 0), (0, 0))
        kb = jnp.concatenate([jnp.pad(kc, pad)[:, :-1], kc], axis=2)
        vb = jnp.concatenate([jnp.pad(vc, pad)[:, :-1], vc], axis=2)
        i = jnp.arange(WINDOW)[:, None]
        j = jnp.arange(2 * WINDOW)[None, :]
        diff = i + WINDOW - j
        blk = jnp.arange(nb)[:, None, None]
        valid = (diff >= 0) & (diff < WINDOW) & ((blk > 0) | (j >= WINDOW))
        o = sink_attention(qb, kb, vb, valid, sinks)
    else:
        wc = kbuf.shape[1]
        k_ctx = jnp.concatenate([kbuf.astype(k_new.dtype), k_new], axis=1)
        v_ctx = jnp.concatenate([vbuf.astype(v_new.dtype), v_new], axis=1)
        i = jnp.arange(T)[:, None]
        j = jnp.arange(wc + T)[None, :]
        diff = i + wc - j
        valid = ((diff >= 0) & (diff < WINDOW))[None]
        o = sink_attention(q.reshape(B, 1, T, KV_HEADS, GROUP, hd), k_ctx[:, None], v_ctx[:, None], valid, sinks)
    return o.reshape(B, T, H * hd)


def conv_ffn(h, buf, w_in, w_conv, b_conv, w_out):
    T = h.shape[1]
    u = h @ w_in
    ext = jnp.concatenate([buf.astype(u.dtype), u], axis=1)
    y = b_conv
    for j in range(CONV_W):
        y = y + ext[:, j:j + T] * w_conv[j]
    gate, up = jnp.split(y, 2, axis=-1)
    return (jax.nn.silu(gate) * up) @ w_out, ext[:, T:]


def setup_inputs(seed: int = 0) -> dict:
    key = jax.random.key(seed)
    ks = iter(jax.random.split(key, 40))

    def nrm(shape, s):
        return jax.random.normal(next(ks), shape, jnp.float32) * s

    D, F = D_MODEL, D_FF
    qd, vd = M_HEADS * M_QK_DIM, M_HEADS * M_V_DIM
    m_in = 2 * qd + 2 * vd + 2 * M_HEADS
    cache_w = min(WINDOW, PAST_LEN)
    kvd = KV_HEADS * HEAD_DIM
    return {
        "x_prompt": nrm((BATCH, SEQ, D), 1.0),
        "x_sample": nrm((DEC_BATCH, DEC_SEQ, D), 1.0),
        "state_mlstm_C": nrm((N_A_LAYERS, DEC_BATCH, M_HEADS, M_QK_DIM, M_V_DIM), 0.3),
        "state_mlstm_n": nrm((N_A_LAYERS, DEC_BATCH, M_HEADS, M_QK_DIM), 0.3),
        "state_mlstm_m": nrm((N_A_LAYERS, DEC_BATCH, M_HEADS), 1.0),
        "cache_conv": nrm((DEPTH, DEC_BATCH, CONV_W - 1, 2 * F), 1.0),
        "cache_k_win": nrm((DEC_BATCH, cache_w, KV_HEADS, HEAD_DIM), 1.0),
        "cache_v_win": nrm((DEC_BATCH, cache_w, KV_HEADS, HEAD_DIM), 1.0),
        "c_prompt": nrm((BATCH, D), 1.0),
        "c_sample": nrm((DEC_BATCH, D), 1.0),
        "w_ada": nrm((DEPTH, D, 6 * D), 0.3 * D ** -0.5),
        "g_norm1": 1.0 + nrm((DEPTH, D), 0.02),
        "g_norm2": 1.0 + nrm((DEPTH, D), 0.02),
        "w_m_in": nrm((N_A_LAYERS, D, m_in), D ** -0.5),
        "b_m_i": nrm((N_A_LAYERS, M_HEADS), 0.5),
        "b_m_f": 3.0 + nrm((N_A_LAYERS, M_HEADS), 0.5),
        "g_m_head": 1.0 + nrm((N_A_LAYERS, D), 0.02),
        "w_m_out": nrm((N_A_LAYERS, vd, D), vd ** -0.5),
        "w_ada_kv": nrm((D, 2 * D), 0.3 * D ** -0.5),
        "g_kv": 1.0 + nrm((D,), 0.02),
        "w_kv": nrm((D, 2 * kvd), D ** -0.5),
        "b_kv": nrm((2 * kvd,), 0.01),
        "w_q": nrm((N_B_LAYERS, D, A_HEADS * HEAD_DIM), D ** -0.5),
        "b_q": nrm((N_B_LAYERS, A_HEADS * HEAD_DIM), 0.01),
        "sinks": nrm((N_B_LAYERS, A_HEADS), 1.0),
        "w_o": nrm((N_B_LAYERS, A_HEADS * HEAD_DIM, D), (A_HEADS * HEAD_DIM) ** -0.5),
        "b_o": nrm((N_B_LAYERS, D), 0.01),
        "w_ffn_in": nrm((DEPTH, D, 2 * F), D ** -0.5),
        "w_conv": nrm((DEPTH, CONV_W, 2 * F), CONV_W ** -0.5),
        "b_conv": nrm((DEPTH, 2 * F), 0.01),
        "w_ffn_out": nrm((DEPTH, F, D), F ** -0.5),
        "g_final": 1.0 + nrm((D,), 0.02),
    }


def reference(x_prompt, x_sample, state_mlstm_C, state_mlstm_n, state_mlstm_m, cache_conv, cache_k_win, cache_v_win, c_prompt, c_sample, w_ada, g_norm1, g_norm2, w_m_in, b_m_i, b_m_f, g_m_head, w_m_out, w_ada_kv, g_kv, w_kv, b_kv, w_q, b_q, sinks, w_o, b_o, w_ffn_in, w_conv, b_conv, w_ffn_out, g_final):
    def run(x, c, pos, C0, n0, m0, conv0, kbuf, vbuf):
        B, T, _ = x.shape
        cs = jax.nn.silu(c.astype(jnp.float32)).astype(x.dtype)
        new_C, new_n, new_m, new_conv = [], [], [], []
        k_new = v_new = k_out = v_out = None
        for l in range(DEPTH):
            sh1, sc1, ga1, sh2, sc2, ga2 = jnp.split(cs @ w_ada[l], 6, axis=-1)
            if l < N_A_LAYERS:
                hn = modulate(x, g_norm1[l], sh1, sc1)
                out, (C, n, m) = mlstm_mixer(hn, w_m_in[l], b_m_i[l], b_m_f[l], g_m_head[l], w_m_out[l], C0[l], n0[l], m0[l])
                new_C.append(C)
                new_n.append(n)
                new_m.append(m)
            else:
                bl = l - N_A_LAYERS
                if bl == 0:
                    k_new, v_new = shared_kv(x, cs, pos, w_ada_kv, g_kv, w_kv, b_kv)
                    if kbuf is None:
                        wc = min(WINDOW, T)
                        k_out, v_out = k_new[:, -wc:], v_new[:, -wc:]
                    else:
                        wc = kbuf.shape[1]
                        k_out = jnp.concatenate([kbuf.astype(k_new.dtype), k_new], axis=1)[:, -wc:]
                        v_out = jnp.concatenate([vbuf.astype(v_new.dtype), v_new], axis=1)[:, -wc:]
                hn = modulate(x, g_norm1[l], sh1, sc1)
                q = rope((hn @ w_q[bl] + b_q[bl]).reshape(B, T, A_HEADS, HEAD_DIM), pos)
                out = window_attention(q, k_new, v_new, kbuf, vbuf, sinks[bl]) @ w_o[bl] + b_o[bl]
            x = x + ga1[:, None, :] * out
            hn = modulate(x, g_norm2[l], sh2, sc2)
            f, cb = conv_ffn(hn, conv0[l], w_ffn_in[l], w_conv[l], b_conv[l], w_ffn_out[l])
            new_conv.append(cb)
            x = x + ga2[:, None, :] * f
        y = rmsnorm(x, g_final)
        return y, jnp.stack(new_C), jnp.stack(new_n), jnp.stack(new_m), jnp.stack(new_conv), k_out, v_out

    Bp, Tp, _ = x_prompt.shape
    C0p = jnp.zeros((N_A_LAYERS, Bp, M_HEADS, M_QK_DIM, M_V_DIM), jnp.float32)
    n0p = jnp.zeros((N_A_LAYERS, Bp, M_HEADS, M_QK_DIM), jnp.float32)
    m0p = jnp.zeros((N_A_LAYERS, Bp, M_HEADS), jnp.float32)
    conv0p = jnp.zeros((DEPTH, Bp, CONV_W - 1, 2 * D_FF), x_prompt.dtype)
    y_prompt, C_p, n_p, m_p, conv_p, k_win_p, v_win_p = run(
        x_prompt, c_prompt, jnp.arange(Tp, dtype=jnp.int32), C0p, n0p, m0p, conv0p, None, None)
    pos_s = PAST_LEN + jnp.arange(x_sample.shape[1], dtype=jnp.int32)
    y_sample, C_s, n_s, m_s, conv_s, k_win_s, v_win_s = run(
        x_sample, c_sample, pos_s, state_mlstm_C, state_mlstm_n, state_mlstm_m, cache_conv, cache_k_win, cache_v_win)
    return (y_prompt, y_sample, C_p, n_p, m_p, conv_p, k_win_p, v_win_p, C_s, n_s, m_s, conv_s, k_win_s, v_win_s)
```

```python
import contextlib
import numpy as np
import concourse.bass as bass
import concourse.mybir as mybir
from concourse.bass_utils import run_bass_kernel_spmd

F32 = mybir.dt.float32
BF16 = mybir.dt.bfloat16
AF = mybir.ActivationFunctionType
ALU = mybir.AluOpType
AX = mybir.AxisListType

D = 4096
KC = 32
FF = 11008
FC = 86
NSL = 172
H = 8
NPRE = 896
NFULL = 1152
NSMP = 4
BLK = 128
EPS = 1e-6
FGRP = [(0, 22), (22, 22), (44, 21), (65, 21)]
NCMAX = 388
NSLOT = 3
ARENA_BYTES = 57000

ENGS = ("pe", "act", "dve", "pool", "sp")


class Buf:
    __slots__ = ("w", "r")

    def __init__(self):
        self.w = {}
        self.r = {}


class Prog:
    def __init__(self, nc, es):
        self.nc = nc
        self.es = es
        self.q = {e: [] for e in ENGS}
        self.waited = {e: {} for e in ENGS}
        self.cnt = {}
        self.sems = {}
        for e in ("pe", "act", "dve", "pool"):
            self.new_sem(e)
        self.rr = 0

    def new_sem(self, key):
        self.sems[key] = self.es.enter_context(self.nc.semaphore("s_" + key))
        self.cnt[key] = 0
        return key

    @staticmethod
    def _deps(reads, writes):
        d = {}
        for b in reads:
            for k, v in b.w.items():
                if d.get(k, 0) < v:
                    d[k] = v
        for b in writes:
            for k, v in b.w.items():
                if d.get(k, 0) < v:
                    d[k] = v
            for k, v in b.r.items():
                if d.get(k, 0) < v:
                    d[k] = v
        return d

    def _record(self, eng, deps, fn, semkey, amount, reads, writes, skip_self=False):
        waits = []
        wd = self.waited[eng]
        for k, v in deps.items():
            if skip_self and k == eng:
                continue
            if wd.get(k, 0) < v:
                wd[k] = v
                waits.append((k, v))
        self.cnt[semkey] += amount
        val = self.cnt[semkey]
        for b in writes:
            b.w = {semkey: val}
            b.r = {}
        for b in reads:
            b.r[semkey] = val
        self.q[eng].append((waits, fn, semkey, amount))

    def op(self, eng, method, reads, writes, **kw):
        self._record(eng, self._deps(reads, writes), lambda h: getattr(h, method)(**kw), eng, 1,
                     reads, writes, skip_self=(eng == "pe"))

    def mm(self, out, pairs, reads, writes):
        n = len(pairs)

        def fn(h):
            ins = None
            for i, (l, r) in enumerate(pairs):
                ins = h.matmul(out, lhsT=l, rhs=r, start=(i == 0), stop=(i == n - 1))
            return ins
        self._record("pe", self._deps(reads, writes), fn, "pe", 1, reads, writes, skip_self=True)

    def dma(self, qeng, semkey, reads, writes, **kw):
        deps = self._deps(reads, writes)
        if deps.get(semkey, 0) < self.cnt[semkey]:
            deps[semkey] = self.cnt[semkey]
        self._record(qeng, deps, lambda h: h.dma_start(**kw), semkey, 16, reads, writes)

    def final_wait(self, eng):
        waits = [(k, v) for k, v in self.cnt.items() if v > 0]
        self.q[eng].append((waits, None, None, 0))

    def emit(self):
        nc = self.nc
        with nc.Block() as block:
            def run(engname):
                def body(h):
                    for waits, fn, semkey, amount in self.q[engname]:
                        for k, v in waits:
                            h.wait_ge(self.sems[k], v)
                        if fn is not None:
                            fn(h).then_inc(self.sems[semkey], amount)
                return body
            block.tensor(run("pe"))
            block.scalar(run("act"))
            block.vector(run("dve"))
            block.gpsimd(run("pool"))
            block.sync(run("sp"))


class _Lazy(dict):
    def __init__(self, b, shapes, kind):
        super().__init__()
        self.b, self.shapes, self.kind = b, shapes, kind

    def __missing__(self, name):
        shp = list(self.shapes[name])
        ov = self.b.dbg.get("shape_" + name)
        if ov is not None:
            shp = list(ov)
        ap = self.b.nc.dram_tensor(name, shp, F32, kind=self.kind).ap()
        self[name] = ap
        return ap


class T:
    __slots__ = ("ap", "b")

    def __init__(self, ap, b=None):
        self.ap = ap
        self.b = b if b is not None else Buf()


class Builder:
    def __init__(self, dbg=None):
        self.nc = bass.Bass("TRN2", target_bir_lowering=False)
        self.es = contextlib.ExitStack()
        self.dbg = dbg or {}

    def dbg_dump(self, name, t, ap=None):
        ap = t.ap if ap is None else ap
        shp = [int(x) for x in ap.shape]
        self.dout.shapes[name] = shp
        self.store(self.dout[name], t, ap)

    def inp(self, name, shape, dt=F32):
        self.din[name] = self.nc.dram_tensor(name, list(shape), dt, kind="ExternalInput").ap()
        return self.din[name]

    def outp(self, name, shape, dt=F32):
        self.dout[name] = self.nc.dram_tensor(name, list(shape), dt, kind="ExternalOutput").ap()
        return self.dout[name]

    def sb(self, name, shape, dt=F32):
        return T(self.es.enter_context(self.nc.sbuf_tensor(name, list(shape), dt))[:])

    def arena_reset(self):
        P = self.P
        ev = {}
        for t in self.arena_live:
            for dct in (t.b.w, t.b.r):
                for k, v in dct.items():
                    if ev.get(k, 0) < v:
                        ev[k] = v
        self.arena_prev = ev
        self.arena_live = []
        self.arena_off = 0

    def carve(self, shape, dt=F32):
        esz = 4 if dt == F32 else 2
        n = 1
        for s in shape[1:]:
            n *= s
        nbytes = (n * esz + 7) // 8 * 8
        off = self.arena_off
        assert off + nbytes <= ARENA_BYTES, ("arena overflow", off, nbytes)
        self.arena_off += nbytes
        ap = self.arena[0:shape[0], off // 2:(off + n * esz) // 2]
        if dt == F32:
            ap = ap.bitcast(F32)
        if len(shape) == 3:
            ap = ap.rearrange("p (a b) -> p a b", a=shape[1])
        elif len(shape) == 4:
            ap = ap.rearrange("p (a b c) -> p a b c", a=shape[1], b=shape[2])
        t = T(ap)
        t.b.r = dict(self.arena_prev)
        self.arena_live.append(t)
        return t

    def wslab(self, src, kcn, ncol=128):
        i = self.wi
        self.wi += 1
        s = i % NSLOT
        slot = self.slots[s]
        view = slot.ap[:, 0:kcn * ncol].rearrange("p (k n) -> p k n", k=kcn)
        self.P.dma("pool", f"w{s}", [], [slot.b], out=view, in_=src)
        return T(view, slot.b)

    def small_load(self, dst, src, q="sp", dram_buf=None, dst_ap=None):
        k = f"d{self.P.rr % 8}"
        self.P.rr += 1
        self.P.dma(q, k, [] if dram_buf is None else [dram_buf], [dst.b], out=(dst.ap if dst_ap is None else dst_ap), in_=src)

    def store(self, dst_dram, src, src_ap=None, q="sp", dram_buf=None):
        k = f"o{self.P.rr % 8}"
        self.P.rr += 1
        self.P.dma(q, k, [src.b], [] if dram_buf is None else [dram_buf], out=dst_dram, in_=(src.ap if src_ap is None else src_ap))

    def build(self):
        nc, es = self.nc, self.es
        with es:
            self.P = P = Prog(nc, es)
            for i in range(NSLOT):
                P.new_sem(f"w{i}")
            for i in range(8):
                P.new_sem(f"d{i}")
                P.new_sem(f"o{i}")
            self.declare_dram()
            self.alloc()
            stage = self.dbg.get("stage", 99)
            self.prologue()
            if stage >= 1:
                self.adaln()
            if "dump_mod" in self.dbg:
                self.dbg_dump("dbg_modT", self.modT)
                self.dbg_dump("dbg_Amod", self.Amod)
            self.first_state = True
            if stage >= 2:
                for ti, nb in enumerate((3, 3, 1)[:self.dbg.get("npre", 3)]):
                    self.run_tile("pre", ti, nb, ti * 384, False)
            if "dump_pre" in self.dbg:
                self.dbg_dump("dbg_nst", self.nst); self.dbg_dump("dbg_mst", self.mst)
                self.dout.shapes["dbg_C"] = [H, 128, 2, 512]
                for h in range(H):
                    self.P.dma("sp", f"o{h}", [self.cscr_b[h]], [], out=self.dout["dbg_C"][h], in_=self.C_scr[h])
            if stage >= 3:
                for ti in range(self.dbg.get("nmain", 3)):
                    self.run_tile("main", ti, 3, ti * 384, ti == 0)
            P.final_wait("sp")
            P.emit()
        return nc

    IN_SHAPES = {
        "xpre": [128, KC, NPRE], "xfull": [128, KC, NFULL], "xs": [128, KC, NSMP], "c5T": [128, KC, 5], "flag": [128, 1],
        "rope": [3, 128, 2, NCMAX], "st_C": [4, H, 128, 2, 512], "st_n": [128, H, 2, 4], "st_m": [8, 4],
        "cconv": [2, 128, NSL, 4, 2], "cconv_r1": [2, 4, 2 * FF], "ckT": [4, 128, 8, 127], "ck_old": [4, 127, 512],
        "cv_old": [4, 127, 512], "w_ada_s": [448, 128, KC, 128], "w_min_s": [96, 128, KC, 128], "w_gate": [128, KC, 16],
        "w_mout_s": [2, 32, 128, 16, 128], "w_ffi_s": [2, NSL, 128, KC, 128], "w_ffo_s": [2, 4, 32, 128, 22, 128],
        "w_kv_s": [12, 128, KC, 128], "w_q_s": [32, 128, KC, 128], "w_o_s": [32, 128, KC, 128],
        "gT": [128, 6, KC], "gheadT": [128, KC], "bif": [8, 2], "bkv": [128, 12], "bqT": [128, KC], "boT": [128, KC],
        "nsinks_bc": [128, 64], "sinks_bc": [128, 64], "sinksT": [64, 2], "wconvT": [128, 2, NSL, 3], "bconvT": [128, 2, NSL],
        "ident": [128, 128], "Rmat": [128, 128], "cmask": [128, 128], "amask": [2, 128, 256], "bm": [8, 8, 4], "mg": [64, 16]}
    OUT_SHAPES = {
        "yT": [128, KC, NFULL], "ysT": [128, KC, NSMP], "C_out": [H, 128, 2, 512], "n_out": [128, H, 2], "m_out": [8, 1],
        "convp": [2, 128, NSL, 2], "kwinp": [64, 8, 128], "vwinp": [128, 4, 128], "Cs_out": [4, H, 128, 2, 512],
        "ns_out": [128, H, 2, 4], "ms_out": [8, 4], "convs_old": [2, 4, 2 * FF], "convs_new": [2, 128, NSL, 4],
        "kwins_old": [4, 127, 512], "knew": [64, 8, 4], "vwins_old": [4, 127, 512], "vnew": [4, 512]}

    def declare_dram(self):
        self.din = _Lazy(self, self.IN_SHAPES, "ExternalInput")
        self.dout = _Lazy(self, self.OUT_SHAPES, "ExternalOutput")
        self.C_scr = self.nc.dram_tensor("C_scr", [H, 128, 2, 512], F32).ap()
        self.cscr_b = [Buf() for _ in range(H)]

    def alloc(self):
        sb, nc, es = self.sb, self.nc, self.es
        self.xT = sb("xT", [128, KC, NCMAX]); self.hn = sb("hn", [128, KC, NCMAX], BF16)
        self.slots = [sb(f"slot{i}", [128, KC * 128], BF16) for i in range(NSLOT)]
        self.wi = 0
        self.modT = sb("modT", [128, 448, 5]); self.Amod = sb("Amod", [128, 5, KC, 5])
        self.gT = sb("gTs", [128, 6, KC]); self.gheadT = sb("gheadTs", [128, KC])
        self.bif = sb("bifs", [8, 2]); self.bif15 = sb("bif15", [8, 2]); self.bkv = sb("bkvs", [128, 12])
        self.bqT = sb("bqTs", [128, KC]); self.boT = sb("boTs", [128, KC])
        self.nsinks = sb("nsinks", [128, 64]); self.sinks = sb("sinkss", [128, 64]); self.sinksT = sb("sinksTs", [64, 2])
        self.wconvT = sb("wconvTs", [128, 2, NSL, 3]); self.bconvT = sb("bconvTs", [128, 2, NSL])
        self.ident = sb("idents", [128, 128]); self.identb = sb("identb", [128, 128], BF16)
        self.ones = sb("ones", [128, 128]); self.onesb = sb("onesb", [128, 2], BF16)
        self.Rmat = sb("Rmats", [128, 128]); self.cmask = sb("cmasks", [128, 128]); self.amask = sb("amasks", [128, 2, 256])
        self.bm = sb("bms", [8, 8, 4]); self.flag = sb("flags", [128, 1]); self.mg = sb("mgs", [64, 16])
        self.wg = sb("wg", [128, KC, 16], BF16)
        self.ropeT = sb("ropeT", [128, 2, NCMAX])
        self.nst = sb("nst", [128, H, 2]); self.mst = sb("mst", [8, 1])
        self.uhist = sb("uhist", [128, 2, NSL, 2])
        self.kTd = sb("kTd", [128, 8, 128 + NCMAX], BF16); self.vtok = sb("vtok", [128, 4, 512], BF16)
        self.sq = [sb(f"sq{i}", [128, NCMAX]) for i in range(2)]
        self.tmp = [sb(f"tmp{i}", [128, NCMAX]) for i in range(2)]
        self.rstd = sb("rstd", [128, NCMAX]); self.tmps = sb("tmps", [128, 4])
        self.cs32 = sb("cs32", [128, KC, 5]); self.csT = sb("csT", [128, KC, 5], BF16)
        self.epsT = sb("epsT", [128, 1])
        self.arena = es.enter_context(nc.sbuf_tensor("arena", [128, ARENA_BYTES // 2], BF16))[:]
        self.arena_live = []
        self.arena_prev = {}
        self.arena_off = 0
        self.ps = [T(es.enter_context(nc.psum_tensor(f"ps{i}", [128, 512], F32))[:]) for i in range(8)]
        self.pj = 0

    def pbank(self):
        self.pj ^= 1
        return self.ps[2 if self.pj else 7]

    def prologue(self):
        P, L, d = self.P, self.small_load, self.din
        for t, n in ((self.gT, "gT"), (self.gheadT, "gheadT"), (self.bif, "bif"), (self.bkv, "bkv"), (self.bqT, "bqT"),
                     (self.boT, "boT"), (self.nsinks, "nsinks_bc"), (self.sinks, "sinks_bc"), (self.sinksT, "sinksT"),
                     (self.wconvT, "wconvT"), (self.bconvT, "bconvT"), (self.ident, "ident"), (self.Rmat, "Rmat"),
                     (self.cmask, "cmask"), (self.bm, "bm"), (self.flag, "flag"), (self.cs32, "c5T"), (self.mg, "mg")):
            L(t, d[n])
        L(self.amask, d["amask"].rearrange("v p s -> p v s"))
        P.dma("pool", "d0", [], [self.wg.b], out=self.wg.ap, in_=d["w_gate"])
        P.op("dve", "tensor_copy", [self.ident.b], [self.identb.b], out=self.identb.ap, in_=self.ident.ap)
        P.op("dve", "memset", [], [self.ones.b], ap=self.ones.ap, constant=1.0)
        P.op("dve", "memset", [], [self.onesb.b], ap=self.onesb.ap, constant=1.0)
        P.op("dve", "memset", [], [self.epsT.b], ap=self.epsT.ap, constant=EPS)
        P.op("dve", "memset", [], [self.uhist.b], ap=self.uhist.ap, constant=0.0)
        P.op("dve", "memset", [], [self.kTd.b], ap=self.kTd.ap, constant=0.0)
        P.op("dve", "memset", [], [self.vtok.b], ap=self.vtok.ap, constant=0.0)
        P.op("dve", "memset", [], [self.nst.b], ap=self.nst.ap, constant=0.0)
        P.op("dve", "memset", [], [self.mst.b], ap=self.mst.ap, constant=0.0)
        P.op("dve", "tensor_scalar", [self.bif.b], [self.bif15.b], out=self.bif15.ap, in0=self.bif.ap,
             scalar1=1.0 / 15.0, scalar2=None, op0=ALU.mult)
        P.op("act", "activation", [self.cs32.b], [self.csT.b], out=self.csT.ap, in_=self.cs32.ap, func=AF.Silu)
        for src, dst in (("cconv_r1", "convs_old"), ("ck_old", "kwins_old"), ("cv_old", "vwins_old")):
            k = f"o{P.rr % 8}"
            P.rr += 1
            P.dma("sp", k, [], [], out=self.dout[dst], in_=d[src])

    def adaln(self):
        P = self.P
        src = self.din["w_ada_s"]
        for s in range(self.dbg.get("ada_slabs", 448)):
            slab = self.wslab(src[s], KC)
            bank = self.ps[(s // 64) % 2]
            col = (s % 64) * 5
            P.mm(bank.ap[:, col:col + 5], [(slab.ap[:, kc, :], self.csT.ap[:, kc, :]) for kc in range(KC)],
                 [slab.b, self.csT.b], [bank.b])
            if s % 64 == 63:
                s0 = s - 63
                P.op("act", "activation", [bank.b], [self.modT.b],
                     out=self.modT.ap[:, s0:s0 + 64, :].rearrange("p a b -> p (a b)"), in_=bank.ap[:, 0:320], func=AF.Copy)
        for n, (gi, sc, sh) in enumerate(NORMS):
            for q in range(5):
                P.op("dve", "scalar_tensor_tensor", [self.modT.b, self.gT.b], [self.Amod.b],
                     out=self.Amod.ap[:, n, :, q], in0=self.modT.ap[:, sc * 32:(sc + 1) * 32, q], scalar=1.0,
                     in1=self.gT.ap[:, gi, :], op0=ALU.add, op1=ALU.mult)

    def norm(self, n, ncols, npc, has_s, out_final=None):
        P, xT, hn = self.P, self.xT, self.hn
        pn = self.ps[6]
        for kc in range(KC):
            sq = self.sq[kc % 2]
            P.op("act", "activation", [xT.b], [sq.b], out=sq.ap[:, :ncols], in_=xT.ap[:, kc, :ncols], func=AF.Square)
            P.op("pe", "matmul", [sq.b, self.ones.b], [pn.b], out=pn.ap[:, :ncols], lhsT=self.ones.ap, rhs=sq.ap[:, :ncols],
                 start=(kc == 0), stop=(kc == KC - 1))
        P.op("act", "activation", [pn.b, self.epsT.b], [self.rstd.b], out=self.rstd.ap[:, :ncols], in_=pn.ap[:, :ncols],
             func=AF.Sqrt, scale=1.0 / D, bias=self.epsT.ap[:, 0:1])
        P.op("dve", "reciprocal", [self.rstd.b], [self.rstd.b], out=self.rstd.ap[:, :ncols], in_=self.rstd.ap[:, :ncols])
        for kc in range(KC):
            tmp = self.tmp[kc % 2]
            P.op("dve", "tensor_tensor", [xT.b, self.rstd.b], [tmp.b], out=tmp.ap[:, :ncols], in0=xT.ap[:, kc, :ncols],
                 in1=self.rstd.ap[:, :ncols], op=ALU.mult)
            if n == 5:
                P.op("act", "activation", [tmp.b, self.gT.b], [out_final.b], out=out_final.ap[:, kc, :ncols], in_=tmp.ap[:, :ncols],
                     func=AF.Copy, scale=self.gT.ap[:, 5, kc:kc + 1])
                continue
            gi, sc, sh = NORMS[n]
            P.op("act", "activation", [tmp.b, self.Amod.b, self.modT.b], [hn.b], out=hn.ap[:, kc, :npc], in_=tmp.ap[:, :npc],
                 func=AF.Identity, scale=self.Amod.ap[:, n, kc, 0:1], bias=self.modT.ap[:, sh * 32 + kc, 0:1])
            if has_s:
                P.op("dve", "tensor_tensor", [tmp.b, self.Amod.b], [self.tmps.b], out=self.tmps.ap, in0=tmp.ap[:, npc:npc + 4],
                     in1=self.Amod.ap[:, n, kc, 1:5], op=ALU.mult)
                P.op("dve", "tensor_tensor", [self.tmps.b, self.modT.b], [hn.b], out=hn.ap[:, kc, npc:npc + 4], in0=self.tmps.ap,
                     in1=self.modT.ap[:, sh * 32 + kc, 1:5], op=ALU.add)

    def proj(self, src, kcn, rhs_t, ncols, M=128):
        slab = self.wslab(src, kcn)
        bank = self.pbank()
        self.P.mm(bank.ap[0:M, :ncols], [(slab.ap[:, kc, 0:M], rhs_t.ap[:, kc, :ncols]) for kc in range(kcn)],
                  [slab.b, rhs_t.b], [bank.b])
        return bank

    def xupdate(self, bank, c, sec, ncols, npc, has_s, bias=None):
        P, xT, modT = self.P, self.xT, self.modT
        src = bank.ap
        rd = [bank.b]
        if bias is not None:
            t = self.tmp[0]
            P.op("dve", "tensor_scalar", [bank.b, bias.b], [t.b], out=t.ap[:, :ncols], in0=bank.ap[:, :ncols],
                 scalar1=bias.ap[:, c:c + 1], scalar2=None, op0=ALU.add)
            src = t.ap
            rd = [t.b]
        P.op("dve", "scalar_tensor_tensor", rd + [modT.b, xT.b], [xT.b], out=xT.ap[:, c, :npc], in0=src[:, :npc],
             scalar=modT.ap[:, sec * 32 + c, 0:1], in1=xT.ap[:, c, :npc], op0=ALU.mult, op1=ALU.add)
        if has_s:
            P.op("dve", "tensor_tensor", rd + [modT.b], [self.tmps.b], out=self.tmps.ap, in0=src[:, npc:npc + 4],
                 in1=modT.ap[:, sec * 32 + c, 1:5], op=ALU.mult)
            P.op("dve", "tensor_tensor", [self.tmps.b, xT.b], [xT.b], out=xT.ap[:, c, npc:npc + 4], in0=self.tmps.ap,
                 in1=xT.ap[:, c, npc:npc + 4], op=ALU.add)

    def run_tile(self, kind, ti, nblk, col0, has_s):
        P, d = self.P, self.din
        npc = nblk * BLK
        ncols = npc + (4 if has_s else 0)
        self.arena_reset()
        src = d["xpre"] if kind == "pre" else d["xfull"]
        P.dma("sp", "d0", [], [self.xT.b], out=self.xT.ap[:, :, :npc], in_=src[:, :, col0:col0 + npc])
        if has_s:
            P.dma("sp", "d1", [], [self.xT.b], out=self.xT.ap[:, :, npc:npc + 4], in_=d["xs"])
            self.xT.b.w["d0"] = P.cnt["d0"]
        if not self.dbg.get("skip_mixer"):
            self.norm(0, ncols, npc, has_s)
            self.mixer(kind, ti, nblk, ncols, npc, has_s)
        if kind == "pre":
            return
        ms = self.dbg.get("main_stop", 99)
        if ms <= 1:
            return self.dbg_dump(f"dbg_x{ti}", self.xT)
        self.arena_reset()
        if not self.dbg.get("skip_ffn0"):
            self.norm(1, ncols, npc, has_s)
            self.ffn(0, ti, ncols, npc, has_s)
        if ms <= 2:
            return self.dbg_dump(f"dbg_x{ti}", self.xT)
        self.arena_reset()
        self.attention(ti, nblk, ncols, npc, has_s)
        if ms <= 3:
            return self.dbg_dump(f"dbg_x{ti}", self.xT)
        self.arena_reset()
        self.norm(4, ncols, npc, has_s)
        self.ffn(1, ti, ncols, npc, has_s)
        self.arena_reset()
        ybuf = self.carve([128, KC, NCMAX])
        self.norm(5, ncols, npc, has_s, out_final=ybuf)
        self.store(self.dout["yT"][:, :, col0:col0 + npc], ybuf, ybuf.ap[:, :, :npc])
        if has_s:
            self.store(self.dout["ysT"], ybuf, ybuf.ap[:, :, npc:npc + 4])
        if ti == 2:
            for l in range(2):
                self.store(self.dout["convp"][l], self.uhist, self.uhist.ap[:, l])

    def mixer(self, kind, ti, nblk, ncols, npc, has_s):
        P, d, ps, hn = self.P, self.din, self.ps, self.hn
        main = kind == "main"
        C = self.carve
        qT = C([128, 2, NCMAX], BF16); kT = C([128, 2, NCMAX], BF16)
        vT = C([128, 4, NCMAX], BF16); oT = C([128, 4, NCMAX], BF16)
        ktok = [C([128, 256], BF16) for _ in range(2)]
        vtk = [C([128, 512], BF16) for _ in range(2)]
        otk = [C([128, 512], BF16) for _ in range(2)]
        PT = [C([128, 128], BF16) for _ in range(2)]
        Cp = C([128, 2, 512], BF16); npb = C([128, 2], BF16)
        hsg = [C([128, 512], BF16) for _ in range(2)]
        junk = C([128, 512], BF16)
        Cst = [C([128, 2, 512]) for _ in range(2)]
        hsT = C([128, 16, NCMAX], BF16)
        ig = C([8, NCMAX]); lf = C([8, NCMAX]); bt = C([8, NCMAX]); at = C([8, NCMAX]); ut = C([8, NCMAX]); dnf = C([8, NCMAX])
        tokm = C([128, 3, 2, 8]); sdbc = C([128, 8, 4]); Rexp = C([8, 8, 4])
        P.op("dve", "memset", [], [Rexp.b], ap=Rexp.ap, constant=0.0)
        sm = C([8, 16]); cvec = C([8, 4]); negc = C([8, 4]); sdv = C([8, 4])
        col = C([128, 8])
        ones8 = self.ones.ap[0:8, 0:BLK]
        pg = ps[0]
        P.mm(pg.ap[0:8, :ncols], [(self.wg.ap[:, kc, 0:8], hn.ap[:, kc, :ncols]) for kc in range(KC)], [self.wg.b, hn.b], [pg.b])
        pf = ps[1]
        P.mm(pf.ap[0:8, :ncols], [(self.wg.ap[:, kc, 8:16], hn.ap[:, kc, :ncols]) for kc in range(KC)], [self.wg.b, hn.b], [pf.b])
        P.op("act", "activation", [pg.b, self.bif15.b], [ig.b], out=ig.ap[:, :ncols], in_=pg.ap[0:8, :ncols], func=AF.Tanh,
             scale=1.0 / 15.0, bias=self.bif15.ap[:, 0:1])
        P.op("dve", "tensor_scalar", [ig.b], [ig.b], out=ig.ap[:, :ncols], in0=ig.ap[:, :ncols], scalar1=15.0, scalar2=None, op0=ALU.mult)
        P.op("act", "activation", [pf.b, self.bif15.b], [lf.b], out=lf.ap[:, :ncols], in_=pf.ap[0:8, :ncols], func=AF.Tanh,
             scale=1.0 / 15.0, bias=self.bif15.ap[:, 1:2])
        P.op("act", "activation", [lf.b], [lf.b], out=lf.ap[:, :ncols], in_=lf.ap[:, :ncols], func=AF.Exp, scale=-15.0)
        P.op("dve", "tensor_scalar", [lf.b], [lf.b], out=lf.ap[:, :ncols], in0=lf.ap[:, :ncols], scalar1=1.0, scalar2=None, op0=ALU.add)
        P.op("act", "activation", [lf.b], [lf.b], out=lf.ap[:, :ncols], in_=lf.ap[:, :ncols], func=AF.Ln)
        P.op("dve", "tensor_scalar", [lf.b], [lf.b], out=lf.ap[:, :ncols], in0=lf.ap[:, :ncols], scalar1=-1.0, scalar2=None, op0=ALU.mult)
        if main and ti == 0:
            P.op("dve", "tensor_scalar", [self.nst.b, self.flag.b], [self.nst.b], out=self.nst.ap, in0=self.nst.ap,
                 scalar1=self.flag.ap[:, 0:1], scalar2=None, op0=ALU.mult)
            P.op("dve", "tensor_scalar", [self.mst.b, self.flag.b], [self.mst.b], out=self.mst.ap, in0=self.mst.ap,
                 scalar1=self.flag.ap[0:8, 0:1], scalar2=None, op0=ALU.mult)
        mst = self.mst
        for blk in range(nblk):
            cs = slice(blk * BLK, (blk + 1) * BLK)
            P.op("dve", "tensor_tensor_scan", [lf.b, self.ones.b], [bt.b], out=bt.ap[:, cs], data0=ones8, data1=lf.ap[:, cs],
                 initial=0.0, op0=ALU.mult, op1=ALU.add)
            P.op("dve", "tensor_tensor", [ig.b, bt.b], [at.b], out=at.ap[:, cs], in0=ig.ap[:, cs], in1=bt.ap[:, cs], op=ALU.subtract)
            P.op("dve", "tensor_reduce", [at.b], [sm.b], out=sm.ap[:, blk:blk + 1], in_=at.ap[:, cs], axis=AX.X, op=ALU.max)
            P.op("dve", "tensor_tensor", [sm.b, mst.b], [cvec.b], out=cvec.ap[:, blk:blk + 1], in0=sm.ap[:, blk:blk + 1],
                 in1=mst.ap[:, 0:1], op=ALU.max)
            P.op("dve", "tensor_scalar", [cvec.b], [negc.b], out=negc.ap[:, blk:blk + 1], in0=cvec.ap[:, blk:blk + 1],
                 scalar1=-1.0, scalar2=None, op0=ALU.mult)
            P.op("act", "activation", [mst.b, negc.b], [sdv.b], out=sdv.ap[:, blk:blk + 1], in_=mst.ap[:, 0:1], func=AF.Exp,
                 bias=negc.ap[:, blk:blk + 1])
            P.op("act", "activation", [at.b, negc.b], [ut.b], out=ut.ap[:, cs], in_=at.ap[:, cs], func=AF.Exp,
                 bias=negc.ap[:, blk:blk + 1])
            P.op("act", "activation", [bt.b, negc.b], [dnf.b], out=dnf.ap[:, cs], in_=bt.ap[:, cs], func=AF.Exp, scale=-1.0,
                 bias=negc.ap[:, blk:blk + 1])
            P.op("dve", "tensor_tensor", [bt.b, cvec.b], [mst.b], out=mst.ap[:, 0:1], in0=bt.ap[:, (blk + 1) * BLK - 1:(blk + 1) * BLK],
                 in1=cvec.ap[:, blk:blk + 1], op=ALU.add)
        pt = ps[0]
        for blk in range(nblk):
            cs = slice(blk * BLK, (blk + 1) * BLK)
            for j, src in enumerate((ut, dnf)):
                o0 = (blk * 2 + j) * 8
                P.op("pe", "matmul", [src.b, self.ident.b], [pt.b], out=pt.ap[:, o0:o0 + 8], lhsT=src.ap[0:8, cs],
                     rhs=self.ident.ap[0:8, 0:8], start=True, stop=True)
        P.op("dve", "tensor_copy", [pt.b], [tokm.b], out=tokm.ap[:, 0:nblk].rearrange("p a b c -> p (a b c)"),
             in_=pt.ap[:, 0:nblk * 16])
        for h in range(H):
            P.op("dve", "tensor_tensor", [sdv.b, self.bm.b], [Rexp.b], out=Rexp.ap[:, h, 0:nblk], in0=sdv.ap[:, 0:nblk],
                 in1=self.bm.ap[:, h, 0:nblk], op=ALU.mult)
        pb_ = ps[1]
        P.op("pe", "matmul", [Rexp.b, self.ones.b], [pb_.b], out=pb_.ap[:, 0:32], lhsT=self.ones.ap[0:8, :],
             rhs=Rexp.ap.rearrange("p a b -> p (a b)"), start=True, stop=True)
        P.op("dve", "tensor_copy", [pb_.b], [sdbc.b], out=sdbc.ap.rearrange("p a b -> p (a b)"), in_=pb_.ap[:, 0:32])
        if has_s:
            S = self.sample_gates(ig, lf, npc)
        wsrc = d["w_min_s"]
        for h in range(H):
            Cs = Cst[h % 2]
            if self.first_state:
                P.op("dve", "memset", [], [Cs.b], ap=Cs.ap, constant=0.0)
            else:
                self.small_load(Cs, self.C_scr[h], dram_buf=self.cscr_b[h])
                if main and ti == 0:
                    P.op("dve", "tensor_scalar", [Cs.b, self.flag.b], [Cs.b], out=Cs.ap, in0=Cs.ap, scalar1=self.flag.ap[:, 0:1],
                         scalar2=None, op0=ALU.mult)
            if main:
                for c in range(2):
                    bk = self.proj(wsrc[h * 2 + c], KC, hn, ncols)
                    P.op("act", "activation", [bk.b], [qT.b], out=qT.ap[:, c, :ncols], in_=bk.ap[:, :ncols], func=AF.Copy, scale=1.0 / 16.0)
                    if has_s:
                        P.op("dve", "tensor_scalar", [bk.b], [S["q32"].b], out=S["q32"].ap[:, c, :], in0=bk.ap[:, npc:npc + 4],
                             scalar1=1.0 / 16.0, scalar2=None, op0=ALU.mult)
            for c in range(2):
                bk = self.proj(wsrc[16 + h * 2 + c], KC, hn, ncols)
                P.op("act", "activation", [bk.b], [kT.b], out=kT.ap[:, c, :ncols], in_=bk.ap[:, :ncols], func=AF.Copy)
                if has_s:
                    P.op("dve", "tensor_copy", [bk.b], [S["k32"].b], out=S["k32"].ap[:, c, :], in_=bk.ap[:, npc:npc + 4])
            for c in range(4):
                bk = self.proj(wsrc[32 + h * 4 + c], KC, hn, ncols)
                P.op("act", "activation", [bk.b], [vT.b], out=vT.ap[:, c, :ncols], in_=bk.ap[:, :ncols], func=AF.Copy)
                if has_s:
                    P.op("dve", "tensor_copy", [bk.b], [S["v32"].b], out=S["v32"].ap[:, c, :], in_=bk.ap[:, npc:npc + 4])
            if main:
                for c in range(4):
                    bk = self.proj(wsrc[64 + h * 4 + c], KC, hn, ncols)
                    P.op("act", "activation", [bk.b], [oT.b], out=oT.ap[:, c, :ncols], in_=bk.ap[:, :ncols], func=AF.Sigmoid)
                    if has_s:
                        P.op("act", "activation", [bk.b], [S["o32"].b], out=S["o32"].ap[:, c, :], in_=bk.ap[:, npc:npc + 4], func=AF.Sigmoid)
            hh = h % 4
            for blk in range(nblk):
                cs = slice(blk * BLK, (blk + 1) * BLK)
                kt, vt, ot, pT, hg = ktok[blk % 2], vtk[blk % 2], otk[blk % 2], PT[blk % 2], hsg[blk % 2]
                uS = tokm.ap[:, blk, 0, h:h + 1]
                dS = tokm.ap[:, blk, 1, h:h + 1]
                sdS = sdbc.ap[:, h, blk:blk + 1]
                b0 = ps[0]
                for c in range(2):
                    P.op("pe", "matmul", [kT.b, self.identb.b], [b0.b], out=b0.ap[:, c * 128:(c + 1) * 128], lhsT=kT.ap[:, c, cs],
                         rhs=self.identb.ap, start=True, stop=True)
                P.op("dve", "tensor_scalar", [b0.b, tokm.b], [kt.b], out=kt.ap, in0=b0.ap[:, 0:256], scalar1=uS, scalar2=None, op0=ALU.mult)
                b1 = ps[1]
                for c in range(4):
                    P.op("pe", "matmul", [vT.b, self.identb.b], [b1.b], out=b1.ap[:, c * 128:(c + 1) * 128], lhsT=vT.ap[:, c, cs],
                         rhs=self.identb.ap, start=True, stop=True)
                P.op("act", "activation", [b1.b], [vt.b], out=vt.ap, in_=b1.ap, func=AF.Copy)
                P.op("dve", "tensor_scalar", [Cs.b, sdbc.b], [Cp.b], out=Cp.ap, in0=Cs.ap, scalar1=sdS, scalar2=None, op0=ALU.mult)
                P.op("dve", "tensor_scalar", [self.nst.b, sdbc.b], [npb.b], out=npb.ap, in0=self.nst.ap[:, h, :], scalar1=sdS, scalar2=None, op0=ALU.mult)
                if main:
                    b2 = ps[4]
                    for c in range(4):
                        P.op("pe", "matmul", [oT.b, self.identb.b], [b2.b], out=b2.ap[:, c * 128:(c + 1) * 128], lhsT=oT.ap[:, c, cs],
                             rhs=self.identb.ap, start=True, stop=True)
                    P.op("act", "activation", [b2.b], [ot.b], out=ot.ap, in_=b2.ap, func=AF.Copy)
                    b3 = ps[3]
                    P.mm(b3.ap[:, 0:128], [(kT.ap[:, c, cs], qT.ap[:, c, cs]) for c in range(2)], [kT.b, qT.b], [b3.b])
                    P.op("dve", "scalar_tensor_tensor", [b3.b, tokm.b, self.cmask.b], [pT.b], out=pT.ap, in0=b3.ap[:, 0:128], scalar=uS,
                         in1=self.cmask.ap, op0=ALU.mult, op1=ALU.mult)
                    acc = ps[5]
                    P.mm(acc.ap, [(pT.ap, vt.ap), (qT.ap[:, 0, cs], Cp.ap[:, 0, :]), (qT.ap[:, 1, cs], Cp.ap[:, 1, :])],
                         [pT.b, vt.b, qT.b, Cp.b], [acc.b])
                    dac = ps[3]
                    P.mm(dac.ap[:, 256:257], [(pT.ap, self.onesb.ap[:, 0:1]), (qT.ap[:, 0, cs], npb.ap[:, 0:1]), (qT.ap[:, 1, cs], npb.ap[:, 1:2])],
                         [pT.b, self.onesb.b, qT.b, npb.b], [dac.b])
                    P.op("act", "activation", [acc.b], [junk.b, col.b], out=junk.ap, in_=acc.ap, func=AF.Square, accum_out=col.ap[:, 0:1])
                    P.op("dve", "tensor_scalar", [dac.b], [col.b], out=col.ap[:, 6:7], in0=dac.ap[:, 256:257], scalar1=-1.0, scalar2=None,
                         op0=ALU.mult)
                    P.op("dve", "tensor_tensor", [dac.b, col.b], [col.b], out=col.ap[:, 7:8], in0=dac.ap[:, 256:257], in1=col.ap[:, 6:7], op=ALU.max)
                    P.op("dve", "tensor_tensor", [col.b, tokm.b], [col.b], out=col.ap[:, 1:2], in0=col.ap[:, 7:8], in1=dS, op=ALU.max)
                    P.op("dve", "tensor_scalar", [col.b], [col.b], out=col.ap[:, 2:3], in0=col.ap[:, 1:2], scalar1=col.ap[:, 1:2], scalar2=EPS,
                         op0=ALU.mult, op1=ALU.mult)
                    P.op("dve", "scalar_tensor_tensor", [col.b], [col.b], out=col.ap[:, 3:4], in0=col.ap[:, 0:1], scalar=1.0 / 512.0,
                         in1=col.ap[:, 2:3], op0=ALU.mult, op1=ALU.add)
                    P.op("act", "activation", [col.b], [col.b], out=col.ap[:, 4:5], in_=col.ap[:, 3:4], func=AF.Sqrt)
                    P.op("dve", "reciprocal", [col.b], [col.b], out=col.ap[:, 5:6], in_=col.ap[:, 4:5])
                    P.op("dve", "scalar_tensor_tensor", [acc.b, col.b, ot.b], [hg.b], out=hg.ap, in0=acc.ap, scalar=col.ap[:, 5:6], in1=ot.ap,
                         op0=ALU.mult, op1=ALU.mult)
                    b6 = ps[6]
                    for c in range(4):
                        P.op("pe", "matmul", [hg.b, self.identb.b], [b6.b], out=b6.ap[:, c * 128:(c + 1) * 128], lhsT=hg.ap[:, c * 128:(c + 1) * 128],
                             rhs=self.identb.ap, start=True, stop=True)
                    for c in range(4):
                        P.op("act", "activation", [b6.b, self.gheadT.b], [hsT.b], out=hsT.ap[:, hh * 4 + c, cs], in_=b6.ap[:, c * 128:(c + 1) * 128],
                             func=AF.Copy, scale=self.gheadT.ap[:, h * 4 + c:h * 4 + c + 1])
                for c in range(2):
                    dC = ps[c]
                    P.mm(dC.ap, [(kt.ap[:, c * 128:(c + 1) * 128], vt.ap)], [kt.b, vt.b], [dC.b])
                    P.op("dve", "scalar_tensor_tensor", [Cs.b, sdbc.b, dC.b], [Cs.b], out=Cs.ap[:, c, :], in0=Cs.ap[:, c, :], scalar=sdS,
                         in1=dC.ap, op0=ALU.mult, op1=ALU.add)
                dn_ = ps[3]
                for c in range(2):
                    P.mm(dn_.ap[:, 260 + c:261 + c], [(kt.ap[:, c * 128:(c + 1) * 128], self.onesb.ap[:, 0:1])], [kt.b, self.onesb.b], [dn_.b])
                P.op("dve", "scalar_tensor_tensor", [self.nst.b, sdbc.b, dn_.b], [self.nst.b], out=self.nst.ap[:, h, :], in0=self.nst.ap[:, h, :],
                     scalar=sdS, in1=dn_.ap[:, 260:262], op0=ALU.mult, op1=ALU.add)
            self.store(self.C_scr[h], Cs, dram_buf=self.cscr_b[h])
            if main and ti == 2:
                self.store(self.dout["C_out"][h], Cs)
            if main and has_s:
                self.sample_head(h, S, hsT, hh, npc, Cst)
            if main and hh == 3:
                g = h // 4
                for c in range(KC):
                    bk = self.proj(d["w_mout_s"][g, c], 16, hsT, ncols)
                    self.xupdate(bk, c, 2, ncols, npc, has_s)
        self.first_state = False
        if main and ti == 2:
            self.store(self.dout["n_out"], self.nst)
            self.store(self.dout["m_out"], self.mst)
        if main and has_s:
            self.store(self.dout["ns_out"], S["nsn"])
            self.store(self.dout["ms_out"], S["mt"])


    def bcast8(self, tab, out_bc, bank):
        P = self.P
        R = self.carve([8, 8, 4])
        for h in range(H):
            P.op("dve", "tensor_tensor", [tab.b, self.bm.b], [R.b], out=R.ap[:, h, :], in0=tab.ap, in1=self.bm.ap[:, h, :], op=ALU.mult)
        P.op("pe", "matmul", [R.b, self.ones.b], [bank.b], out=bank.ap[:, 0:32], lhsT=self.ones.ap[0:8, :],
             rhs=R.ap.rearrange("p a b -> p (a b)"), start=True, stop=True)
        P.op("dve", "tensor_copy", [bank.b], [out_bc.b], out=out_bc.ap.rearrange("p a b -> p (a b)"), in_=bank.ap[:, 0:32])

    def sample_gates(self, ig, lf, npc):
        P, C, d = self.P, self.carve, self.din
        S = {"q32": C([128, 2, 4]), "k32": C([128, 2, 4]), "v32": C([128, 4, 4]), "o32": C([128, 4, 4]),
             "ws": C([128, 8, 4]), "si": C([128, 8, 4]), "dnf": C([128, 8, 4]), "mt": C([8, 4]),
             "ns": C([128, H, 2, 4]), "nsn": C([128, H, 2, 4])}
        ms = C([8, 4]); t1 = C([8, 4]); d1 = C([8, 4]); d2 = C([8, 4]); nm = C([8, 4])
        self.small_load(ms, d["st_m"]); self.small_load(S["ns"], d["st_n"])
        igs, lfs, mt = ig.ap[:, npc:npc + 4], lf.ap[:, npc:npc + 4], S["mt"]
        P.op("dve", "tensor_tensor", [lf.b, ms.b], [t1.b], out=t1.ap, in0=lfs, in1=ms.ap, op=ALU.add)
        P.op("dve", "tensor_tensor", [t1.b, ig.b], [mt.b], out=mt.ap, in0=t1.ap, in1=igs, op=ALU.max)
        P.op("dve", "tensor_tensor", [ig.b, mt.b], [d1.b], out=d1.ap, in0=igs, in1=mt.ap, op=ALU.subtract)
        P.op("dve", "tensor_tensor", [t1.b, mt.b], [d2.b], out=d2.ap, in0=t1.ap, in1=mt.ap, op=ALU.subtract)
        P.op("dve", "tensor_scalar", [mt.b], [nm.b], out=nm.ap, in0=mt.ap, scalar1=-1.0, scalar2=None, op0=ALU.mult)
        for t in (d1, d2, nm):
            P.op("act", "activation", [t.b], [t.b], out=t.ap, in_=t.ap, func=AF.Exp)
        self.bcast8(d1, S["ws"], self.ps[0]); self.bcast8(d2, S["si"], self.ps[1]); self.bcast8(nm, S["dnf"], self.ps[0])
        S["prod"] = C([128, 2, 8]); S["sums"] = C([128, 32]); S["c4"] = C([128, 16, 4]); S["hT"] = C([128, 4, 4])
        S["Dg"] = C([128, 4, 128]); S["wk"] = C([128, 2]); S["t44"] = C([128, 4, 4])
        return S

    def sample_head(self, h, S, hsT, hh, npc, Cst):
        P, d, ps = self.P, self.din, self.ps
        q32, k32, v32, o32, c4 = S["q32"], S["k32"], S["v32"], S["o32"], S["c4"]
        wsb, sib, dnb = S["ws"].ap[:, h, :], S["si"].ap[:, h, :], S["dnf"].ap[:, h, :]
        prod, sums = S["prod"], S["sums"]
        P.op("dve", "tensor_tensor", [q32.b, k32.b], [prod.b], out=prod.ap[:, :, 0:4], in0=q32.ap, in1=k32.ap, op=ALU.mult)
        P.op("dve", "tensor_tensor", [q32.b, S["ns"].b], [prod.b], out=prod.ap[:, :, 4:8], in0=q32.ap, in1=S["ns"].ap[:, h], op=ALU.mult)
        pA = ps[3]
        P.op("pe", "matmul", [prod.b, self.ones.b], [pA.b], out=pA.ap[:, 0:16], lhsT=self.ones.ap, rhs=prod.ap.rearrange("p a b -> p (a b)"),
             start=True, stop=True)
        P.op("dve", "tensor_copy", [pA.b], [sums.b], out=sums.ap[:, 0:16], in_=pA.ap[:, 0:16])
        qkn = c4.ap[:, 0:2].rearrange("p a b -> p (a b)")
        P.op("dve", "tensor_tensor", [sums.b], [c4.b], out=qkn, in0=sums.ap[:, 0:8], in1=sums.ap[:, 8:16], op=ALU.add)
        a1, t, den, dn, rdn, a1n, sin_ = (c4.ap[:, i, :] for i in range(2, 9))
        P.op("dve", "tensor_tensor", [S["ws"].b, c4.b], [c4.b], out=a1, in0=wsb, in1=c4.ap[:, 0, :], op=ALU.mult)
        P.op("dve", "tensor_tensor", [S["si"].b, c4.b], [c4.b], out=t, in0=sib, in1=c4.ap[:, 1, :], op=ALU.mult)
        P.op("dve", "tensor_tensor", [c4.b], [c4.b], out=den, in0=t, in1=a1, op=ALU.add)
        P.op("dve", "tensor_scalar", [c4.b], [c4.b], out=dn, in0=den, scalar1=-1.0, scalar2=None, op0=ALU.mult)
        P.op("dve", "tensor_tensor", [c4.b], [c4.b], out=dn, in0=dn, in1=den, op=ALU.max)
        P.op("dve", "tensor_tensor", [c4.b, S["dnf"].b], [c4.b], out=dn, in0=dn, in1=dnb, op=ALU.max)
        P.op("dve", "reciprocal", [c4.b], [c4.b], out=rdn, in_=dn)
        P.op("dve", "tensor_tensor", [c4.b], [c4.b], out=a1n, in0=a1, in1=rdn, op=ALU.mult)
        P.op("dve", "tensor_tensor", [c4.b, S["si"].b], [c4.b], out=sin_, in0=sib, in1=rdn, op=ALU.mult)
        pB = ps[5]
        for j in range(NSMP):
            Cs = Cst[j % 2]
            self.small_load(Cs, d["st_C"][j, h])
            for vc in range(4):
                P.mm(pB.ap[:, vc * 4 + j:vc * 4 + j + 1], [(Cs.ap[:, c, vc * 128:(vc + 1) * 128], q32.ap[:, c, j:j + 1]) for c in range(2)],
                     [Cs.b, q32.b], [pB.b])
            pV = ps[4]
            for vc in range(4):
                P.op("dve", "tensor_scalar", [self.ident.b, v32.b], [S["Dg"].b], out=S["Dg"].ap[:, vc, :], in0=self.ident.ap,
                     scalar1=v32.ap[:, vc, j:j + 1], scalar2=None, op0=ALU.mult)
            for vc in range(4):
                P.op("pe", "matmul", [S["Dg"].b, self.ones.b], [pV.b], out=pV.ap[:, vc * 128:(vc + 1) * 128], lhsT=self.ones.ap,
                     rhs=S["Dg"].ap[:, vc, :], start=True, stop=True)
            P.op("dve", "tensor_scalar", [k32.b, S["ws"].b], [S["wk"].b], out=S["wk"].ap, in0=k32.ap[:, :, j], scalar1=wsb[:, j:j + 1],
                 scalar2=None, op0=ALU.mult)
            for c in range(2):
                P.op("dve", "tensor_scalar", [Cs.b, S["si"].b], [Cs.b], out=Cs.ap[:, c, :], in0=Cs.ap[:, c, :], scalar1=sib[:, j:j + 1],
                     scalar2=None, op0=ALU.mult)
                P.op("dve", "scalar_tensor_tensor", [pV.b, S["wk"].b, Cs.b], [Cs.b], out=Cs.ap[:, c, :], in0=pV.ap, scalar=S["wk"].ap[:, c:c + 1],
                     in1=Cs.ap[:, c, :], op0=ALU.mult, op1=ALU.add)
            P.op("dve", "scalar_tensor_tensor", [S["ns"].b, S["si"].b, S["wk"].b], [S["nsn"].b], out=S["nsn"].ap[:, h, :, j],
                 in0=S["ns"].ap[:, h, :, j], scalar=sib[:, j:j + 1], in1=S["wk"].ap, op0=ALU.mult, op1=ALU.add)
            self.store(self.dout["Cs_out"][j, h], Cs)
        hT, t44 = S["hT"], S["t44"]
        for vc in range(4):
            P.op("dve", "tensor_tensor", [v32.b, c4.b], [t44.b], out=t44.ap[:, vc, :], in0=v32.ap[:, vc, :], in1=a1n, op=ALU.mult)
            P.op("dve", "tensor_tensor", [pB.b, c4.b], [hT.b], out=hT.ap[:, vc, :], in0=pB.ap[:, vc * 4:vc * 4 + 4], in1=sin_, op=ALU.mult)
        P.op("dve", "tensor_tensor", [hT.b, t44.b], [hT.b], out=hT.ap, in0=hT.ap, in1=t44.ap, op=ALU.add)
        P.op("dve", "tensor_tensor", [hT.b], [t44.b], out=t44.ap, in0=hT.ap, in1=hT.ap, op=ALU.mult)
        P.op("pe", "matmul", [t44.b, self.ones.b], [pA.b], out=pA.ap[:, 16:32], lhsT=self.ones.ap, rhs=t44.ap.rearrange("p a b -> p (a b)"),
             start=True, stop=True)
        P.op("dve", "tensor_copy", [pA.b], [sums.b], out=sums.ap[:, 16:32], in_=pA.ap[:, 16:32])
        ssq, rs = c4.ap[:, 9, :], c4.ap[:, 10, :]
        P.op("dve", "tensor_tensor", [sums.b], [c4.b], out=ssq, in0=sums.ap[:, 16:20], in1=sums.ap[:, 20:24], op=ALU.add)
        P.op("dve", "tensor_tensor", [sums.b, c4.b], [c4.b], out=ssq, in0=ssq, in1=sums.ap[:, 24:28], op=ALU.add)
        P.op("dve", "tensor_tensor", [sums.b, c4.b], [c4.b], out=ssq, in0=ssq, in1=sums.ap[:, 28:32], op=ALU.add)
        P.op("act", "activation", [c4.b, self.epsT.b], [c4.b], out=rs, in_=ssq, func=AF.Sqrt, scale=1.0 / 512.0, bias=self.epsT.ap[:, 0:1])
        P.op("dve", "reciprocal", [c4.b], [c4.b], out=rs, in_=rs)
        for vc in range(4):
            P.op("dve", "tensor_tensor", [hT.b, c4.b], [t44.b], out=t44.ap[:, vc, :], in0=hT.ap[:, vc, :], in1=rs, op=ALU.mult)
            P.op("dve", "tensor_tensor", [t44.b, o32.b], [t44.b], out=t44.ap[:, vc, :], in0=t44.ap[:, vc, :], in1=o32.ap[:, vc, :], op=ALU.mult)
            P.op("dve", "tensor_scalar", [t44.b, self.gheadT.b], [hsT.b], out=hsT.ap[:, hh * 4 + vc, npc:npc + 4], in0=t44.ap[:, vc, :],
                 scalar1=self.gheadT.ap[:, h * 4 + vc:h * 4 + vc + 1], scalar2=None, op0=ALU.mult)

    def ffn(self, l, ti, ncols, npc, has_s):
        P, d, C, hn = self.P, self.din, self.carve, self.hn
        act = C([128, 22, NCMAX], BF16)
        ue = [[C([128, NCMAX + 2]) for _ in range(2)] for _ in range(2)]
        y = [[C([128, NCMAX]) for _ in range(2)] for _ in range(2)]
        sg = [C([128, NCMAX]) for _ in range(2)]
        wcv, bcv, uh = self.wconvT, self.bconvT, self.uhist
        if has_s:
            cc = C([128, NSL, 4, 2]); unew = C([128, NSL, 4])
            self.small_load(cc, d["cconv"][l])
        for g, (f0, nf) in enumerate(FGRP):
            for jj in range(nf):
                j = f0 + jj
                for which in range(2):
                    sl = 2 * j + which
                    bank = self.proj(d["w_ffi_s"][l, sl], KC, hn, ncols)
                    u = ue[which][j % 2]
                    yy = y[which][j % 2]
                    w0, w1, w2 = (wcv.ap[:, l, sl, i:i + 1] for i in range(3))
                    P.op("dve", "tensor_copy", [uh.b], [u.b], out=u.ap[:, 0:2], in_=uh.ap[:, l, sl, :])
                    P.op("act", "activation", [bank.b], [u.b], out=u.ap[:, 2:2 + ncols], in_=bank.ap[:, :ncols], func=AF.Copy)
                    P.op("dve", "tensor_copy", [u.b], [uh.b], out=uh.ap[:, l, sl, :], in_=u.ap[:, npc:npc + 2])
                    P.op("dve", "tensor_scalar", [u.b, wcv.b, bcv.b], [yy.b], out=yy.ap[:, :npc], in0=u.ap[:, 0:npc], scalar1=w0,
                         scalar2=bcv.ap[:, l, sl:sl + 1], op0=ALU.mult, op1=ALU.add)
                    P.op("dve", "scalar_tensor_tensor", [u.b, wcv.b, yy.b], [yy.b], out=yy.ap[:, :npc], in0=u.ap[:, 1:npc + 1], scalar=w1,
                         in1=yy.ap[:, :npc], op0=ALU.mult, op1=ALU.add)
                    P.op("dve", "scalar_tensor_tensor", [u.b, wcv.b, yy.b], [yy.b], out=yy.ap[:, :npc], in0=u.ap[:, 2:npc + 2], scalar=w2,
                         in1=yy.ap[:, :npc], op0=ALU.mult, op1=ALU.add)
                    if has_s:
                        ys = yy.ap[:, npc:npc + 4]
                        P.op("dve", "tensor_scalar", [cc.b, wcv.b, bcv.b], [yy.b], out=ys, in0=cc.ap[:, sl, :, 0], scalar1=w0,
                             scalar2=bcv.ap[:, l, sl:sl + 1], op0=ALU.mult, op1=ALU.add)
                        P.op("dve", "scalar_tensor_tensor", [cc.b, wcv.b, yy.b], [yy.b], out=ys, in0=cc.ap[:, sl, :, 1], scalar=w1,
                             in1=ys, op0=ALU.mult, op1=ALU.add)
                        P.op("dve", "scalar_tensor_tensor", [u.b, wcv.b, yy.b], [yy.b], out=ys, in0=u.ap[:, 2 + npc:6 + npc], scalar=w2,
                             in1=ys, op0=ALU.mult, op1=ALU.add)
                        P.op("dve", "tensor_copy", [u.b], [unew.b], out=unew.ap[:, sl, :], in_=u.ap[:, 2 + npc:6 + npc])
                s_ = sg[j % 2]
                P.op("act", "activation", [y[0][j % 2].b], [s_.b], out=s_.ap[:, :ncols], in_=y[0][j % 2].ap[:, :ncols], func=AF.Silu)
                P.op("dve", "tensor_tensor", [s_.b, y[1][j % 2].b], [act.b], out=act.ap[:, jj, :ncols], in0=s_.ap[:, :ncols],
                     in1=y[1][j % 2].ap[:, :ncols], op=ALU.mult)
            for c in range(KC):
                bank = self.proj(d["w_ffo_s"][l, g, c][:, 0:nf, :], nf, act, ncols)
                self.xupdate(bank, c, 5 + 6 * l, ncols, npc, has_s)
        if has_s:
            self.store(self.dout["convs_new"][l], unew)

    def rope(self, kq, ncols, out_t, out_ap):
        P = self.P
        pr = self.ps[6]
        P.op("pe", "matmul", [kq.b, self.Rmat.b], [pr.b], out=pr.ap[:, :ncols], lhsT=self.Rmat.ap, rhs=kq.ap[:, :ncols], start=True, stop=True)
        t1, t2 = self.tmp
        P.op("dve", "tensor_tensor", [kq.b, self.ropeT.b], [t1.b], out=t1.ap[:, :ncols], in0=kq.ap[:, :ncols], in1=self.ropeT.ap[:, 0, :ncols], op=ALU.mult)
        P.op("dve", "tensor_tensor", [pr.b, self.ropeT.b], [t2.b], out=t2.ap[:, :ncols], in0=pr.ap[:, :ncols], in1=self.ropeT.ap[:, 1, :ncols], op=ALU.mult)
        P.op("dve", "tensor_tensor", [t1.b, t2.b], [out_t.b], out=out_ap, in0=t1.ap[:, :ncols], in1=t2.ap[:, :ncols], op=ALU.add)

    def attention(self, ti, nblk, ncols, npc, has_s):
        P, d, C, ps, hn = self.P, self.din, self.carve, self.ps, self.hn
        kTd, vtok = self.kTd, self.vtok
        kq32 = [C([128, NCMAX]) for _ in range(2)]
        kr32 = [C([128, NCMAX]) for _ in range(2)]
        qr = [C([128, NCMAX], BF16) for _ in range(2)]
        v32 = C([128, 4, NCMAX])
        oT = C([128, KC, NCMAX], BF16)
        Sm = [C([128, 256]) for _ in range(2)]
        Pf = [C([128, 256]) for _ in range(2)]
        Pn = [C([128, 256], BF16) for _ in range(2)]
        PTs = [C([128, 256], BF16) for _ in range(2)]
        colA = [C([128, 8]) for _ in range(2)]
        self.small_load(self.ropeT, d["rope"][ti])
        self.norm(2, ncols, npc, has_s)
        for g in range(8):
            bank = self.proj(d["w_kv_s"][g], KC, hn, ncols)
            kq, kr = kq32[g % 2], kr32[g % 2]
            P.op("dve", "tensor_scalar", [bank.b, self.bkv.b], [kq.b], out=kq.ap[:, :ncols], in0=bank.ap[:, :ncols], scalar1=self.bkv.ap[:, g:g + 1],
                 scalar2=None, op0=ALU.add)
            self.rope(kq, ncols, kr, kr.ap[:, :ncols])
            P.op("act", "activation", [kr.b], [kTd.b], out=kTd.ap[:, g, 128:128 + ncols], in_=kr.ap[:, :ncols], func=AF.Copy)
            if ti == 2:
                self.store(self.dout["kwinp"][:, g, :], kr, kr.ap[0:64, npc - 128:npc])
            if has_s:
                self.store(self.dout["knew"][:, g, :], kr, kr.ap[0:64, npc:npc + 4])
        for c in range(4):
            bank = self.proj(d["w_kv_s"][8 + c], KC, hn, ncols)
            P.op("dve", "tensor_scalar", [bank.b, self.bkv.b], [v32.b], out=v32.ap[:, c, :ncols], in0=bank.ap[:, :ncols],
                 scalar1=self.bkv.ap[:, 8 + c:9 + c], scalar2=None, op0=ALU.add)
        if ti == 2:
            self.store(self.dout["vwinp"], v32, v32.ap[:, :, npc - 128:npc])
        for blk in range(nblk):
            cs = slice(blk * BLK, (blk + 1) * BLK)
            pv = ps[0 + (blk % 2)]
            for c in range(4):
                P.op("pe", "matmul", [v32.b, self.ident.b], [pv.b], out=pv.ap[:, c * 128:(c + 1) * 128], lhsT=v32.ap[:, c, cs], rhs=self.ident.ap,
                     start=True, stop=True)
            P.op("act", "activation", [pv.b], [vtok.b], out=vtok.ap[:, blk + 1, :], in_=pv.ap, func=AF.Copy)
        if has_s:
            vts = C([4, 512])
            pvs = ps[3]
            for c in range(4):
                P.op("pe", "matmul", [v32.b, self.ident.b], [pvs.b], out=pvs.ap[0:4, c * 128:(c + 1) * 128], lhsT=v32.ap[:, c, npc:npc + 4],
                     rhs=self.ident.ap, start=True, stop=True)
            P.op("dve", "tensor_copy", [pvs.b], [vts.b], out=vts.ap, in_=pvs.ap[0:4, :])
            self.vnew_b = Buf()
            self.store(self.dout["vnew"], vts, dram_buf=self.vnew_b)
            qTs = C([128, KC, 4], BF16)
        astop = self.dbg.get("attn_stop", 99)
        if astop <= 1:
            return
        self.norm(3, ncols, npc, has_s)
        for c in range(KC):
            bank = self.proj(d["w_q_s"][c], KC, hn, ncols)
            kq, q = kq32[c % 2], qr[c % 2]
            P.op("dve", "tensor_scalar", [bank.b, self.bqT.b], [kq.b], out=kq.ap[:, :ncols], in0=bank.ap[:, :ncols], scalar1=self.bqT.ap[:, c:c + 1],
                 scalar2=None, op0=ALU.add)
            self.rope(kq, ncols, q, q.ap[:, :ncols])
            if has_s:
                P.op("dve", "tensor_copy", [q.b], [qTs.b], out=qTs.ap[:, c, :], in_=q.ap[:, npc:npc + 4])
            g = c // 4
            for blk in range(nblk):
                cs = slice(blk * BLK, (blk + 1) * BLK)
                mk = self.amask.ap[:, 1 if (ti == 0 and blk == 0) else 0, :]
                pO = ps[5]
                for half in range(2):
                    rows = slice(half * 64, half * 64 + 64)
                    h = 2 * c + half
                    cA, sm_, pf, pn, pts = colA[half], Sm[half], Pf[half], Pn[half], PTs[half]
                    pS = ps[0 + half]
                    P.mm(pS.ap[:, 0:256], [(q.ap[rows, cs], kTd.ap[rows, g, blk * BLK:blk * BLK + 256])], [q.b, kTd.b], [pS.b])
                    P.op("dve", "tensor_tensor", [pS.b, self.amask.b], [sm_.b], out=sm_.ap, in0=pS.ap[:, 0:256], in1=mk, op=ALU.add)
                    P.op("dve", "tensor_reduce", [sm_.b], [cA.b], out=cA.ap[:, 0:1], in_=sm_.ap, axis=AX.X, op=ALU.max)
                    P.op("dve", "tensor_scalar", [cA.b, self.nsinks.b], [cA.b], out=cA.ap[:, 1:2], in0=cA.ap[:, 0:1], scalar1=-0.125,
                         scalar2=self.nsinks.ap[:, h:h + 1], op0=ALU.mult, op1=ALU.min)
                    P.op("act", "activation", [sm_.b, cA.b], [pf.b, cA.b], out=pf.ap, in_=sm_.ap, func=AF.Exp, scale=0.125, bias=cA.ap[:, 1:2],
                         accum_out=cA.ap[:, 2:3])
                    P.op("act", "activation", [cA.b, self.sinks.b], [cA.b], out=cA.ap[:, 3:4], in_=cA.ap[:, 1:2], func=AF.Exp,
                         bias=self.sinks.ap[:, h:h + 1])
                    P.op("dve", "tensor_tensor", [cA.b], [cA.b], out=cA.ap[:, 4:5], in0=cA.ap[:, 2:3], in1=cA.ap[:, 3:4], op=ALU.add)
                    P.op("dve", "reciprocal", [cA.b], [cA.b], out=cA.ap[:, 5:6], in_=cA.ap[:, 4:5])
                    P.op("dve", "tensor_scalar", [pf.b, cA.b], [pn.b], out=pn.ap, in0=pf.ap, scalar1=cA.ap[:, 5:6], scalar2=None, op0=ALU.mult)
                    pT = ps[3 + half]
                    for j in range(2):
                        P.op("pe", "matmul", [pn.b, self.identb.b], [pT.b], out=pT.ap[:, j * 128:(j + 1) * 128], lhsT=pn.ap[:, j * 128:(j + 1) * 128],
                             rhs=self.identb.ap, start=True, stop=True)
                    P.op("act", "activation", [pT.b], [pts.b], out=pts.ap, in_=pT.ap[:, 0:256], func=AF.Copy)
                    P.mm(pO.ap[rows, 0:128], [(vtok.ap[:, blk, g * 64:(g + 1) * 64], pts.ap[:, 0:128]),
                                              (vtok.ap[:, blk + 1, g * 64:(g + 1) * 64], pts.ap[:, 128:256])], [vtok.b, pts.b], [pO.b])
                P.op("act", "activation", [pO.b], [oT.b], out=oT.ap[:, c, cs], in_=pO.ap[:, 0:128], func=AF.Copy)
        if astop <= 2:
            return
        if has_s:
            Ks = [C([128, 8, 128], BF16) for _ in range(2)]
            Vs = [C([128, 512], BF16) for _ in range(2)]
            STs = C([128, 64]); PnS = C([64, 128], BF16); PfS = C([64, 128]); PTS = C([128, 64], BF16); cS = C([64, 8])
            pO2 = ps[6]
            opad = C([64, 128])
            sa_stop = self.dbg.get("sa_stop", 99)
            for j in range(NSMP):
                K_, V_ = Ks[j % 2], Vs[j % 2]
                P.dma("pool", "d2", [], [K_.b], out=K_.ap[:, :, 0:127], in_=d["ckT"][j])
                P.op("dve", "tensor_copy", [kTd.b], [K_.b], out=K_.ap[:, :, 127], in_=kTd.ap[:, :, 128 + npc + j])
                P.dma("pool", "d3", [], [V_.b], out=V_.ap[0:127, :], in_=d["cv_old"][j])
                P.dma("pool", "d4", [self.vnew_b], [V_.b], out=V_.ap[127:128, :], in_=self.dout["vnew"][j:j + 1, :])
                V_.b.w["d3"] = P.cnt["d3"]
                if sa_stop <= 1:
                    continue
                for g in range(8):
                    for half in range(2):
                        rows = slice(half * 64, half * 64 + 64)
                        pSh = ps[half]
                        P.mm(pSh.ap[:, 4 * g:4 * g + 4], [(K_.ap[rows, g, :], qTs.ap[rows, 4 * g:4 * g + 4, j])], [K_.b, qTs.b], [pSh.b])
                ST3 = STs.ap.rearrange("p (g e) -> p g e", e=8)
                for half in range(2):
                    P.op("dve", "tensor_copy", [ps[half].b], [STs.b], out=ST3[:, :, 4 * half:4 * half + 4],
                         in_=ps[half].ap[:, 0:32].rearrange("p (g k) -> p g k", k=4))
                if sa_stop <= 2:
                    continue
                pS2 = ps[3]
                P.op("pe", "matmul", [STs.b, self.ident.b], [pS2.b], out=pS2.ap[0:64, 0:128], lhsT=STs.ap, rhs=self.ident.ap, start=True, stop=True)
                P.op("dve", "tensor_reduce", [pS2.b], [cS.b], out=cS.ap[:, 0:1], in_=pS2.ap[0:64, 0:128], axis=AX.X, op=ALU.max)
                P.op("dve", "tensor_scalar", [cS.b, self.sinksT.b], [cS.b], out=cS.ap[:, 1:2], in0=cS.ap[:, 0:1], scalar1=-0.125,
                     scalar2=self.sinksT.ap[:, 1:2], op0=ALU.mult, op1=ALU.min)
                P.op("act", "activation", [pS2.b, cS.b], [PfS.b, cS.b], out=PfS.ap, in_=pS2.ap[0:64, 0:128], func=AF.Exp, scale=0.125,
                     bias=cS.ap[:, 1:2], accum_out=cS.ap[:, 2:3])
                P.op("act", "activation", [cS.b, self.sinksT.b], [cS.b], out=cS.ap[:, 3:4], in_=cS.ap[:, 1:2], func=AF.Exp, bias=self.sinksT.ap[:, 0:1])
                P.op("dve", "tensor_tensor", [cS.b], [cS.b], out=cS.ap[:, 4:5], in0=cS.ap[:, 2:3], in1=cS.ap[:, 3:4], op=ALU.add)
                P.op("dve", "reciprocal", [cS.b], [cS.b], out=cS.ap[:, 5:6], in_=cS.ap[:, 4:5])
                P.op("dve", "tensor_scalar", [PfS.b, cS.b], [PnS.b], out=PnS.ap, in0=PfS.ap, scalar1=cS.ap[:, 5:6], scalar2=None, op0=ALU.mult)
                pP = ps[4]
                P.op("pe", "matmul", [PnS.b, self.identb.b], [pP.b], out=pP.ap[:, 0:64], lhsT=PnS.ap, rhs=self.identb.ap[0:64, 0:64], start=True, stop=True)
                P.op("act", "activation", [pP.b], [PTS.b], out=PTS.ap, in_=pP.ap[:, 0:64], func=AF.Copy)
                if sa_stop <= 3:
                    continue
                pO1 = ps[5]
                P.mm(pO1.ap[0:64, 0:512], [(PTS.ap, V_.ap)], [PTS.b, V_.b], [pO1.b])
                for hf in range(2):
                    osl = opad.ap[:, hf * 64:(hf + 1) * 64]
                    P.op("dve", "tensor_scalar", [pO1.b, self.mg.b], [opad.b], out=osl, in0=pO1.ap[0:64, 0:64], scalar1=self.mg.ap[:, hf * 8:hf * 8 + 1],
                         scalar2=None, op0=ALU.mult)
                    for g in range(1, 8):
                        P.op("dve", "scalar_tensor_tensor", [pO1.b, self.mg.b, opad.b], [opad.b], out=osl, in0=pO1.ap[0:64, g * 64:(g + 1) * 64],
                             scalar=self.mg.ap[:, hf * 8 + g:hf * 8 + g + 1], in1=osl, op0=ALU.mult, op1=ALU.add)
                if sa_stop <= 4:
                    continue
                P.op("pe", "matmul", [opad.b, self.ident.b], [pO2.b], out=pO2.ap[:, j * 64:(j + 1) * 64], lhsT=opad.ap, rhs=self.ident.ap[0:64, 0:64],
                     start=True, stop=True)
            if sa_stop > 5:
                tmpO = C([128, 256])
                P.op("act", "activation", [pO2.b], [tmpO.b], out=tmpO.ap, in_=pO2.ap[:, 0:256], func=AF.Copy)
                for j in range(NSMP):
                    for half in range(2):
                        rows = slice(half * 64, half * 64 + 64)
                        P.op("dve", "tensor_copy", [tmpO.b], [oT.b], out=oT.ap[rows, :, npc + j].rearrange("p (g k) -> p g k", k=4),
                             in_=tmpO.ap[rows, j * 64:(j + 1) * 64].rearrange("p (g e) -> p g e", e=8)[:, :, 4 * half:4 * half + 4])
        if astop <= 3:
            return
        for c in range(KC):
            bank = self.proj(d["w_o_s"][c], KC, oT, ncols)
            self.xupdate(bank, c, 8, ncols, npc, has_s, bias=self.boT)
        P.op("dve", "tensor_copy", [kTd.b], [kTd.b], out=kTd.ap[:, :, 0:128], in_=kTd.ap[:, :, npc:npc + 128])
        P.op("dve", "tensor_copy", [vtok.b], [vtok.b], out=vtok.ap[:, 0, :], in_=vtok.ap[:, nblk, :])


NORMS = [(0, 1, 0), (1, 4, 3), (2, 13, 12), (3, 7, 6), (4, 10, 9)]


def _slab(W):
    K, N = W.shape
    return np.ascontiguousarray(W.reshape(K // 128, 128, N // 128, 128).transpose(2, 1, 0, 3))


def _fm(x):
    T_ = x.shape[0]
    return np.ascontiguousarray(x.T.reshape(KC, 128, T_).transpose(1, 0, 2))


def _vecT(v):
    lead = v.shape[:-1]
    a = v.reshape(lead + (KC, 128))
    return np.ascontiguousarray(np.moveaxis(a, -1, 0))


_SL_COL = np.empty((NSL, 128), np.int64)
for _sl in range(NSL):
    _SL_COL[_sl] = (_sl % 2) * FF + (_sl // 2) * 128 + np.arange(128)


def _rope_tables(pos):
    half = 32
    freq = (np.float32(10000.0) ** (-np.arange(half, dtype=np.float32) / np.float32(half))).astype(np.float32)
    ang = (pos.astype(np.float32)[None, :] * freq[:, None]).astype(np.float32)
    cos, sin = np.cos(ang).astype(np.float32), np.sin(ang).astype(np.float32)
    p = np.arange(128)
    f = (p % 64) % 32
    sign = np.where((p % 64) < 32, -1.0, 1.0).astype(np.float32)
    return cos[f], sin[f] * sign[:, None]


def _shared_inputs(I, names=None):
    f = np.float32
    S = {}

    def want(n):
        return names is None or n in names
    if want("w_ada_s"):
        S["w_ada_s"] = np.concatenate([_slab(I["w_ada"][0]), _slab(I["w_ada"][1]), _slab(I["w_ada_kv"])], 0)
    if want("w_min_s") or want("w_gate"):
        wmi = I["w_m_in"][0]
        S["w_min_s"] = _slab(wmi[:, :12288])
        S["w_gate"] = np.ascontiguousarray(wmi[:, 12288:].reshape(KC, 128, 16).transpose(1, 0, 2))
    if want("w_mout_s"):
        S["w_mout_s"] = np.ascontiguousarray(I["w_m_out"][0].reshape(2, 16, 128, 32, 128).transpose(0, 3, 2, 1, 4))
    if want("w_ffi_s"):
        order = np.empty(NSL, np.int64)
        order[0::2] = np.arange(FC)
        order[1::2] = FC + np.arange(FC)
        S["w_ffi_s"] = np.stack([_slab(I["w_ffn_in"][l])[order] for l in range(2)])
    if want("w_ffo_s"):
        ffo = np.zeros((2, 4, 32, 128, 22, 128), f)
        for l in range(2):
            W = I["w_ffn_out"][l].reshape(FC, 128, 32, 128)
            for g, (f0, nf) in enumerate(FGRP):
                ffo[l, g, :, :, :nf, :] = W[f0:f0 + nf].transpose(2, 1, 0, 3)
        S["w_ffo_s"] = ffo
    if want("w_kv_s"):
        wk = I["w_kv"][:, :512].reshape(KC, 128, 8, 64).transpose(2, 1, 0, 3)
        S["w_kv_s"] = np.ascontiguousarray(np.concatenate([np.concatenate([wk, wk], -1), _slab(I["w_kv"][:, 512:])], 0))
    if want("w_q_s"):
        S["w_q_s"] = _slab(I["w_q"][0])
    if want("w_o_s"):
        S["w_o_s"] = _slab(I["w_o"][0])
    S["gT"] = _vecT(np.stack([I["g_norm1"][0], I["g_norm2"][0], I["g_kv"], I["g_norm1"][1], I["g_norm2"][1], I["g_final"]]))
    S["gheadT"] = _vecT(I["g_m_head"][0])
    S["bif"] = np.ascontiguousarray(np.stack([I["b_m_i"][0], I["b_m_f"][0]], 1))
    bk = I["b_kv"][:512].reshape(8, 64)
    S["bkv"] = np.ascontiguousarray(np.concatenate([np.concatenate([bk, bk], 1).T, I["b_kv"][512:].reshape(4, 128).T], 1))
    S["bqT"] = _vecT(I["b_q"][0]); S["boT"] = _vecT(I["b_o"][0])
    sk = I["sinks"][0]
    S["sinks_bc"] = np.ascontiguousarray(np.broadcast_to(sk[None, :], (128, 64)))
    S["nsinks_bc"] = np.ascontiguousarray(np.broadcast_to(np.negative(sk)[None, :], (128, 64)))
    hp = np.arange(64)
    hh = 8 * (hp // 8) + 2 * (hp % 4) + ((hp % 8) // 4)
    S["sinksT"] = np.ascontiguousarray(np.stack([sk[hh], np.negative(sk[hh])], 1))
    S["wconvT"] = np.ascontiguousarray(I["w_conv"][:, :, _SL_COL].transpose(3, 0, 2, 1))
    S["bconvT"] = np.ascontiguousarray(I["b_conv"][:, _SL_COL].transpose(2, 0, 1))
    S["ident"] = np.eye(128, dtype=f)
    m = np.arange(128)
    partner = np.where((m % 64) < 32, m + 32, m - 32)
    R = np.zeros((128, 128), f)
    R[partner, m] = 1.0
    S["Rmat"] = R
    S["cmask"] = (m[:, None] <= m[None, :]).astype(f)
    t = np.arange(128)[:, None]
    j = np.arange(256)[None, :]
    valid = np.where(j < 128, j > t, (j - 128) <= t)
    am = np.where(valid, 0.0, -30000.0).astype(f)
    am1 = am.copy()
    am1[:, :128] = -30000.0
    S["amask"] = np.stack([am, am1])
    bm = np.zeros((8, 8, 4), f)
    for k in range(8):
        bm[k, k, :] = 1.0
    S["bm"] = bm
    hp_ = np.arange(64)
    mg = np.zeros((64, 16), f)
    mg[hp_, ((hp_ % 8) // 4) * 8 + hp_ // 8] = 1.0
    S["mg"] = mg
    return {k: np.ascontiguousarray(v, dtype=f) for k, v in S.items()}


def _core_inputs(I, core):
    f = np.float32
    b, half = core // 2, core % 2
    xp = I["x_prompt"][b]
    s0 = 0 if half == 0 else 896
    sq = slice(4 * core, 4 * core + 4)
    M = {}
    M["xpre"] = _fm(xp[0:NPRE]); M["xfull"] = _fm(xp[s0:s0 + NFULL]); M["xs"] = _fm(I["x_sample"][sq, 0])
    M["c5T"] = _fm(np.concatenate([I["c_prompt"][b:b + 1], I["c_sample"][sq]], 0))
    M["flag"] = np.full((128, 1), float(half), f)
    rp = np.zeros((3, 128, 2, NCMAX), f)
    for ti in range(3):
        pos = np.concatenate([s0 + ti * 384 + np.arange(384), np.full(4, 16384)])
        c_, s_ = _rope_tables(pos)
        rp[ti, :, 0], rp[ti, :, 1] = c_, s_
    M["rope"] = rp
    M["st_C"] = I["state_mlstm_C"][0, sq].reshape(4, H, 2, 128, 512).transpose(0, 1, 3, 2, 4)
    M["st_n"] = I["state_mlstm_n"][0, sq].reshape(4, H, 2, 128).transpose(3, 1, 2, 0)
    M["st_m"] = I["state_mlstm_m"][0, sq].T
    cc = I["cache_conv"][:, sq]
    M["cconv"] = cc[:, :, :, _SL_COL].transpose(0, 4, 3, 1, 2)
    M["cconv_r1"] = cc[:, :, 1, :]
    ck = I["cache_k_win"][sq, 1:]
    ckt = ck.transpose(0, 3, 2, 1)
    M["ckT"] = np.concatenate([ckt, ckt], 1)
    M["ck_old"] = ck.reshape(4, 127, 512)
    M["cv_old"] = I["cache_v_win"][sq, 1:].reshape(4, 127, 512)
    return {k: np.ascontiguousarray(v, dtype=f) for k, v in M.items()}


_NC_CACHE = []


def kernel(**inputs):
    I = {k: np.asarray(v) for k, v in inputs.items()}
    if not _NC_CACHE:
        _NC_CACHE.append(Builder().build())
    nc = _NC_CACHE[0]
    shared = _shared_inputs(I)
    in_maps = []
    for core in range(8):
        m = dict(shared)
        m.update(_core_inputs(I, core))
        in_maps.append(m)
    res = run_bass_kernel_spmd(nc, in_maps, core_ids=list(range(8))).results
    f = np.float32
    y_prompt = np.zeros((4, 2048, D), f); y_sample = np.zeros((32, 1, D), f)
    C_p = np.zeros((1, 4, H, 256, 512), f); n_p = np.zeros((1, 4, H, 256), f); m_p = np.zeros((1, 4, H), f)
    conv_p = np.zeros((2, 4, 2, 2 * FF), f); kw_p = np.zeros((4, 128, 8, 64), f); vw_p = np.zeros((4, 128, 8, 64), f)
    C_s = np.zeros((1, 32, H, 256, 512), f); n_s = np.zeros((1, 32, H, 256), f); m_s = np.zeros((1, 32, H), f)
    conv_s = np.zeros((2, 32, 2, 2 * FF), f); kw_s = np.zeros((32, 128, 8, 64), f); vw_s = np.zeros((32, 128, 8, 64), f)
    for core in range(8):
        r = res[core]
        b, half = core // 2, core % 2
        sq = slice(4 * core, 4 * core + 4)
        yt = r["yT"].transpose(2, 1, 0).reshape(NFULL, D)
        if half == 0:
            y_prompt[b, 0:NFULL] = yt
        else:
            y_prompt[b, NFULL:2048] = yt[256:]
            C_p[0, b] = r["C_out"].transpose(0, 2, 1, 3).reshape(H, 256, 512)
            n_p[0, b] = r["n_out"].transpose(1, 2, 0).reshape(H, 256)
            m_p[0, b] = r["m_out"][:, 0]
            cp = r["convp"]
            for l in range(2):
                conv_p[l, b][:, _SL_COL] = cp[l].transpose(2, 1, 0)
            kw_p[b] = r["kwinp"].transpose(2, 1, 0)
            vw_p[b] = r["vwinp"].transpose(2, 1, 0).reshape(128, 4, 2, 64).reshape(128, 8, 64)
        y_sample[sq, 0] = r["ysT"].transpose(2, 1, 0).reshape(4, D)
        C_s[0, sq] = r["Cs_out"].transpose(0, 1, 3, 2, 4).reshape(4, H, 256, 512)
        n_s[0, sq] = r["ns_out"].transpose(3, 1, 2, 0).reshape(4, H, 256)
        m_s[0, sq] = r["ms_out"].T
        conv_s[:, sq, 0] = r["convs_old"]
        cn = r["convs_new"]
        for l in range(2):
            conv_s[l, sq, 1][:, _SL_COL] = cn[l].transpose(2, 1, 0)
        kw_s[sq, 0:127] = r["kwins_old"].reshape(4, 127, 8, 64)
        kw_s[sq, 127] = r["knew"].transpose(2, 1, 0)
        vw_s[sq, 0:127] = r["vwins_old"].reshape(4, 127, 8, 64)
        vw_s[sq, 127] = r["vnew"].reshape(4, 8, 64)
    return (y_prompt, y_sample, C_p, n_p, m_p, conv_p, kw_p, vw_p, C_s, n_s, m_s, conv_s, kw_s, vw_s)
```

```python
import contextlib
import numpy as np
import concourse.bass as bass
import concourse.mybir as mybir
from concourse.bass_utils import run_bass_kernel_spmd

F32 = mybir.dt.float32
BF16 = mybir.dt.bfloat16
AF = mybir.ActivationFunctionType
ALU = mybir.AluOpType
AX = mybir.AxisListType

D = 4096
KC = 32
FF = 11008
FC = 86
NSL = 172
H = 8
NPRE = 896
NFULL = 1152
NSMP = 4
BLK = 128
EPS = 1e-6
FGRP = [(0, 22), (22, 22), (44, 21), (65, 21)]
NCMAX = 388
NSLOT = 3
ARENA_BYTES = 57000

ENGS = ("pe", "act", "dve", "pool", "sp")


class Buf:
    __slots__ = ("w", "r")

    def __init__(self):
        self.w = {}
        self.r = {}


class Prog:
    def __init__(self, nc, es):
        self.nc = nc
        self.es = es
        self.q = {e: [] for e in ENGS}
        self.waited = {e: {} for e in ENGS}
        self.cnt = {}
        self.sems = {}
        for e in ("pe", "act", "dve", "pool"):
            self.new_sem(e)
        self.rr = 0

    def new_sem(self, key):
        self.sems[key] = self.es.enter_context(self.nc.semaphore("s_" + key))
        self.cnt[key] = 0
        return key

    @staticmethod
    def _deps(reads, writes):
        d = {}
        for b in reads:
            for k, v in b.w.items():
                if d.get(k, 0) < v:
                    d[k] = v
        for b in writes:
            for k, v in b.w.items():
                if d.get(k, 0) < v:
                    d[k] = v
            for k, v in b.r.items():
                if d.get(k, 0) < v:
                    d[k] = v
        return d

    def _record(self, eng, deps, fn, semkey, amount, reads, writes, skip_self=False):
        waits = []
        wd = self.waited[eng]
        for k, v in deps.items():
            if skip_self and k == eng:
                continue
            if wd.get(k, 0) < v:
                wd[k] = v
                waits.append((k, v))
        self.cnt[semkey] += amount
        val = self.cnt[semkey]
        for b in writes:
            b.w = {semkey: val}
            b.r = {}
        for b in reads:
            b.r[semkey] = val
        self.q[eng].append((waits, fn, semkey, amount))

    def op(self, eng, method, reads, writes, **kw):
        self._record(eng, self._deps(reads, writes), lambda h: getattr(h, method)(**kw), eng, 1,
                     reads, writes, skip_self=(eng == "pe"))

    def mm(self, out, pairs, reads, writes, start=True, stop=True):
        n = len(pairs)

        def fn(h):
            ins = None
            for i, (l, r) in enumerate(pairs):
                ins = h.matmul(out, lhsT=l, rhs=r, start=(start and i == 0), stop=(stop and i == n - 1))
            return ins
        self._record("pe", self._deps(reads, writes), fn, "pe", 1, reads, writes, skip_self=True)

    def dma(self, qeng, semkey, reads, writes, **kw):
        deps = self._deps(reads, writes)
        if deps.get(semkey, 0) < self.cnt[semkey]:
            deps[semkey] = self.cnt[semkey]
        self._record(qeng, deps, lambda h: h.dma_start(**kw), semkey, 16, reads, writes)

    def final_wait(self, eng):
        waits = [(k, v) for k, v in self.cnt.items() if v > 0]
        self.q[eng].append((waits, None, None, 0))

    def emit(self):
        nc = self.nc
        with nc.Block() as block:
            def run(engname):
                def body(h):
                    for waits, fn, semkey, amount in self.q[engname]:
                        for k, v in waits:
                            h.wait_ge(self.sems[k], v)
                        if fn is not None:
                            fn(h).then_inc(self.sems[semkey], amount)
                return body
            block.tensor(run("pe"))
            block.scalar(run("act"))
            block.vector(run("dve"))
            block.gpsimd(run("pool"))
            block.sync(run("sp"))


class _Lazy(dict):
    def __init__(self, b, shapes, kind):
        super().__init__()
        self.b, self.shapes, self.kind = b, shapes, kind

    def __missing__(self, name):
        shp = list(self.shapes[name])
        ov = self.b.dbg.get("shape_" + name)
        if ov is not None:
            shp = list(ov)
        ap = self.b.nc.dram_tensor(name, shp, F32, kind=self.kind).ap()
        self[name] = ap
        return ap


class T:
    __slots__ = ("ap", "b")

    def __init__(self, ap, b=None):
        self.ap = ap
        self.b = b if b is not None else Buf()


class Builder:
    def __init__(self, dbg=None):
        self.nc = bass.Bass("TRN2", target_bir_lowering=False)
        self.es = contextlib.ExitStack()
        self.dbg = dbg or {}

    def dbg_dump(self, name, t, ap=None):
        ap = t.ap if ap is None else ap
        shp = [int(x) for x in ap.shape]
        self.dout.shapes[name] = shp
        self.store(self.dout[name], t, ap)

    def inp(self, name, shape, dt=F32):
        self.din[name] = self.nc.dram_tensor(name, list(shape), dt, kind="ExternalInput").ap()
        return self.din[name]

    def outp(self, name, shape, dt=F32):
        self.dout[name] = self.nc.dram_tensor(name, list(shape), dt, kind="ExternalOutput").ap()
        return self.dout[name]

    def sb(self, name, shape, dt=F32):
        return T(self.es.enter_context(self.nc.sbuf_tensor(name, list(shape), dt))[:])

    def arena_reset(self):
        P = self.P
        ev = {}
        for t in self.arena_live:
            for dct in (t.b.w, t.b.r):
                for k, v in dct.items():
                    if ev.get(k, 0) < v:
                        ev[k] = v
        self.arena_prev = ev
        self.arena_live = []
        self.arena_off = 0

    def carve(self, shape, dt=F32):
        esz = 4 if dt == F32 else 2
        n = 1
        for s in shape[1:]:
            n *= s
        nbytes = (n * esz + 7) // 8 * 8
        off = self.arena_off
        assert off + nbytes <= ARENA_BYTES, ("arena overflow", off, nbytes)
        self.arena_off += nbytes
        ap = self.arena[0:shape[0], off // 2:(off + n * esz) // 2]
        if dt == F32:
            ap = ap.bitcast(F32)
        if len(shape) == 3:
            ap = ap.rearrange("p (a b) -> p a b", a=shape[1])
        elif len(shape) == 4:
            ap = ap.rearrange("p (a b c) -> p a b c", a=shape[1], b=shape[2])
        t = T(ap)
        t.b.r = dict(self.arena_prev)
        self.arena_live.append(t)
        return t

    def wslab(self, src, kcn, ncol=128):
        i = self.wi
        self.wi += 1
        s = i % NSLOT
        slot = self.slots[s]
        view = slot.ap[:, 0:kcn * ncol].rearrange("p (k n) -> p k n", k=kcn)
        self.P.dma("pool", f"w{s}", [], [slot.b], out=view, in_=src)
        return T(view, slot.b)

    def small_load(self, dst, src, q="sp", dram_buf=None, dst_ap=None):
        k = f"d{self.P.rr % 8}"
        self.P.rr += 1
        self.P.dma(q, k, [] if dram_buf is None else [dram_buf], [dst.b], out=(dst.ap if dst_ap is None else dst_ap), in_=src)

    def store(self, dst_dram, src, src_ap=None, q="sp", dram_buf=None):
        k = f"o{self.P.rr % 8}"
        self.P.rr += 1
        self.P.dma(q, k, [src.b], [] if dram_buf is None else [dram_buf], out=dst_dram, in_=(src.ap if src_ap is None else src_ap))

    def build(self):
        nc, es = self.nc, self.es
        with es:
            self.P = P = Prog(nc, es)
            for i in range(NSLOT):
                P.new_sem(f"w{i}")
            for i in range(8):
                P.new_sem(f"d{i}")
                P.new_sem(f"o{i}")
            self.declare_dram()
            self.alloc()
            stage = self.dbg.get("stage", 99)
            self.prologue()
            if stage >= 1:
                self.adaln()
            if "dump_mod" in self.dbg:
                self.dbg_dump("dbg_modT", self.modT)
                self.dbg_dump("dbg_Amod", self.Amod)
            self.first_state = True
            if stage >= 2:
                for ti, nb in enumerate((3, 3, 1)[:self.dbg.get("npre", 3)]):
                    self.run_tile("pre", ti, nb, ti * 384, False)
            if "dump_pre" in self.dbg:
                self.dbg_dump("dbg_nst", self.nst); self.dbg_dump("dbg_mst", self.mst)
                self.dout.shapes["dbg_C"] = [H, 128, 2, 512]
                for h in range(H):
                    self.P.dma("sp", f"o{h}", [self.cscr_b[h]], [], out=self.dout["dbg_C"][h], in_=self.C_scr[h])
            if stage >= 3:
                for ti in range(self.dbg.get("nmain", 3)):
                    self.run_tile("main", ti, 3, ti * 384, ti == 0)
            P.final_wait("sp")
            P.emit()
        return nc

    IN_SHAPES = {
        "xpre": [128, KC, NPRE], "xfull": [128, KC, NFULL], "xs": [128, KC, NSMP], "c5T": [128, KC, 5], "flag": [128, 1],
        "rope": [3, 128, 2, NCMAX], "st_C": [4, H, 128, 2, 512], "st_n": [128, H, 2, 4], "st_m": [8, 4],
        "cconv": [2, 128, NSL, 4, 2], "cconv_r1": [2, 4, 2 * FF], "ckT": [4, 128, 8, 127], "ck_old": [4, 127, 512],
        "cv_old": [4, 127, 512], "w_ada_s": [448, 128, KC, 128], "w_min_s": [96, 128, KC, 128], "w_gate": [128, KC, 16],
        "w_mout_s": [2, 32, 128, 16, 128], "w_ffi_s": [2, NSL, 128, KC, 128], "w_ffo_s": [2, 4, 32, 128, 22, 128],
        "w_kv_s": [12, 128, KC, 128], "w_q_s": [32, 128, KC, 128], "w_o_s": [32, 128, KC, 128],
        "gT": [128, 6, KC], "gheadT": [128, KC], "bif": [8, 2], "bkv": [128, 12], "bqT": [128, KC], "boT": [128, KC],
        "nsinks_bc": [128, 64], "sinks_bc": [128, 64], "sinksT": [64, 2], "wconvT": [128, 2, NSL, 3], "bconvT": [128, 2, NSL],
        "ident": [128, 128], "Rmat": [128, 128], "cmask": [128, 128], "amask": [2, 128, 256], "bm": [8, 8, 4], "mg": [64, 16]}
    OUT_SHAPES = {
        "yT": [128, KC, NFULL], "ysT": [128, KC, NSMP], "C_out": [H, 128, 2, 512], "n_out": [128, H, 2], "m_out": [8, 1],
        "convp": [2, 128, NSL, 2], "kwinp": [64, 8, 128], "vwinp": [128, 4, 128], "Cs_out": [4, H, 128, 2, 512],
        "ns_out": [128, H, 2, 4], "ms_out": [8, 4], "convs_old": [2, 4, 2 * FF], "convs_new": [2, 128, NSL, 4],
        "kwins_old": [4, 127, 512], "knew": [64, 8, 4], "vwins_old": [4, 127, 512], "vnew": [4, 512]}

    def declare_dram(self):
        self.din = _Lazy(self, self.IN_SHAPES, "ExternalInput")
        self.dout = _Lazy(self, self.OUT_SHAPES, "ExternalOutput")
        self.C_scr = self.nc.dram_tensor("C_scr", [H, 128, 2, 512], F32).ap()
        self.cscr_b = [Buf() for _ in range(H)]

    def alloc(self):
        sb, nc, es = self.sb, self.nc, self.es
        self.xT = sb("xT", [128, KC, NCMAX]); self.hn = sb("hn", [128, KC, NCMAX], BF16)
        self.slots = [sb(f"slot{i}", [128, KC * 128], BF16) for i in range(NSLOT)]
        self.wi = 0
        self.modT = sb("modT", [128, 448, 5]); self.Amod = sb("Amod", [128, 5, KC, 5])
        self.gT = sb("gTs", [128, 6, KC]); self.gheadT = sb("gheadTs", [128, KC])
        self.bif = sb("bifs", [8, 2]); self.bif15 = sb("bif15", [8, 2]); self.bkv = sb("bkvs", [128, 12])
        self.bqT = sb("bqTs", [128, KC]); self.boT = sb("boTs", [128, KC])
        self.nsinks = sb("nsinks", [128, 64]); self.sinks = sb("sinkss", [128, 64]); self.sinksT = sb("sinksTs", [64, 2])
        self.wconvT = sb("wconvTs", [128, 2, NSL, 3]); self.bconvT = sb("bconvTs", [128, 2, NSL])
        self.ident = sb("idents", [128, 128]); self.identb = sb("identb", [128, 128], BF16)
        self.ones = sb("ones", [128, 128]); self.onesb = sb("onesb", [128, 2], BF16)
        self.Rmat = sb("Rmats", [128, 128]); self.cmask = sb("cmasks", [128, 128]); self.amask = sb("amasks", [128, 2, 256])
        self.bm = sb("bms", [8, 8, 4]); self.flag = sb("flags", [128, 1]); self.mg = sb("mgs", [64, 16])
        self.wg = sb("wg", [128, KC, 16], BF16)
        self.ropeT = sb("ropeT", [128, 2, NCMAX])
        self.nst = sb("nst", [128, H, 2]); self.mst = sb("mst", [8, 1])
        self.uhist = sb("uhist", [128, 2, NSL, 2])
        self.kTd = sb("kTd", [128, 8, 128 + NCMAX], BF16); self.vtok = sb("vtok", [128, 4, 512], BF16)
        self.sq = [sb(f"sq{i}", [128, NCMAX]) for i in range(2)]
        self.tmp = [sb(f"tmp{i}", [128, NCMAX]) for i in range(2)]
        self.rstd = sb("rstd", [128, NCMAX]); self.tmps = sb("tmps", [128, 4])
        self.cs32 = sb("cs32", [128, KC, 5]); self.csT = sb("csT", [128, KC, 5], BF16)
        self.epsT = sb("epsT", [128, 1])
        self.arena = es.enter_context(nc.sbuf_tensor("arena", [128, ARENA_BYTES // 2], BF16))[:]
        self.arena_live = []
        self.arena_prev = {}
        self.arena_off = 0
        self.ps = [T(es.enter_context(nc.psum_tensor(f"ps{i}", [128, 512], F32))[:]) for i in range(8)]
        self.pj = 0

    def pbank(self):
        self.pj ^= 1
        return self.ps[2 if self.pj else 7]

    def prologue(self):
        P, L, d = self.P, self.small_load, self.din
        for t, n in ((self.gT, "gT"), (self.gheadT, "gheadT"), (self.bif, "bif"), (self.bkv, "bkv"), (self.bqT, "bqT"),
                     (self.boT, "boT"), (self.nsinks, "nsinks_bc"), (self.sinks, "sinks_bc"), (self.sinksT, "sinksT"),
                     (self.wconvT, "wconvT"), (self.bconvT, "bconvT"), (self.ident, "ident"), (self.Rmat, "Rmat"),
                     (self.cmask, "cmask"), (self.bm, "bm"), (self.flag, "flag"), (self.cs32, "c5T"), (self.mg, "mg")):
            L(t, d[n])
        L(self.amask, d["amask"].rearrange("v p s -> p v s"))
        P.dma("pool", "d0", [], [self.wg.b], out=self.wg.ap, in_=d["w_gate"])
        P.op("dve", "tensor_copy", [self.ident.b], [self.identb.b], out=self.identb.ap, in_=self.ident.ap)
        P.op("dve", "memset", [], [self.ones.b], ap=self.ones.ap, constant=1.0)
        P.op("dve", "memset", [], [self.onesb.b], ap=self.onesb.ap, constant=1.0)
        P.op("dve", "memset", [], [self.epsT.b], ap=self.epsT.ap, constant=EPS)
        P.op("dve", "memset", [], [self.uhist.b], ap=self.uhist.ap, constant=0.0)
        P.op("dve", "memset", [], [self.kTd.b], ap=self.kTd.ap, constant=0.0)
        P.op("dve", "memset", [], [self.vtok.b], ap=self.vtok.ap, constant=0.0)
        P.op("dve", "memset", [], [self.nst.b], ap=self.nst.ap, constant=0.0)
        P.op("dve", "memset", [], [self.mst.b], ap=self.mst.ap, constant=0.0)
        P.op("dve", "tensor_scalar", [self.bif.b], [self.bif15.b], out=self.bif15.ap, in0=self.bif.ap,
             scalar1=1.0 / 15.0, scalar2=None, op0=ALU.mult)
        P.op("act", "activation", [self.cs32.b], [self.csT.b], out=self.csT.ap, in_=self.cs32.ap, func=AF.Silu)
        for src, dst in (("cconv_r1", "convs_old"), ("ck_old", "kwins_old"), ("cv_old", "vwins_old")):
            k = f"o{P.rr % 8}"
            P.rr += 1
            P.dma("sp", k, [], [], out=self.dout[dst], in_=d[src])

    def adaln(self):
        P = self.P
        src = self.din["w_ada_s"]
        for s in range(self.dbg.get("ada_slabs", 448)):
            slab = self.wslab(src[s], KC)
            bank = self.ps[(s // 64) % 2]
            col = (s % 64) * 5
            P.mm(bank.ap[:, col:col + 5], [(slab.ap[:, kc, :], self.csT.ap[:, kc, :]) for kc in range(KC)],
                 [slab.b, self.csT.b], [bank.b])
            if s % 64 == 63:
                s0 = s - 63
                P.op("act", "activation", [bank.b], [self.modT.b],
                     out=self.modT.ap[:, s0:s0 + 64, :].rearrange("p a b -> p (a b)"), in_=bank.ap[:, 0:320], func=AF.Copy)
        for n, (gi, sc, sh) in enumerate(NORMS):
            for q in range(5):
                P.op("dve", "scalar_tensor_tensor", [self.modT.b, self.gT.b], [self.Amod.b],
                     out=self.Amod.ap[:, n, :, q], in0=self.modT.ap[:, sc * 32:(sc + 1) * 32, q], scalar=1.0,
                     in1=self.gT.ap[:, gi, :], op0=ALU.add, op1=ALU.mult)

    def norm(self, n, ncols, npc, has_s, out_final=None):
        P, xT, hn = self.P, self.xT, self.hn
        pn = self.ps[6]
        for kc in range(KC):
            sq = self.sq[kc % 2]
            P.op("act", "activation", [xT.b], [sq.b], out=sq.ap[:, :ncols], in_=xT.ap[:, kc, :ncols], func=AF.Square)
            P.op("pe", "matmul", [sq.b, self.ones.b], [pn.b], out=pn.ap[:, :ncols], lhsT=self.ones.ap, rhs=sq.ap[:, :ncols],
                 start=(kc == 0), stop=(kc == KC - 1))
        P.op("act", "activation", [pn.b, self.epsT.b], [self.rstd.b], out=self.rstd.ap[:, :ncols], in_=pn.ap[:, :ncols],
             func=AF.Sqrt, scale=1.0 / D, bias=self.epsT.ap[:, 0:1])
        P.op("dve", "reciprocal", [self.rstd.b], [self.rstd.b], out=self.rstd.ap[:, :ncols], in_=self.rstd.ap[:, :ncols])
        for kc in range(KC):
            tmp = self.tmp[kc % 2]
            P.op("dve", "tensor_tensor", [xT.b, self.rstd.b], [tmp.b], out=tmp.ap[:, :ncols], in0=xT.ap[:, kc, :ncols],
                 in1=self.rstd.ap[:, :ncols], op=ALU.mult)
            if n == 5:
                P.op("act", "activation", [tmp.b, self.gT.b], [out_final.b], out=out_final.ap[:, kc, :ncols], in_=tmp.ap[:, :ncols],
                     func=AF.Copy, scale=self.gT.ap[:, 5, kc:kc + 1])
                continue
            gi, sc, sh = NORMS[n]
            P.op("act", "activation", [tmp.b, self.Amod.b, self.modT.b], [hn.b], out=hn.ap[:, kc, :npc], in_=tmp.ap[:, :npc],
                 func=AF.Identity, scale=self.Amod.ap[:, n, kc, 0:1], bias=self.modT.ap[:, sh * 32 + kc, 0:1])
            if has_s:
                P.op("dve", "tensor_tensor", [tmp.b, self.Amod.b], [self.tmps.b], out=self.tmps.ap, in0=tmp.ap[:, npc:npc + 4],
                     in1=self.Amod.ap[:, n, kc, 1:5], op=ALU.mult)
                P.op("dve", "tensor_tensor", [self.tmps.b, self.modT.b], [hn.b], out=hn.ap[:, kc, npc:npc + 4], in0=self.tmps.ap,
                     in1=self.modT.ap[:, sh * 32 + kc, 1:5], op=ALU.add)

    def proj(self, src, kcn, rhs_t, ncols, M=128):
        slab = self.wslab(src, kcn)
        bank = self.pbank()
        self.P.mm(bank.ap[0:M, :ncols], [(slab.ap[:, kc, 0:M], rhs_t.ap[:, kc, :ncols]) for kc in range(kcn)],
                  [slab.b, rhs_t.b], [bank.b])
        return bank

    def xupdate(self, bank, c, sec, ncols, npc, has_s, bias=None):
        P, xT, modT = self.P, self.xT, self.modT
        src = bank.ap
        rd = [bank.b]
        if bias is not None:
            t = self.tmp[0]
            P.op("dve", "tensor_scalar", [bank.b, bias.b], [t.b], out=t.ap[:, :ncols], in0=bank.ap[:, :ncols],
                 scalar1=bias.ap[:, c:c + 1], scalar2=None, op0=ALU.add)
            src = t.ap
            rd = [t.b]
        P.op("dve", "scalar_tensor_tensor", rd + [modT.b, xT.b], [xT.b], out=xT.ap[:, c, :npc], in0=src[:, :npc],
             scalar=modT.ap[:, sec * 32 + c, 0:1], in1=xT.ap[:, c, :npc], op0=ALU.mult, op1=ALU.add)
        if has_s:
            P.op("dve", "tensor_tensor", rd + [modT.b], [self.tmps.b], out=self.tmps.ap, in0=src[:, npc:npc + 4],
                 in1=modT.ap[:, sec * 32 + c, 1:5], op=ALU.mult)
            P.op("dve", "tensor_tensor", [self.tmps.b, xT.b], [xT.b], out=xT.ap[:, c, npc:npc + 4], in0=self.tmps.ap,
                 in1=xT.ap[:, c, npc:npc + 4], op=ALU.add)

    def run_tile(self, kind, ti, nblk, col0, has_s):
        P, d = self.P, self.din
        npc = nblk * BLK
        ncols = npc + (4 if has_s else 0)
        self.arena_reset()
        src = d["xpre"] if kind == "pre" else d["xfull"]
        P.dma("sp", "d0", [], [self.xT.b], out=self.xT.ap[:, :, :npc], in_=src[:, :, col0:col0 + npc])
        if has_s:
            P.dma("sp", "d1", [], [self.xT.b], out=self.xT.ap[:, :, npc:npc + 4], in_=d["xs"])
            self.xT.b.w["d0"] = P.cnt["d0"]
        if not self.dbg.get("skip_mixer"):
            self.norm(0, ncols, npc, has_s)
            self.mixer(kind, ti, nblk, ncols, npc, has_s)
        if kind == "pre":
            return
        ms = self.dbg.get("main_stop", 99)
        if ms <= 1:
            return self.dbg_dump(f"dbg_x{ti}", self.xT)
        self.arena_reset()
        if not self.dbg.get("skip_ffn0"):
            self.norm(1, ncols, npc, has_s)
            self.ffn(0, ti, ncols, npc, has_s)
        if ms <= 2:
            return self.dbg_dump(f"dbg_x{ti}", self.xT)
        self.arena_reset()
        self.attention(ti, nblk, ncols, npc, has_s)
        if ms <= 3:
            return self.dbg_dump(f"dbg_x{ti}", self.xT)
        self.arena_reset()
        self.norm(4, ncols, npc, has_s)
        self.ffn(1, ti, ncols, npc, has_s)
        self.arena_reset()
        ybuf = self.carve([128, KC, NCMAX])
        self.norm(5, ncols, npc, has_s, out_final=ybuf)
        self.store(self.dout["yT"][:, :, col0:col0 + npc], ybuf, ybuf.ap[:, :, :npc])
        if has_s:
            self.store(self.dout["ysT"], ybuf, ybuf.ap[:, :, npc:npc + 4])
        if ti == 2:
            for l in range(2):
                self.store(self.dout["convp"][l], self.uhist, self.uhist.ap[:, l])

    def mixer(self, kind, ti, nblk, ncols, npc, has_s):
        P, d, ps, hn = self.P, self.din, self.ps, self.hn
        main = kind == "main"
        C = self.carve
        qT = C([128, 2, NCMAX], BF16); kT = C([128, 2, NCMAX], BF16)
        vT = C([128, 4, NCMAX], BF16); oT = C([128, 4, NCMAX], BF16)
        ktok = [C([128, 256], BF16) for _ in range(2)]
        vtk = [C([128, 512], BF16) for _ in range(2)]
        otk = [C([128, 512], BF16) for _ in range(2)]
        PT = [C([128, 128], BF16) for _ in range(2)]
        Cp = C([128, 2, 512], BF16); npb = C([128, 2], BF16)
        hsg = [C([128, 512], BF16) for _ in range(2)]
        junk = C([128, 512], BF16)
        Cst = [C([128, 2, 512]) for _ in range(2)]
        hsT = C([128, 16, NCMAX], BF16)
        ig = C([8, NCMAX]); lf = C([8, NCMAX]); bt = C([8, NCMAX]); at = C([8, NCMAX]); ut = C([8, NCMAX]); dnf = C([8, NCMAX])
        tokm = C([128, 3, 2, 8]); sdbc = C([128, 8, 4]); Rexp = C([8, 8, 4])
        P.op("dve", "memset", [], [Rexp.b], ap=Rexp.ap, constant=0.0)
        sm = C([8, 16]); cvec = C([8, 4]); negc = C([8, 4]); sdv = C([8, 4])
        col = C([128, 8])
        ones8 = self.ones.ap[0:8, 0:BLK]
        pg = ps[0]
        P.mm(pg.ap[0:8, :ncols], [(self.wg.ap[:, kc, 0:8], hn.ap[:, kc, :ncols]) for kc in range(KC)], [self.wg.b, hn.b], [pg.b])
        pf = ps[1]
        P.mm(pf.ap[0:8, :ncols], [(self.wg.ap[:, kc, 8:16], hn.ap[:, kc, :ncols]) for kc in range(KC)], [self.wg.b, hn.b], [pf.b])
        P.op("act", "activation", [pg.b, self.bif15.b], [ig.b], out=ig.ap[:, :ncols], in_=pg.ap[0:8, :ncols], func=AF.Tanh,
             scale=1.0 / 15.0, bias=self.bif15.ap[:, 0:1])
        P.op("dve", "tensor_scalar", [ig.b], [ig.b], out=ig.ap[:, :ncols], in0=ig.ap[:, :ncols], scalar1=15.0, scalar2=None, op0=ALU.mult)
        P.op("act", "activation", [pf.b, self.bif15.b], [lf.b], out=lf.ap[:, :ncols], in_=pf.ap[0:8, :ncols], func=AF.Tanh,
             scale=1.0 / 15.0, bias=self.bif15.ap[:, 1:2])
        P.op("act", "activation", [lf.b], [lf.b], out=lf.ap[:, :ncols], in_=lf.ap[:, :ncols], func=AF.Exp, scale=-15.0)
        P.op("dve", "tensor_scalar", [lf.b], [lf.b], out=lf.ap[:, :ncols], in0=lf.ap[:, :ncols], scalar1=1.0, scalar2=None, op0=ALU.add)
        P.op("act", "activation", [lf.b], [lf.b], out=lf.ap[:, :ncols], in_=lf.ap[:, :ncols], func=AF.Ln)
        P.op("dve", "tensor_scalar", [lf.b], [lf.b], out=lf.ap[:, :ncols], in0=lf.ap[:, :ncols], scalar1=-1.0, scalar2=None, op0=ALU.mult)
        if main and ti == 0:
            P.op("dve", "tensor_scalar", [self.nst.b, self.flag.b], [self.nst.b], out=self.nst.ap, in0=self.nst.ap,
                 scalar1=self.flag.ap[:, 0:1], scalar2=None, op0=ALU.mult)
            P.op("dve", "tensor_scalar", [self.mst.b, self.flag.b], [self.mst.b], out=self.mst.ap, in0=self.mst.ap,
                 scalar1=self.flag.ap[0:8, 0:1], scalar2=None, op0=ALU.mult)
        mst = self.mst
        for blk in range(nblk):
            cs = slice(blk * BLK, (blk + 1) * BLK)
            P.op("dve", "tensor_tensor_scan", [lf.b, self.ones.b], [bt.b], out=bt.ap[:, cs], data0=ones8, data1=lf.ap[:, cs],
                 initial=0.0, op0=ALU.mult, op1=ALU.add)
            P.op("dve", "tensor_tensor", [ig.b, bt.b], [at.b], out=at.ap[:, cs], in0=ig.ap[:, cs], in1=bt.ap[:, cs], op=ALU.subtract)
            P.op("dve", "tensor_reduce", [at.b], [sm.b], out=sm.ap[:, blk:blk + 1], in_=at.ap[:, cs], axis=AX.X, op=ALU.max)
            P.op("dve", "tensor_tensor", [sm.b, mst.b], [cvec.b], out=cvec.ap[:, blk:blk + 1], in0=sm.ap[:, blk:blk + 1],
                 in1=mst.ap[:, 0:1], op=ALU.max)
            P.op("dve", "tensor_scalar", [cvec.b], [negc.b], out=negc.ap[:, blk:blk + 1], in0=cvec.ap[:, blk:blk + 1],
                 scalar1=-1.0, scalar2=None, op0=ALU.mult)
            P.op("act", "activation", [mst.b, negc.b], [sdv.b], out=sdv.ap[:, blk:blk + 1], in_=mst.ap[:, 0:1], func=AF.Exp,
                 bias=negc.ap[:, blk:blk + 1])
            P.op("act", "activation", [at.b, negc.b], [ut.b], out=ut.ap[:, cs], in_=at.ap[:, cs], func=AF.Exp,
                 bias=negc.ap[:, blk:blk + 1])
            P.op("act", "activation", [bt.b, negc.b], [dnf.b], out=dnf.ap[:, cs], in_=bt.ap[:, cs], func=AF.Exp, scale=-1.0,
                 bias=negc.ap[:, blk:blk + 1])
            P.op("dve", "tensor_tensor", [bt.b, cvec.b], [mst.b], out=mst.ap[:, 0:1], in0=bt.ap[:, (blk + 1) * BLK - 1:(blk + 1) * BLK],
                 in1=cvec.ap[:, blk:blk + 1], op=ALU.add)
        pt = ps[0]
        for blk in range(nblk):
            cs = slice(blk * BLK, (blk + 1) * BLK)
            for j, src in enumerate((ut, dnf)):
                o0 = (blk * 2 + j) * 8
                P.op("pe", "matmul", [src.b, self.ident.b], [pt.b], out=pt.ap[:, o0:o0 + 8], lhsT=src.ap[0:8, cs],
                     rhs=self.ident.ap[0:8, 0:8], start=True, stop=True)
        P.op("dve", "tensor_copy", [pt.b], [tokm.b], out=tokm.ap[:, 0:nblk].rearrange("p a b c -> p (a b c)"),
             in_=pt.ap[:, 0:nblk * 16])
        for h in range(H):
            P.op("dve", "tensor_tensor", [sdv.b, self.bm.b], [Rexp.b], out=Rexp.ap[:, h, 0:nblk], in0=sdv.ap[:, 0:nblk],
                 in1=self.bm.ap[:, h, 0:nblk], op=ALU.mult)
        pb_ = ps[1]
        P.op("pe", "matmul", [Rexp.b, self.ones.b], [pb_.b], out=pb_.ap[:, 0:32], lhsT=self.ones.ap[0:8, :],
             rhs=Rexp.ap.rearrange("p a b -> p (a b)"), start=True, stop=True)
        P.op("dve", "tensor_copy", [pb_.b], [sdbc.b], out=sdbc.ap.rearrange("p a b -> p (a b)"), in_=pb_.ap[:, 0:32])
        if has_s:
            S = self.sample_gates(ig, lf, npc)
        wsrc = d["w_min_s"]
        for h in range(H):
            Cs = Cst[h % 2]
            if self.first_state:
                P.op("dve", "memset", [], [Cs.b], ap=Cs.ap, constant=0.0)
            else:
                self.small_load(Cs, self.C_scr[h], dram_buf=self.cscr_b[h])
                if main and ti == 0:
                    P.op("dve", "tensor_scalar", [Cs.b, self.flag.b], [Cs.b], out=Cs.ap, in0=Cs.ap, scalar1=self.flag.ap[:, 0:1],
                         scalar2=None, op0=ALU.mult)
            if main:
                for c in range(2):
                    bk = self.proj(wsrc[h * 2 + c], KC, hn, ncols)
                    P.op("act", "activation", [bk.b], [qT.b], out=qT.ap[:, c, :ncols], in_=bk.ap[:, :ncols], func=AF.Copy, scale=1.0 / 16.0)
                    if has_s:
                        P.op("dve", "tensor_scalar", [bk.b], [S["q32"].b], out=S["q32"].ap[:, c, :], in0=bk.ap[:, npc:npc + 4],
                             scalar1=1.0 / 16.0, scalar2=None, op0=ALU.mult)
            for c in range(2):
                bk = self.proj(wsrc[16 + h * 2 + c], KC, hn, ncols)
                P.op("act", "activation", [bk.b], [kT.b], out=kT.ap[:, c, :ncols], in_=bk.ap[:, :ncols], func=AF.Copy)
                if has_s:
                    P.op("dve", "tensor_copy", [bk.b], [S["k32"].b], out=S["k32"].ap[:, c, :], in_=bk.ap[:, npc:npc + 4])
            for c in range(4):
                bk = self.proj(wsrc[32 + h * 4 + c], KC, hn, ncols)
                P.op("act", "activation", [bk.b], [vT.b], out=vT.ap[:, c, :ncols], in_=bk.ap[:, :ncols], func=AF.Copy)
                if has_s:
                    P.op("dve", "tensor_copy", [bk.b], [S["v32"].b], out=S["v32"].ap[:, c, :], in_=bk.ap[:, npc:npc + 4])
            if main:
                for c in range(4):
                    bk = self.proj(wsrc[64 + h * 4 + c], KC, hn, ncols)
                    P.op("act", "activation", [bk.b], [oT.b], out=oT.ap[:, c, :ncols], in_=bk.ap[:, :ncols], func=AF.Sigmoid)
                    if has_s:
                        P.op("act", "activation", [bk.b], [S["o32"].b], out=S["o32"].ap[:, c, :], in_=bk.ap[:, npc:npc + 4], func=AF.Sigmoid)
            hh = h % 4
            for blk in range(nblk):
                cs = slice(blk * BLK, (blk + 1) * BLK)
                kt, vt, ot, pT, hg = ktok[blk % 2], vtk[blk % 2], otk[blk % 2], PT[blk % 2], hsg[blk % 2]
                uS = tokm.ap[:, blk, 0, h:h + 1]
                dS = tokm.ap[:, blk, 1, h:h + 1]
                sdS = sdbc.ap[:, h, blk:blk + 1]
                b0 = ps[0]
                for c in range(2):
                    P.op("pe", "matmul", [kT.b, self.identb.b], [b0.b], out=b0.ap[:, c * 128:(c + 1) * 128], lhsT=kT.ap[:, c, cs],
                         rhs=self.identb.ap, start=True, stop=True)
                P.op("dve", "tensor_scalar", [b0.b, tokm.b], [kt.b], out=kt.ap, in0=b0.ap[:, 0:256], scalar1=uS, scalar2=None, op0=ALU.mult)
                b1 = ps[1]
                for c in range(4):
                    P.op("pe", "matmul", [vT.b, self.identb.b], [b1.b], out=b1.ap[:, c * 128:(c + 1) * 128], lhsT=vT.ap[:, c, cs],
                         rhs=self.identb.ap, start=True, stop=True)
                P.op("act", "activation", [b1.b], [vt.b], out=vt.ap, in_=b1.ap, func=AF.Copy)
                P.op("dve", "tensor_scalar", [Cs.b, sdbc.b], [Cp.b], out=Cp.ap, in0=Cs.ap, scalar1=sdS, scalar2=None, op0=ALU.mult)
                P.op("dve", "tensor_scalar", [self.nst.b, sdbc.b], [npb.b], out=npb.ap, in0=self.nst.ap[:, h, :], scalar1=sdS, scalar2=None, op0=ALU.mult)
                if main:
                    b2 = ps[4]
                    for c in range(4):
                        P.op("pe", "matmul", [oT.b, self.identb.b], [b2.b], out=b2.ap[:, c * 128:(c + 1) * 128], lhsT=oT.ap[:, c, cs],
                             rhs=self.identb.ap, start=True, stop=True)
                    P.op("act", "activation", [b2.b], [ot.b], out=ot.ap, in_=b2.ap, func=AF.Copy)
                    b3 = ps[3]
                    P.mm(b3.ap[:, 0:128], [(kT.ap[:, c, cs], qT.ap[:, c, cs]) for c in range(2)], [kT.b, qT.b], [b3.b])
                    P.op("dve", "scalar_tensor_tensor", [b3.b, tokm.b, self.cmask.b], [pT.b], out=pT.ap, in0=b3.ap[:, 0:128], scalar=uS,
                         in1=self.cmask.ap, op0=ALU.mult, op1=ALU.mult)
                    acc = ps[5]
                    P.mm(acc.ap, [(pT.ap, vt.ap), (qT.ap[:, 0, cs], Cp.ap[:, 0, :]), (qT.ap[:, 1, cs], Cp.ap[:, 1, :])],
                         [pT.b, vt.b, qT.b, Cp.b], [acc.b])
                    dac = ps[3]
                    P.mm(dac.ap[:, 256:257], [(pT.ap, self.onesb.ap[:, 0:1]), (qT.ap[:, 0, cs], npb.ap[:, 0:1]), (qT.ap[:, 1, cs], npb.ap[:, 1:2])],
                         [pT.b, self.onesb.b, qT.b, npb.b], [dac.b])
                    P.op("act", "activation", [acc.b], [junk.b, col.b], out=junk.ap, in_=acc.ap, func=AF.Square, accum_out=col.ap[:, 0:1])
                    P.op("dve", "tensor_scalar", [dac.b], [col.b], out=col.ap[:, 6:7], in0=dac.ap[:, 256:257], scalar1=-1.0, scalar2=None,
                         op0=ALU.mult)
                    P.op("dve", "tensor_tensor", [dac.b, col.b], [col.b], out=col.ap[:, 7:8], in0=dac.ap[:, 256:257], in1=col.ap[:, 6:7], op=ALU.max)
                    P.op("dve", "tensor_tensor", [col.b, tokm.b], [col.b], out=col.ap[:, 1:2], in0=col.ap[:, 7:8], in1=dS, op=ALU.max)
                    P.op("dve", "tensor_scalar", [col.b], [col.b], out=col.ap[:, 2:3], in0=col.ap[:, 1:2], scalar1=col.ap[:, 1:2], scalar2=EPS,
                         op0=ALU.mult, op1=ALU.mult)
                    P.op("dve", "scalar_tensor_tensor", [col.b], [col.b], out=col.ap[:, 3:4], in0=col.ap[:, 0:1], scalar=1.0 / 512.0,
                         in1=col.ap[:, 2:3], op0=ALU.mult, op1=ALU.add)
                    P.op("act", "activation", [col.b], [col.b], out=col.ap[:, 4:5], in_=col.ap[:, 3:4], func=AF.Sqrt)
                    P.op("dve", "reciprocal", [col.b], [col.b], out=col.ap[:, 5:6], in_=col.ap[:, 4:5])
                    P.op("dve", "scalar_tensor_tensor", [acc.b, col.b, ot.b], [hg.b], out=hg.ap, in0=acc.ap, scalar=col.ap[:, 5:6], in1=ot.ap,
                         op0=ALU.mult, op1=ALU.mult)
                    b6 = ps[6]
                    for c in range(4):
                        P.op("pe", "matmul", [hg.b, self.identb.b], [b6.b], out=b6.ap[:, c * 128:(c + 1) * 128], lhsT=hg.ap[:, c * 128:(c + 1) * 128],
                             rhs=self.identb.ap, start=True, stop=True)
                    for c in range(4):
                        P.op("act", "activation", [b6.b, self.gheadT.b], [hsT.b], out=hsT.ap[:, hh * 4 + c, cs], in_=b6.ap[:, c * 128:(c + 1) * 128],
                             func=AF.Copy, scale=self.gheadT.ap[:, h * 4 + c:h * 4 + c + 1])
                for c in range(2):
                    dC = ps[c]
                    P.mm(dC.ap, [(kt.ap[:, c * 128:(c + 1) * 128], vt.ap)], [kt.b, vt.b], [dC.b])
                    P.op("dve", "scalar_tensor_tensor", [Cs.b, sdbc.b, dC.b], [Cs.b], out=Cs.ap[:, c, :], in0=Cs.ap[:, c, :], scalar=sdS,
                         in1=dC.ap, op0=ALU.mult, op1=ALU.add)
                dn_ = ps[3]
                for c in range(2):
                    P.mm(dn_.ap[:, 260 + c:261 + c], [(kt.ap[:, c * 128:(c + 1) * 128], self.onesb.ap[:, 0:1])], [kt.b, self.onesb.b], [dn_.b])
                P.op("dve", "scalar_tensor_tensor", [self.nst.b, sdbc.b, dn_.b], [self.nst.b], out=self.nst.ap[:, h, :], in0=self.nst.ap[:, h, :],
                     scalar=sdS, in1=dn_.ap[:, 260:262], op0=ALU.mult, op1=ALU.add)
            self.store(self.C_scr[h], Cs, dram_buf=self.cscr_b[h])
            if main and ti == 2:
                self.store(self.dout["C_out"][h], Cs)
            if main and has_s:
                self.sample_head(h, S, hsT, hh, npc, Cst)
            if main and hh == 3:
                g = h // 4
                for c in range(KC):
                    bk = self.proj(d["w_mout_s"][g, c], 16, hsT, ncols)
                    self.xupdate(bk, c, 2, ncols, npc, has_s)
        self.first_state = False
        if main and ti == 2:
            self.store(self.dout["n_out"], self.nst)
            self.store(self.dout["m_out"], self.mst)
        if main and has_s:
            self.store(self.dout["ns_out"], S["nsn"])
            self.store(self.dout["ms_out"], S["mt"])


    def bcast8(self, tab, out_bc, bank):
        P = self.P
        R = self.carve([8, 8, 4])
        for h in range(H):
            P.op("dve", "tensor_tensor", [tab.b, self.bm.b], [R.b], out=R.ap[:, h, :], in0=tab.ap, in1=self.bm.ap[:, h, :], op=ALU.mult)
        P.op("pe", "matmul", [R.b, self.ones.b], [bank.b], out=bank.ap[:, 0:32], lhsT=self.ones.ap[0:8, :],
             rhs=R.ap.rearrange("p a b -> p (a b)"), start=True, stop=True)
        P.op("dve", "tensor_copy", [bank.b], [out_bc.b], out=out_bc.ap.rearrange("p a b -> p (a b)"), in_=bank.ap[:, 0:32])

    def sample_gates(self, ig, lf, npc):
        P, C, d = self.P, self.carve, self.din
        S = {"q32": C([128, 2, 4]), "k32": C([128, 2, 4]), "v32": C([128, 4, 4]), "o32": C([128, 4, 4]),
             "ws": C([128, 8, 4]), "si": C([128, 8, 4]), "dnf": C([128, 8, 4]), "mt": C([8, 4]),
             "ns": C([128, H, 2, 4]), "nsn": C([128, H, 2, 4])}
        ms = C([8, 4]); t1 = C([8, 4]); d1 = C([8, 4]); d2 = C([8, 4]); nm = C([8, 4])
        self.small_load(ms, d["st_m"]); self.small_load(S["ns"], d["st_n"])
        igs, lfs, mt = ig.ap[:, npc:npc + 4], lf.ap[:, npc:npc + 4], S["mt"]
        P.op("dve", "tensor_tensor", [lf.b, ms.b], [t1.b], out=t1.ap, in0=lfs, in1=ms.ap, op=ALU.add)
        P.op("dve", "tensor_tensor", [t1.b, ig.b], [mt.b], out=mt.ap, in0=t1.ap, in1=igs, op=ALU.max)
        P.op("dve", "tensor_tensor", [ig.b, mt.b], [d1.b], out=d1.ap, in0=igs, in1=mt.ap, op=ALU.subtract)
        P.op("dve", "tensor_tensor", [t1.b, mt.b], [d2.b], out=d2.ap, in0=t1.ap, in1=mt.ap, op=ALU.subtract)
        P.op("dve", "tensor_scalar", [mt.b], [nm.b], out=nm.ap, in0=mt.ap, scalar1=-1.0, scalar2=None, op0=ALU.mult)
        for t in (d1, d2, nm):
            P.op("act", "activation", [t.b], [t.b], out=t.ap, in_=t.ap, func=AF.Exp)
        self.bcast8(d1, S["ws"], self.ps[0]); self.bcast8(d2, S["si"], self.ps[1]); self.bcast8(nm, S["dnf"], self.ps[0])
        S["prod"] = C([128, 2, 8]); S["sums"] = C([128, 32]); S["c4"] = C([128, 16, 4]); S["hT"] = C([128, 4, 4])
        S["Dg"] = C([128, 4, 128]); S["wk"] = C([128, 2]); S["t44"] = C([128, 4, 4])
        return S

    def sample_head(self, h, S, hsT, hh, npc, Cst):
        P, d, ps = self.P, self.din, self.ps
        q32, k32, v32, o32, c4 = S["q32"], S["k32"], S["v32"], S["o32"], S["c4"]
        wsb, sib, dnb = S["ws"].ap[:, h, :], S["si"].ap[:, h, :], S["dnf"].ap[:, h, :]
        prod, sums = S["prod"], S["sums"]
        P.op("dve", "tensor_tensor", [q32.b, k32.b], [prod.b], out=prod.ap[:, :, 0:4], in0=q32.ap, in1=k32.ap, op=ALU.mult)
        P.op("dve", "tensor_tensor", [q32.b, S["ns"].b], [prod.b], out=prod.ap[:, :, 4:8], in0=q32.ap, in1=S["ns"].ap[:, h], op=ALU.mult)
        pA = ps[3]
        P.op("pe", "matmul", [prod.b, self.ones.b], [pA.b], out=pA.ap[:, 0:16], lhsT=self.ones.ap, rhs=prod.ap.rearrange("p a b -> p (a b)"),
             start=True, stop=True)
        P.op("dve", "tensor_copy", [pA.b], [sums.b], out=sums.ap[:, 0:16], in_=pA.ap[:, 0:16])
        qkn = c4.ap[:, 0:2].rearrange("p a b -> p (a b)")
        P.op("dve", "tensor_tensor", [sums.b], [c4.b], out=qkn, in0=sums.ap[:, 0:8], in1=sums.ap[:, 8:16], op=ALU.add)
        a1, t, den, dn, rdn, a1n, sin_ = (c4.ap[:, i, :] for i in range(2, 9))
        P.op("dve", "tensor_tensor", [S["ws"].b, c4.b], [c4.b], out=a1, in0=wsb, in1=c4.ap[:, 0, :], op=ALU.mult)
        P.op("dve", "tensor_tensor", [S["si"].b, c4.b], [c4.b], out=t, in0=sib, in1=c4.ap[:, 1, :], op=ALU.mult)
        P.op("dve", "tensor_tensor", [c4.b], [c4.b], out=den, in0=t, in1=a1, op=ALU.add)
        P.op("dve", "tensor_scalar", [c4.b], [c4.b], out=dn, in0=den, scalar1=-1.0, scalar2=None, op0=ALU.mult)
        P.op("dve", "tensor_tensor", [c4.b], [c4.b], out=dn, in0=dn, in1=den, op=ALU.max)
        P.op("dve", "tensor_tensor", [c4.b, S["dnf"].b], [c4.b], out=dn, in0=dn, in1=dnb, op=ALU.max)
        P.op("dve", "reciprocal", [c4.b], [c4.b], out=rdn, in_=dn)
        P.op("dve", "tensor_tensor", [c4.b], [c4.b], out=a1n, in0=a1, in1=rdn, op=ALU.mult)
        P.op("dve", "tensor_tensor", [c4.b, S["si"].b], [c4.b], out=sin_, in0=sib, in1=rdn, op=ALU.mult)
        pB = ps[5]
        for j in range(NSMP):
            Cs = Cst[j % 2]
            self.small_load(Cs, d["st_C"][j, h])
            for vc in range(4):
                P.mm(pB.ap[:, vc * 4 + j:vc * 4 + j + 1], [(Cs.ap[:, c, vc * 128:(vc + 1) * 128], q32.ap[:, c, j:j + 1]) for c in range(2)],
                     [Cs.b, q32.b], [pB.b])
            pV = ps[4]
            for vc in range(4):
                P.op("dve", "tensor_scalar", [self.ident.b, v32.b], [S["Dg"].b], out=S["Dg"].ap[:, vc, :], in0=self.ident.ap,
                     scalar1=v32.ap[:, vc, j:j + 1], scalar2=None, op0=ALU.mult)
            for vc in range(4):
                P.op("pe", "matmul", [S["Dg"].b, self.ones.b], [pV.b], out=pV.ap[:, vc * 128:(vc + 1) * 128], lhsT=self.ones.ap,
                     rhs=S["Dg"].ap[:, vc, :], start=True, stop=True)
            P.op("dve", "tensor_scalar", [k32.b, S["ws"].b], [S["wk"].b], out=S["wk"].ap, in0=k32.ap[:, :, j], scalar1=wsb[:, j:j + 1],
                 scalar2=None, op0=ALU.mult)
            for c in range(2):
                P.op("dve", "tensor_scalar", [Cs.b, S["si"].b], [Cs.b], out=Cs.ap[:, c, :], in0=Cs.ap[:, c, :], scalar1=sib[:, j:j + 1],
                     scalar2=None, op0=ALU.mult)
                P.op("dve", "scalar_tensor_tensor", [pV.b, S["wk"].b, Cs.b], [Cs.b], out=Cs.ap[:, c, :], in0=pV.ap, scalar=S["wk"].ap[:, c:c + 1],
                     in1=Cs.ap[:, c, :], op0=ALU.mult, op1=ALU.add)
            P.op("dve", "scalar_tensor_tensor", [S["ns"].b, S["si"].b, S["wk"].b], [S["nsn"].b], out=S["nsn"].ap[:, h, :, j],
                 in0=S["ns"].ap[:, h, :, j], scalar=sib[:, j:j + 1], in1=S["wk"].ap, op0=ALU.mult, op1=ALU.add)
            self.store(self.dout["Cs_out"][j, h], Cs)
        hT, t44 = S["hT"], S["t44"]
        for vc in range(4):
            P.op("dve", "tensor_tensor", [v32.b, c4.b], [t44.b], out=t44.ap[:, vc, :], in0=v32.ap[:, vc, :], in1=a1n, op=ALU.mult)
            P.op("dve", "tensor_tensor", [pB.b, c4.b], [hT.b], out=hT.ap[:, vc, :], in0=pB.ap[:, vc * 4:vc * 4 + 4], in1=sin_, op=ALU.mult)
        P.op("dve", "tensor_tensor", [hT.b, t44.b], [hT.b], out=hT.ap, in0=hT.ap, in1=t44.ap, op=ALU.add)
        P.op("dve", "tensor_tensor", [hT.b], [t44.b], out=t44.ap, in0=hT.ap, in1=hT.ap, op=ALU.mult)
        P.op("pe", "matmul", [t44.b, self.ones.b], [pA.b], out=pA.ap[:, 16:32], lhsT=self.ones.ap, rhs=t44.ap.rearrange("p a b -> p (a b)"),
             start=True, stop=True)
        P.op("dve", "tensor_copy", [pA.b], [sums.b], out=sums.ap[:, 16:32], in_=pA.ap[:, 16:32])
        ssq, rs = c4.ap[:, 9, :], c4.ap[:, 10, :]
        P.op("dve", "tensor_tensor", [sums.b], [c4.b], out=ssq, in0=sums.ap[:, 16:20], in1=sums.ap[:, 20:24], op=ALU.add)
        P.op("dve", "tensor_tensor", [sums.b, c4.b], [c4.b], out=ssq, in0=ssq, in1=sums.ap[:, 24:28], op=ALU.add)
        P.op("dve", "tensor_tensor", [sums.b, c4.b], [c4.b], out=ssq, in0=ssq, in1=sums.ap[:, 28:32], op=ALU.add)
        P.op("act", "activation", [c4.b, self.epsT.b], [c4.b], out=rs, in_=ssq, func=AF.Sqrt, scale=1.0 / 512.0, bias=self.epsT.ap[:, 0:1])
        P.op("dve", "reciprocal", [c4.b], [c4.b], out=rs, in_=rs)
        for vc in range(4):
            P.op("dve", "tensor_tensor", [hT.b, c4.b], [t44.b], out=t44.ap[:, vc, :], in0=hT.ap[:, vc, :], in1=rs, op=ALU.mult)
            P.op("dve", "tensor_tensor", [t44.b, o32.b], [t44.b], out=t44.ap[:, vc, :], in0=t44.ap[:, vc, :], in1=o32.ap[:, vc, :], op=ALU.mult)
            P.op("dve", "tensor_scalar", [t44.b, self.gheadT.b], [hsT.b], out=hsT.ap[:, hh * 4 + vc, npc:npc + 4], in0=t44.ap[:, vc, :],
                 scalar1=self.gheadT.ap[:, h * 4 + vc:h * 4 + vc + 1], scalar2=None, op0=ALU.mult)

    def ffn(self, l, ti, ncols, npc, has_s):
        P, d, C, hn = self.P, self.din, self.carve, self.hn
        act = C([128, 22, NCMAX], BF16)
        ue = [[C([128, NCMAX + 2]) for _ in range(2)] for _ in range(2)]
        y = [[C([128, NCMAX]) for _ in range(2)] for _ in range(2)]
        sg = [C([128, NCMAX]) for _ in range(2)]
        wcv, bcv, uh = self.wconvT, self.bconvT, self.uhist
        if has_s:
            cc = C([128, NSL, 4, 2]); unew = C([128, NSL, 4])
            self.small_load(cc, d["cconv"][l])
        for g, (f0, nf) in enumerate(FGRP):
            for jj in range(nf):
                j = f0 + jj
                for which in range(2):
                    sl = 2 * j + which
                    bank = self.proj(d["w_ffi_s"][l, sl], KC, hn, ncols)
                    u = ue[which][j % 2]
                    yy = y[which][j % 2]
                    w0, w1, w2 = (wcv.ap[:, l, sl, i:i + 1] for i in range(3))
                    P.op("dve", "tensor_copy", [uh.b], [u.b], out=u.ap[:, 0:2], in_=uh.ap[:, l, sl, :])
                    P.op("act", "activation", [bank.b], [u.b], out=u.ap[:, 2:2 + ncols], in_=bank.ap[:, :ncols], func=AF.Copy)
                    P.op("dve", "tensor_copy", [u.b], [uh.b], out=uh.ap[:, l, sl, :], in_=u.ap[:, npc:npc + 2])
                    P.op("dve", "tensor_scalar", [u.b, wcv.b, bcv.b], [yy.b], out=yy.ap[:, :npc], in0=u.ap[:, 0:npc], scalar1=w0,
                         scalar2=bcv.ap[:, l, sl:sl + 1], op0=ALU.mult, op1=ALU.add)
                    P.op("dve", "scalar_tensor_tensor", [u.b, wcv.b, yy.b], [yy.b], out=yy.ap[:, :npc], in0=u.ap[:, 1:npc + 1], scalar=w1,
                         in1=yy.ap[:, :npc], op0=ALU.mult, op1=ALU.add)
                    P.op("dve", "scalar_tensor_tensor", [u.b, wcv.b, yy.b], [yy.b], out=yy.ap[:, :npc], in0=u.ap[:, 2:npc + 2], scalar=w2,
                         in1=yy.ap[:, :npc], op0=ALU.mult, op1=ALU.add)
                    if has_s:
                        ys = yy.ap[:, npc:npc + 4]
                        P.op("dve", "tensor_scalar", [cc.b, wcv.b, bcv.b], [yy.b], out=ys, in0=cc.ap[:, sl, :, 0], scalar1=w0,
                             scalar2=bcv.ap[:, l, sl:sl + 1], op0=ALU.mult, op1=ALU.add)
                        P.op("dve", "scalar_tensor_tensor", [cc.b, wcv.b, yy.b], [yy.b], out=ys, in0=cc.ap[:, sl, :, 1], scalar=w1,
                             in1=ys, op0=ALU.mult, op1=ALU.add)
                        P.op("dve", "scalar_tensor_tensor", [u.b, wcv.b, yy.b], [yy.b], out=ys, in0=u.ap[:, 2 + npc:6 + npc], scalar=w2,
                             in1=ys, op0=ALU.mult, op1=ALU.add)
                        P.op("dve", "tensor_copy", [u.b], [unew.b], out=unew.ap[:, sl, :], in_=u.ap[:, 2 + npc:6 + npc])
                s_ = sg[j % 2]
                P.op("act", "activation", [y[0][j % 2].b], [s_.b], out=s_.ap[:, :ncols], in_=y[0][j % 2].ap[:, :ncols], func=AF.Silu)
                P.op("dve", "tensor_tensor", [s_.b, y[1][j % 2].b], [act.b], out=act.ap[:, jj, :ncols], in0=s_.ap[:, :ncols],
                     in1=y[1][j % 2].ap[:, :ncols], op=ALU.mult)
            for c in range(KC):
                bank = self.proj(d["w_ffo_s"][l, g, c][:, 0:nf, :], nf, act, ncols)
                self.xupdate(bank, c, 5 + 6 * l, ncols, npc, has_s)
        if has_s:
            self.store(self.dout["convs_new"][l], unew)

    def rope(self, kq, ncols, out_t, out_ap):
        P = self.P
        pr = self.ps[6]
        P.op("pe", "matmul", [kq.b, self.Rmat.b], [pr.b], out=pr.ap[:, :ncols], lhsT=self.Rmat.ap, rhs=kq.ap[:, :ncols], start=True, stop=True)
        t1, t2 = self.tmp
        P.op("dve", "tensor_tensor", [kq.b, self.ropeT.b], [t1.b], out=t1.ap[:, :ncols], in0=kq.ap[:, :ncols], in1=self.ropeT.ap[:, 0, :ncols], op=ALU.mult)
        P.op("dve", "tensor_tensor", [pr.b, self.ropeT.b], [t2.b], out=t2.ap[:, :ncols], in0=pr.ap[:, :ncols], in1=self.ropeT.ap[:, 1, :ncols], op=ALU.mult)
        P.op("dve", "tensor_tensor", [t1.b, t2.b], [out_t.b], out=out_ap, in0=t1.ap[:, :ncols], in1=t2.ap[:, :ncols], op=ALU.add)

    def attention(self, ti, nblk, ncols, npc, has_s):
        P, d, C, ps, hn = self.P, self.din, self.carve, self.ps, self.hn
        kTd, vtok = self.kTd, self.vtok
        kq32 = [C([128, NCMAX]) for _ in range(2)]
        kr32 = [C([128, NCMAX]) for _ in range(2)]
        qr = [C([128, NCMAX], BF16) for _ in range(2)]
        v32 = C([128, 4, NCMAX])
        oT = C([128, KC, NCMAX], BF16)
        Sm = [C([128, 256]) for _ in range(2)]
        Pf = [C([128, 256]) for _ in range(2)]
        Pn = [C([128, 256], BF16) for _ in range(2)]
        PTs = [C([128, 256], BF16) for _ in range(2)]
        colA = [C([128, 8]) for _ in range(2)]
        self.small_load(self.ropeT, d["rope"][ti])
        self.norm(2, ncols, npc, has_s)
        for g in range(8):
            bank = self.proj(d["w_kv_s"][g], KC, hn, ncols)
            kq, kr = kq32[g % 2], kr32[g % 2]
            P.op("dve", "tensor_scalar", [bank.b, self.bkv.b], [kq.b], out=kq.ap[:, :ncols], in0=bank.ap[:, :ncols], scalar1=self.bkv.ap[:, g:g + 1],
                 scalar2=None, op0=ALU.add)
            self.rope(kq, ncols, kr, kr.ap[:, :ncols])
            P.op("act", "activation", [kr.b], [kTd.b], out=kTd.ap[:, g, 128:128 + ncols], in_=kr.ap[:, :ncols], func=AF.Copy)
            if ti == 2:
                self.store(self.dout["kwinp"][:, g, :], kr, kr.ap[0:64, npc - 128:npc])
            if has_s:
                self.store(self.dout["knew"][:, g, :], kr, kr.ap[0:64, npc:npc + 4])
        for c in range(4):
            bank = self.proj(d["w_kv_s"][8 + c], KC, hn, ncols)
            P.op("dve", "tensor_scalar", [bank.b, self.bkv.b], [v32.b], out=v32.ap[:, c, :ncols], in0=bank.ap[:, :ncols],
                 scalar1=self.bkv.ap[:, 8 + c:9 + c], scalar2=None, op0=ALU.add)
        if ti == 2:
            self.store(self.dout["vwinp"], v32, v32.ap[:, :, npc - 128:npc])
        for blk in range(nblk):
            cs = slice(blk * BLK, (blk + 1) * BLK)
            pv = ps[0 + (blk % 2)]
            for c in range(4):
                P.op("pe", "matmul", [v32.b, self.ident.b], [pv.b], out=pv.ap[:, c * 128:(c + 1) * 128], lhsT=v32.ap[:, c, cs], rhs=self.ident.ap,
                     start=True, stop=True)
            P.op("act", "activation", [pv.b], [vtok.b], out=vtok.ap[:, blk + 1, :], in_=pv.ap, func=AF.Copy)
        if has_s:
            vts = C([4, 512])
            pvs = ps[3]
            for c in range(4):
                P.op("pe", "matmul", [v32.b, self.ident.b], [pvs.b], out=pvs.ap[0:4, c * 128:(c + 1) * 128], lhsT=v32.ap[:, c, npc:npc + 4],
                     rhs=self.ident.ap, start=True, stop=True)
            P.op("dve", "tensor_copy", [pvs.b], [vts.b], out=vts.ap, in_=pvs.ap[0:4, :])
            self.vnew_b = Buf()
            self.store(self.dout["vnew"], vts, dram_buf=self.vnew_b)
            qTs = C([128, KC, 4], BF16)
        astop = self.dbg.get("attn_stop", 99)
        if astop <= 1:
            return
        self.norm(3, ncols, npc, has_s)
        def q_start(c):
            return {"c": c, "slab": self.wslab(d["w_q_s"][c], KC), "bank": self.pbank()}

        def q_part(st, k):
            lo, hi = (0, 11, 22, 32)[k], (0, 11, 22, 32)[k + 1]
            slab, bank = st["slab"], st["bank"]
            P.mm(bank.ap[:, :ncols], [(slab.ap[:, kc, :], hn.ap[:, kc, :ncols]) for kc in range(lo, hi)], [slab.b, hn.b], [bank.b],
                 start=(lo == 0), stop=(hi == KC))

        def q_finish(st):
            c, bank = st["c"], st["bank"]
            kq, q = kq32[c % 2], qr[c % 2]
            P.op("dve", "tensor_scalar", [bank.b, self.bqT.b], [kq.b], out=kq.ap[:, :ncols], in0=bank.ap[:, :ncols], scalar1=self.bqT.ap[:, c:c + 1],
                 scalar2=None, op0=ALU.add)
            self.rope(kq, ncols, q, q.ap[:, :ncols])
            if has_s:
                P.op("dve", "tensor_copy", [q.b], [qTs.b], out=qTs.ap[:, c, :], in_=q.ap[:, npc:npc + 4])

        st0 = q_start(0)
        for k in range(3):
            q_part(st0, k)
        q_finish(st0)
        H2 = range(2)
        for c in range(KC):
            q = qr[c % 2]
            nxt = q_start(c + 1) if c + 1 < KC else None
            g = c // 4
            for blk in range(nblk):
                cs = slice(blk * BLK, (blk + 1) * BLK)
                mk = self.amask.ap[:, 1 if (ti == 0 and blk == 0) else 0, :]
                pO = ps[5]
                rows = [slice(half * 64, half * 64 + 64) for half in H2]
                for half in H2:
                    pS = ps[half]
                    P.mm(pS.ap[:, 0:256], [(q.ap[rows[half], cs], kTd.ap[rows[half], g, blk * BLK:blk * BLK + 256])], [q.b, kTd.b], [pS.b])
                if nxt is not None and blk < 3:
                    q_part(nxt, blk)
                for half in H2:
                    P.op("dve", "tensor_tensor", [ps[half].b, self.amask.b], [Sm[half].b], out=Sm[half].ap, in0=ps[half].ap[:, 0:256], in1=mk, op=ALU.add)
                for half in H2:
                    P.op("dve", "tensor_reduce", [Sm[half].b], [colA[half].b], out=colA[half].ap[:, 0:1], in_=Sm[half].ap, axis=AX.X, op=ALU.max)
                for half in H2:
                    h = 2 * c + half
                    P.op("dve", "tensor_scalar", [colA[half].b, self.nsinks.b], [colA[half].b], out=colA[half].ap[:, 1:2], in0=colA[half].ap[:, 0:1],
                         scalar1=-0.125, scalar2=self.nsinks.ap[:, h:h + 1], op0=ALU.mult, op1=ALU.min)
                for half in H2:
                    P.op("act", "activation", [Sm[half].b, colA[half].b], [Pf[half].b, colA[half].b], out=Pf[half].ap, in_=Sm[half].ap, func=AF.Exp,
                         scale=0.125, bias=colA[half].ap[:, 1:2], accum_out=colA[half].ap[:, 2:3])
                for half in H2:
                    h = 2 * c + half
                    P.op("act", "activation", [colA[half].b, self.sinks.b], [colA[half].b], out=colA[half].ap[:, 3:4], in_=colA[half].ap[:, 1:2],
                         func=AF.Exp, bias=self.sinks.ap[:, h:h + 1])
                for half in H2:
                    cA = colA[half]
                    P.op("dve", "tensor_tensor", [cA.b], [cA.b], out=cA.ap[:, 4:5], in0=cA.ap[:, 2:3], in1=cA.ap[:, 3:4], op=ALU.add)
                    P.op("dve", "reciprocal", [cA.b], [cA.b], out=cA.ap[:, 5:6], in_=cA.ap[:, 4:5])
                    P.op("dve", "tensor_scalar", [Pf[half].b, cA.b], [Pn[half].b], out=Pn[half].ap, in0=Pf[half].ap, scalar1=cA.ap[:, 5:6], scalar2=None,
                         op0=ALU.mult)
                for half in H2:
                    pT = ps[3 + half]
                    for j in range(2):
                        P.op("pe", "matmul", [Pn[half].b, self.identb.b], [pT.b], out=pT.ap[:, j * 128:(j + 1) * 128],
                             lhsT=Pn[half].ap[:, j * 128:(j + 1) * 128], rhs=self.identb.ap, start=True, stop=True)
                for half in H2:
                    P.op("act", "activation", [ps[3 + half].b], [PTs[half].b], out=PTs[half].ap, in_=ps[3 + half].ap[:, 0:256], func=AF.Copy)
                for half in H2:
                    pts = PTs[half]
                    P.mm(pO.ap[rows[half], 0:128], [(vtok.ap[:, blk, g * 64:(g + 1) * 64], pts.ap[:, 0:128]),
                                                    (vtok.ap[:, blk + 1, g * 64:(g + 1) * 64], pts.ap[:, 128:256])], [vtok.b, pts.b], [pO.b])
                P.op("act", "activation", [pO.b], [oT.b], out=oT.ap[:, c, cs], in_=pO.ap[:, 0:128], func=AF.Copy)
            if nxt is not None:
                q_finish(nxt)
        if astop <= 2:
            return
        if has_s:
            Ks = [C([128, 8, 128], BF16) for _ in range(2)]
            Vs = [C([128, 512], BF16) for _ in range(2)]
            STs = C([128, 64]); PnS = C([64, 128], BF16); PfS = C([64, 128]); PTS = C([128, 64], BF16); cS = C([64, 8])
            pO2 = ps[6]
            opad = C([64, 128])
            sa_stop = self.dbg.get("sa_stop", 99)
            for j in range(NSMP):
                K_, V_ = Ks[j % 2], Vs[j % 2]
                P.dma("pool", "d2", [], [K_.b], out=K_.ap[:, :, 0:127], in_=d["ckT"][j])
                P.op("dve", "tensor_copy", [kTd.b], [K_.b], out=K_.ap[:, :, 127], in_=kTd.ap[:, :, 128 + npc + j])
                P.dma("pool", "d3", [], [V_.b], out=V_.ap[0:127, :], in_=d["cv_old"][j])
                P.dma("pool", "d4", [self.vnew_b], [V_.b], out=V_.ap[127:128, :], in_=self.dout["vnew"][j:j + 1, :])
                V_.b.w["d3"] = P.cnt["d3"]
                if sa_stop <= 1:
                    continue
                for g in range(8):
                    for half in range(2):
                        rows = slice(half * 64, half * 64 + 64)
                        pSh = ps[half]
                        P.mm(pSh.ap[:, 4 * g:4 * g + 4], [(K_.ap[rows, g, :], qTs.ap[rows, 4 * g:4 * g + 4, j])], [K_.b, qTs.b], [pSh.b])
                ST3 = STs.ap.rearrange("p (g e) -> p g e", e=8)
                for half in range(2):
                    P.op("dve", "tensor_copy", [ps[half].b], [STs.b], out=ST3[:, :, 4 * half:4 * half + 4],
                         in_=ps[half].ap[:, 0:32].rearrange("p (g k) -> p g k", k=4))
                if sa_stop <= 2:
                    continue
                pS2 = ps[3]
                P.op("pe", "matmul", [STs.b, self.ident.b], [pS2.b], out=pS2.ap[0:64, 0:128], lhsT=STs.ap, rhs=self.ident.ap, start=True, stop=True)
                P.op("dve", "tensor_reduce", [pS2.b], [cS.b], out=cS.ap[:, 0:1], in_=pS2.ap[0:64, 0:128], axis=AX.X, op=ALU.max)
                P.op("dve", "tensor_scalar", [cS.b, self.sinksT.b], [cS.b], out=cS.ap[:, 1:2], in0=cS.ap[:, 0:1], scalar1=-0.125,
                     scalar2=self.sinksT.ap[:, 1:2], op0=ALU.mult, op1=ALU.min)
                P.op("act", "activation", [pS2.b, cS.b], [PfS.b, cS.b], out=PfS.ap, in_=pS2.ap[0:64, 0:128], func=AF.Exp, scale=0.125,
                     bias=cS.ap[:, 1:2], accum_out=cS.ap[:, 2:3])
                P.op("act", "activation", [cS.b, self.sinksT.b], [cS.b], out=cS.ap[:, 3:4], in_=cS.ap[:, 1:2], func=AF.Exp, bias=self.sinksT.ap[:, 0:1])
                P.op("dve", "tensor_tensor", [cS.b], [cS.b], out=cS.ap[:, 4:5], in0=cS.ap[:, 2:3], in1=cS.ap[:, 3:4], op=ALU.add)
                P.op("dve", "reciprocal", [cS.b], [cS.b], out=cS.ap[:, 5:6], in_=cS.ap[:, 4:5])
                P.op("dve", "tensor_scalar", [PfS.b, cS.b], [PnS.b], out=PnS.ap, in0=PfS.ap, scalar1=cS.ap[:, 5:6], scalar2=None, op0=ALU.mult)
                pP = ps[4]
                P.op("pe", "matmul", [PnS.b, self.identb.b], [pP.b], out=pP.ap[:, 0:64], lhsT=PnS.ap, rhs=self.identb.ap[0:64, 0:64], start=True, stop=True)
                P.op("act", "activation", [pP.b], [PTS.b], out=PTS.ap, in_=pP.ap[:, 0:64], func=AF.Copy)
                if sa_stop <= 3:
                    continue
                pO1 = ps[5]
                P.mm(pO1.ap[0:64, 0:512], [(PTS.ap, V_.ap)], [PTS.b, V_.b], [pO1.b])
                for hf in range(2):
                    osl = opad.ap[:, hf * 64:(hf + 1) * 64]
                    P.op("dve", "tensor_scalar", [pO1.b, self.mg.b], [opad.b], out=osl, in0=pO1.ap[0:64, 0:64], scalar1=self.mg.ap[:, hf * 8:hf * 8 + 1],
                         scalar2=None, op0=ALU.mult)
                    for g in range(1, 8):
                        P.op("dve", "scalar_tensor_tensor", [pO1.b, self.mg.b, opad.b], [opad.b], out=osl, in0=pO1.ap[0:64, g * 64:(g + 1) * 64],
                             scalar=self.mg.ap[:, hf * 8 + g:hf * 8 + g + 1], in1=osl, op0=ALU.mult, op1=ALU.add)
                if sa_stop <= 4:
                    continue
                P.op("pe", "matmul", [opad.b, self.ident.b], [pO2.b], out=pO2.ap[:, j * 64:(j + 1) * 64], lhsT=opad.ap, rhs=self.ident.ap[0:64, 0:64],
                     start=True, stop=True)
            if sa_stop > 5:
                tmpO = C([128, 256])
                P.op("act", "activation", [pO2.b], [tmpO.b], out=tmpO.ap, in_=pO2.ap[:, 0:256], func=AF.Copy)
                for j in range(NSMP):
                    for half in range(2):
                        rows = slice(half * 64, half * 64 + 64)
                        P.op("dve", "tensor_copy", [tmpO.b], [oT.b], out=oT.ap[rows, :, npc + j].rearrange("p (g k) -> p g k", k=4),
                             in_=tmpO.ap[rows, j * 64:(j + 1) * 64].rearrange("p (g e) -> p g e", e=8)[:, :, 4 * half:4 * half + 4])
        if astop <= 3:
            return
        for c in range(KC):
            bank = self.proj(d["w_o_s"][c], KC, oT, ncols)
            self.xupdate(bank, c, 8, ncols, npc, has_s, bias=self.boT)
        P.op("dve", "tensor_copy", [kTd.b], [kTd.b], out=kTd.ap[:, :, 0:128], in_=kTd.ap[:, :, npc:npc + 128])
        P.op("dve", "tensor_copy", [vtok.b], [vtok.b], out=vtok.ap[:, 0, :], in_=vtok.ap[:, nblk, :])


NORMS = [(0, 1, 0), (1, 4, 3), (2, 13, 12), (3, 7, 6), (4, 10, 9)]


def _slab(W):
    K, N = W.shape
    return np.ascontiguousarray(W.reshape(K // 128, 128, N // 128, 128).transpose(2, 1, 0, 3))


def _fm(x):
    T_ = x.shape[0]
    return np.ascontiguousarray(x.T.reshape(KC, 128, T_).transpose(1, 0, 2))


def _vecT(v):
    lead = v.shape[:-1]
    a = v.reshape(lead + (KC, 128))
    return np.ascontiguousarray(np.moveaxis(a, -1, 0))


_SL_COL = np.empty((NSL, 128), np.int64)
for _sl in range(NSL):
    _SL_COL[_sl] = (_sl % 2) * FF + (_sl // 2) * 128 + np.arange(128)


def _rope_tables(pos):
    half = 32
    freq = (np.float32(10000.0) ** (-np.arange(half, dtype=np.float32) / np.float32(half))).astype(np.float32)
    ang = (pos.astype(np.float32)[None, :] * freq[:, None]).astype(np.float32)
    cos, sin = np.cos(ang).astype(np.float32), np.sin(ang).astype(np.float32)
    p = np.arange(128)
    f = (p % 64) % 32
    sign = np.where((p % 64) < 32, -1.0, 1.0).astype(np.float32)
    return cos[f], sin[f] * sign[:, None]


def _shared_inputs(I, names=None):
    f = np.float32
    S = {}

    def want(n):
        return names is None or n in names
    if want("w_ada_s"):
        S["w_ada_s"] = np.concatenate([_slab(I["w_ada"][0]), _slab(I["w_ada"][1]), _slab(I["w_ada_kv"])], 0)
    if want("w_min_s") or want("w_gate"):
        wmi = I["w_m_in"][0]
        S["w_min_s"] = _slab(wmi[:, :12288])
        S["w_gate"] = np.ascontiguousarray(wmi[:, 12288:].reshape(KC, 128, 16).transpose(1, 0, 2))
    if want("w_mout_s"):
        S["w_mout_s"] = np.ascontiguousarray(I["w_m_out"][0].reshape(2, 16, 128, 32, 128).transpose(0, 3, 2, 1, 4))
    if want("w_ffi_s"):
        order = np.empty(NSL, np.int64)
        order[0::2] = np.arange(FC)
        order[1::2] = FC + np.arange(FC)
        S["w_ffi_s"] = np.stack([_slab(I["w_ffn_in"][l])[order] for l in range(2)])
    if want("w_ffo_s"):
        ffo = np.zeros((2, 4, 32, 128, 22, 128), f)
        for l in range(2):
            W = I["w_ffn_out"][l].reshape(FC, 128, 32, 128)
            for g, (f0, nf) in enumerate(FGRP):
                ffo[l, g, :, :, :nf, :] = W[f0:f0 + nf].transpose(2, 1, 0, 3)
        S["w_ffo_s"] = ffo
    if want("w_kv_s"):
        wk = I["w_kv"][:, :512].reshape(KC, 128, 8, 64).transpose(2, 1, 0, 3)
        S["w_kv_s"] = np.ascontiguousarray(np.concatenate([np.concatenate([wk, wk], -1), _slab(I["w_kv"][:, 512:])], 0))
    if want("w_q_s"):
        S["w_q_s"] = _slab(I["w_q"][0])
    if want("w_o_s"):
        S["w_o_s"] = _slab(I["w_o"][0])
    S["gT"] = _vecT(np.stack([I["g_norm1"][0], I["g_norm2"][0], I["g_kv"], I["g_norm1"][1], I["g_norm2"][1], I["g_final"]]))
    S["gheadT"] = _vecT(I["g_m_head"][0])
    S["bif"] = np.ascontiguousarray(np.stack([I["b_m_i"][0], I["b_m_f"][0]], 1))
    bk = I["b_kv"][:512].reshape(8, 64)
    S["bkv"] = np.ascontiguousarray(np.concatenate([np.concatenate([bk, bk], 1).T, I["b_kv"][512:].reshape(4, 128).T], 1))
    S["bqT"] = _vecT(I["b_q"][0]); S["boT"] = _vecT(I["b_o"][0])
    sk = I["sinks"][0]
    S["sinks_bc"] = np.ascontiguousarray(np.broadcast_to(sk[None, :], (128, 64)))
    S["nsinks_bc"] = np.ascontiguousarray(np.broadcast_to(np.negative(sk)[None, :], (128, 64)))
    hp = np.arange(64)
    hh = 8 * (hp // 8) + 2 * (hp % 4) + ((hp % 8) // 4)
    S["sinksT"] = np.ascontiguousarray(np.stack([sk[hh], np.negative(sk[hh])], 1))
    S["wconvT"] = np.ascontiguousarray(I["w_conv"][:, :, _SL_COL].transpose(3, 0, 2, 1))
    S["bconvT"] = np.ascontiguousarray(I["b_conv"][:, _SL_COL].transpose(2, 0, 1))
    S["ident"] = np.eye(128, dtype=f)
    m = np.arange(128)
    partner = np.where((m % 64) < 32, m + 32, m - 32)
    R = np.zeros((128, 128), f)
    R[partner, m] = 1.0
    S["Rmat"] = R
    S["cmask"] = (m[:, None] <= m[None, :]).astype(f)
    t = np.arange(128)[:, None]
    j = np.arange(256)[None, :]
    valid = np.where(j < 128, j > t, (j - 128) <= t)
    am = np.where(valid, 0.0, -30000.0).astype(f)
    am1 = am.copy()
    am1[:, :128] = -30000.0
    S["amask"] = np.stack([am, am1])
    bm = np.zeros((8, 8, 4), f)
    for k in range(8):
        bm[k, k, :] = 1.0
    S["bm"] = bm
    hp_ = np.arange(64)
    mg = np.zeros((64, 16), f)
    mg[hp_, ((hp_ % 8) // 4) * 8 + hp_ // 8] = 1.0
    S["mg"] = mg
    return {k: np.ascontiguousarray(v, dtype=f) for k, v in S.items()}


def _core_inputs(I, core):
    f = np.float32
    b, half = core // 2, core % 2
    xp = I["x_prompt"][b]
    s0 = 0 if half == 0 else 896
    sq = slice(4 * core, 4 * core + 4)
    M = {}
    M["xpre"] = _fm(xp[0:NPRE]); M["xfull"] = _fm(xp[s0:s0 + NFULL]); M["xs"] = _fm(I["x_sample"][sq, 0])
    M["c5T"] = _fm(np.concatenate([I["c_prompt"][b:b + 1], I["c_sample"][sq]], 0))
    M["flag"] = np.full((128, 1), float(half), f)
    rp = np.zeros((3, 128, 2, NCMAX), f)
    for ti in range(3):
        pos = np.concatenate([s0 + ti * 384 + np.arange(384), np.full(4, 16384)])
        c_, s_ = _rope_tables(pos)
        rp[ti, :, 0], rp[ti, :, 1] = c_, s_
    M["rope"] = rp
    M["st_C"] = I["state_mlstm_C"][0, sq].reshape(4, H, 2, 128, 512).transpose(0, 1, 3, 2, 4)
    M["st_n"] = I["state_mlstm_n"][0, sq].reshape(4, H, 2, 128).transpose(3, 1, 2, 0)
    M["st_m"] = I["state_mlstm_m"][0, sq].T
    cc = I["cache_conv"][:, sq]
    M["cconv"] = cc[:, :, :, _SL_COL].transpose(0, 4, 3, 1, 2)
    M["cconv_r1"] = cc[:, :, 1, :]
    ck = I["cache_k_win"][sq, 1:]
    ckt = ck.transpose(0, 3, 2, 1)
    M["ckT"] = np.concatenate([ckt, ckt], 1)
    M["ck_old"] = ck.reshape(4, 127, 512)
    M["cv_old"] = I["cache_v_win"][sq, 1:].reshape(4, 127, 512)
    return {k: np.ascontiguousarray(v, dtype=f) for k, v in M.items()}


_NC_CACHE = []


def kernel(**inputs):
    I = {k: np.asarray(v) for k, v in inputs.items()}
    if not _NC_CACHE:
        _NC_CACHE.append(Builder().build())
    nc = _NC_CACHE[0]
    shared = _shared_inputs(I)
    in_maps = []
    for core in range(8):
        m = dict(shared)
        m.update(_core_inputs(I, core))
        in_maps.append(m)
    res = run_bass_kernel_spmd(nc, in_maps, core_ids=list(range(8))).results
    f = np.float32
    y_prompt = np.zeros((4, 2048, D), f); y_sample = np.zeros((32, 1, D), f)
    C_p = np.zeros((1, 4, H, 256, 512), f); n_p = np.zeros((1, 4, H, 256), f); m_p = np.zeros((1, 4, H), f)
    conv_p = np.zeros((2, 4, 2, 2 * FF), f); kw_p = np.zeros((4, 128, 8, 64), f); vw_p = np.zeros((4, 128, 8, 64), f)
    C_s = np.zeros((1, 32, H, 256, 512), f); n_s = np.zeros((1, 32, H, 256), f); m_s = np.zeros((1, 32, H), f)
    conv_s = np.zeros((2, 32, 2, 2 * FF), f); kw_s = np.zeros((32, 128, 8, 64), f); vw_s = np.zeros((32, 128, 8, 64), f)
    for core in range(8):
        r = res[core]
        b, half = core // 2, core % 2
        sq = slice(4 * core, 4 * core + 4)
        yt = r["yT"].transpose(2, 1, 0).reshape(NFULL, D)
        if half == 0:
            y_prompt[b, 0:NFULL] = yt
        else:
            y_prompt[b, NFULL:2048] = yt[256:]
            C_p[0, b] = r["C_out"].transpose(0, 2, 1, 3).reshape(H, 256, 512)
            n_p[0, b] = r["n_out"].transpose(1, 2, 0).reshape(H, 256)
            m_p[0, b] = r["m_out"][:, 0]
            cp = r["convp"]
            for l in range(2):
                conv_p[l, b][:, _SL_COL] = cp[l].transpose(2, 1, 0)
            kw_p[b] = r["kwinp"].transpose(2, 1, 0)
            vw_p[b] = r["vwinp"].transpose(2, 1, 0).reshape(128, 4, 2, 64).reshape(128, 8, 64)
        y_sample[sq, 0] = r["ysT"].transpose(2, 1, 0).reshape(4, D)
        C_s[0, sq] = r["Cs_out"].transpose(0, 1, 3, 2, 4).reshape(4, H, 256, 512)
        n_s[0, sq] = r["ns_out"].transpose(3, 1, 2, 0).reshape(4, H, 256)
        m_s[0, sq] = r["ms_out"].T
        conv_s[:, sq, 0] = r["convs_old"]
        cn = r["convs_new"]
        for l in range(2):
            conv_s[l, sq, 1][:, _SL_COL] = cn[l].transpose(2, 1, 0)
        kw_s[sq, 0:127] = r["kwins_old"].reshape(4, 127, 8, 64)
        kw_s[sq, 127] = r["knew"].transpose(2, 1, 0)
        vw_s[sq, 0:127] = r["vwins_old"].reshape(4, 127, 8, 64)
        vw_s[sq, 127] = r["vnew"].reshape(4, 8, 64)
    return (y_prompt, y_sample, C_p, n_p, m_p, conv_p, kw_p, vw_p, C_s, n_s, m_s, conv_s, kw_s, vw_s)
```

```python
import contextlib
import numpy as np
import concourse.bass as bass
import concourse.mybir as mybir
from concourse.bass_utils import run_bass_kernel_spmd

F32 = mybir.dt.float32
BF16 = mybir.dt.bfloat16
AF = mybir.ActivationFunctionType
ALU = mybir.AluOpType
AX = mybir.AxisListType

D = 4096
KC = 32
FF = 11008
FC = 86
NSL = 172
H = 8
NPRE = 896
NFULL = 1152
NSMP = 4
BLK = 128
EPS = 1e-6
FGRP = [(0, 22), (22, 22), (44, 21), (65, 21)]
NCMAX = 388
NSLOT = 3
ARENA_BYTES = 57000

ENGS = ("pe", "act", "dve", "pool", "sp")


class Buf:
    __slots__ = ("w", "r")

    def __init__(self):
        self.w = {}
        self.r = {}


class Prog:
    def __init__(self, nc, es):
        self.nc = nc
        self.es = es
        self.q = {e: [] for e in ENGS}
        self.waited = {e: {} for e in ENGS}
        self.cnt = {}
        self.sems = {}
        for e in ("pe", "act", "dve", "pool"):
            self.new_sem(e)
        self.rr = 0

    def new_sem(self, key):
        self.sems[key] = self.es.enter_context(self.nc.semaphore("s_" + key))
        self.cnt[key] = 0
        return key

    @staticmethod
    def _deps(reads, writes):
        d = {}
        for b in reads:
            for k, v in b.w.items():
                if d.get(k, 0) < v:
                    d[k] = v
        for b in writes:
            for k, v in b.w.items():
                if d.get(k, 0) < v:
                    d[k] = v
            for k, v in b.r.items():
                if d.get(k, 0) < v:
                    d[k] = v
        return d

    def _record(self, eng, deps, fn, semkey, amount, reads, writes, skip_self=False):
        waits = []
        wd = self.waited[eng]
        for k, v in deps.items():
            if skip_self and k == eng:
                continue
            if wd.get(k, 0) < v:
                wd[k] = v
                waits.append((k, v))
        self.cnt[semkey] += amount
        val = self.cnt[semkey]
        for b in writes:
            b.w = {semkey: val}
            b.r = {}
        for b in reads:
            b.r[semkey] = val
        self.q[eng].append((waits, fn, semkey, amount))

    def op(self, eng, method, reads, writes, **kw):
        self._record(eng, self._deps(reads, writes), lambda h: getattr(h, method)(**kw), eng, 1,
                     reads, writes, skip_self=(eng == "pe"))

    def mm(self, out, pairs, reads, writes, start=True, stop=True):
        n = len(pairs)

        def fn(h):
            ins = None
            for i, (l, r) in enumerate(pairs):
                ins = h.matmul(out, lhsT=l, rhs=r, start=(start and i == 0), stop=(stop and i == n - 1))
            return ins
        self._record("pe", self._deps(reads, writes), fn, "pe", 1, reads, writes, skip_self=True)

    def dma(self, qeng, semkey, reads, writes, **kw):
        deps = self._deps(reads, writes)
        if deps.get(semkey, 0) < self.cnt[semkey]:
            deps[semkey] = self.cnt[semkey]
        self._record(qeng, deps, lambda h: h.dma_start(**kw), semkey, 16, reads, writes)

    def final_wait(self, eng):
        waits = [(k, v) for k, v in self.cnt.items() if v > 0]
        self.q[eng].append((waits, None, None, 0))

    def emit(self):
        nc = self.nc
        with nc.Block() as block:
            def run(engname):
                def body(h):
                    for waits, fn, semkey, amount in self.q[engname]:
                        for k, v in waits:
                            h.wait_ge(self.sems[k], v)
                        if fn is not None:
                            fn(h).then_inc(self.sems[semkey], amount)
                return body
            block.tensor(run("pe"))
            block.scalar(run("act"))
            block.vector(run("dve"))
            block.gpsimd(run("pool"))
            block.sync(run("sp"))


class _Lazy(dict):
    def __init__(self, b, shapes, kind):
        super().__init__()
        self.b, self.shapes, self.kind = b, shapes, kind

    def __missing__(self, name):
        shp = list(self.shapes[name])
        ov = self.b.dbg.get("shape_" + name)
        if ov is not None:
            shp = list(ov)
        ap = self.b.nc.dram_tensor(name, shp, F32, kind=self.kind).ap()
        self[name] = ap
        return ap


class T:
    __slots__ = ("ap", "b")

    def __init__(self, ap, b=None):
        self.ap = ap
        self.b = b if b is not None else Buf()


class Builder:
    def __init__(self, dbg=None):
        self.nc = bass.Bass("TRN2", target_bir_lowering=False)
        self.es = contextlib.ExitStack()
        self.dbg = dbg or {}

    def dbg_dump(self, name, t, ap=None):
        ap = t.ap if ap is None else ap
        shp = [int(x) for x in ap.shape]
        self.dout.shapes[name] = shp
        self.store(self.dout[name], t, ap)

    def inp(self, name, shape, dt=F32):
        self.din[name] = self.nc.dram_tensor(name, list(shape), dt, kind="ExternalInput").ap()
        return self.din[name]

    def outp(self, name, shape, dt=F32):
        self.dout[name] = self.nc.dram_tensor(name, list(shape), dt, kind="ExternalOutput").ap()
        return self.dout[name]

    def sb(self, name, shape, dt=F32):
        return T(self.es.enter_context(self.nc.sbuf_tensor(name, list(shape), dt))[:])

    def arena_reset(self):
        P = self.P
        ev = {}
        for t in self.arena_live:
            for dct in (t.b.w, t.b.r):
                for k, v in dct.items():
                    if ev.get(k, 0) < v:
                        ev[k] = v
        self.arena_prev = ev
        self.arena_live = []
        self.arena_off = 0

    def carve(self, shape, dt=F32):
        esz = 4 if dt == F32 else 2
        n = 1
        for s in shape[1:]:
            n *= s
        nbytes = (n * esz + 7) // 8 * 8
        off = self.arena_off
        assert off + nbytes <= ARENA_BYTES, ("arena overflow", off, nbytes)
        self.arena_off += nbytes
        ap = self.arena[0:shape[0], off // 2:(off + n * esz) // 2]
        if dt == F32:
            ap = ap.bitcast(F32)
        if len(shape) == 3:
            ap = ap.rearrange("p (a b) -> p a b", a=shape[1])
        elif len(shape) == 4:
            ap = ap.rearrange("p (a b c) -> p a b c", a=shape[1], b=shape[2])
        t = T(ap)
        t.b.r = dict(self.arena_prev)
        self.arena_live.append(t)
        return t

    def wslab(self, src, kcn, ncol=128):
        i = self.wi
        self.wi += 1
        s = i % NSLOT
        slot = self.slots[s]
        view = slot.ap[:, 0:kcn * ncol].rearrange("p (k n) -> p k n", k=kcn)
        self.P.dma("pool", f"w{s}", [], [slot.b], out=view, in_=src)
        return T(view, slot.b)

    def small_load(self, dst, src, q="sp", dram_buf=None, dst_ap=None):
        k = f"d{self.P.rr % 8}"
        self.P.rr += 1
        self.P.dma(q, k, [] if dram_buf is None else [dram_buf], [dst.b], out=(dst.ap if dst_ap is None else dst_ap), in_=src)

    def store(self, dst_dram, src, src_ap=None, q="sp", dram_buf=None):
        k = f"o{self.P.rr % 8}"
        self.P.rr += 1
        self.P.dma(q, k, [src.b], [] if dram_buf is None else [dram_buf], out=dst_dram, in_=(src.ap if src_ap is None else src_ap))

    def build(self):
        nc, es = self.nc, self.es
        with es:
            self.P = P = Prog(nc, es)
            for i in range(NSLOT):
                P.new_sem(f"w{i}")
            for i in range(8):
                P.new_sem(f"d{i}")
                P.new_sem(f"o{i}")
            self.declare_dram()
            self.alloc()
            stage = self.dbg.get("stage", 99)
            self.prologue()
            if stage >= 1:
                self.adaln()
            if "dump_mod" in self.dbg:
                self.dbg_dump("dbg_modT", self.modT)
                self.dbg_dump("dbg_Amod", self.Amod)
            self.first_state = True
            if stage >= 2:
                for ti, nb in enumerate((3, 3, 1)[:self.dbg.get("npre", 3)]):
                    self.run_tile("pre", ti, nb, ti * 384, False)
            if "dump_pre" in self.dbg:
                self.dbg_dump("dbg_nst", self.nst); self.dbg_dump("dbg_mst", self.mst)
                self.dout.shapes["dbg_C"] = [H, 128, 2, 512]
                for h in range(H):
                    self.P.dma("sp", f"o{h}", [self.cscr_b[h]], [], out=self.dout["dbg_C"][h], in_=self.C_scr[h])
            if stage >= 3:
                for ti in range(self.dbg.get("nmain", 3)):
                    self.run_tile("main", ti, 3, ti * 384, ti == 0)
            P.final_wait("sp")
            P.emit()
        return nc

    IN_SHAPES = {
        "xpre": [128, KC, NPRE], "xfull": [128, KC, NFULL], "xs": [128, KC, NSMP], "c5T": [128, KC, 5], "flag": [128, 1],
        "rope": [3, 128, 2, NCMAX], "st_C": [4, H, 128, 2, 512], "st_n": [128, H, 2, 4], "st_m": [8, 4],
        "cconv": [2, 128, NSL, 4, 2], "cconv_r1": [2, 4, 2 * FF], "ckT": [4, 128, 8, 127], "ck_old": [4, 127, 512],
        "cv_old": [4, 127, 512], "w_ada_s": [448, 128, KC, 128], "w_min_s": [96, 128, KC, 128], "w_gate": [128, KC, 16],
        "w_mout_s": [2, 32, 128, 16, 128], "w_ffi_s": [2, NSL, 128, KC, 128], "w_ffo_s": [2, 4, 32, 128, 22, 128],
        "w_kv_s": [12, 128, KC, 128], "w_q_s": [32, 128, KC, 128], "w_o_s": [32, 128, KC, 128],
        "gT": [128, 6, KC], "gheadT": [128, KC], "bif": [8, 2], "bkv": [128, 12], "bqT": [128, KC], "boT": [128, KC],
        "nsinks_bc": [128, 64], "sinks_bc": [128, 64], "sinksT": [64, 2], "wconvT": [128, 2, NSL, 3], "bconvT": [128, 2, NSL],
        "ident": [128, 128], "Rmat": [128, 128], "cmask": [128, 128], "amask": [2, 128, 256], "bm": [8, 8, 4], "mg": [64, 16]}
    OUT_SHAPES = {
        "yT": [128, KC, NFULL], "ysT": [128, KC, NSMP], "C_out": [H, 128, 2, 512], "n_out": [128, H, 2], "m_out": [8, 1],
        "convp": [2, 128, NSL, 2], "kwinp": [64, 8, 128], "vwinp": [128, 4, 128], "Cs_out": [4, H, 128, 2, 512],
        "ns_out": [128, H, 2, 4], "ms_out": [8, 4], "convs_old": [2, 4, 2 * FF], "convs_new": [2, 128, NSL, 4],
        "kwins_old": [4, 127, 512], "knew": [64, 8, 4], "vwins_old": [4, 127, 512], "vnew": [4, 512]}

    def declare_dram(self):
        self.din = _Lazy(self, self.IN_SHAPES, "ExternalInput")
        self.dout = _Lazy(self, self.OUT_SHAPES, "ExternalOutput")
        self.C_scr = self.nc.dram_tensor("C_scr", [H, 128, 2, 512], F32).ap()
        self.cscr_b = [Buf() for _ in range(H)]

    def alloc(self):
        sb, nc, es = self.sb, self.nc, self.es
        self.xT = sb("xT", [128, KC, NCMAX]); self.hn = sb("hn", [128, KC, NCMAX], BF16)
        self.slots = [sb(f"slot{i}", [128, KC * 128], BF16) for i in range(NSLOT)]
        self.wi = 0
        self.modT = sb("modT", [128, 448, 5]); self.Amod = sb("Amod", [128, 5, KC, 5])
        self.gT = sb("gTs", [128, 6, KC]); self.gheadT = sb("gheadTs", [128, KC])
        self.bif = sb("bifs", [8, 2]); self.bif15 = sb("bif15", [8, 2]); self.bkv = sb("bkvs", [128, 12])
        self.bqT = sb("bqTs", [128, KC]); self.boT = sb("boTs", [128, KC])
        self.nsinks = sb("nsinks", [128, 64]); self.sinks = sb("sinkss", [128, 64]); self.sinksT = sb("sinksTs", [64, 2])
        self.wconvT = sb("wconvTs", [128, 2, NSL, 3]); self.bconvT = sb("bconvTs", [128, 2, NSL])
        self.ident = sb("idents", [128, 128]); self.identb = sb("identb", [128, 128], BF16)
        self.ones = sb("ones", [128, 128]); self.onesb = sb("onesb", [128, 2], BF16)
        self.Rmat = sb("Rmats", [128, 128]); self.cmask = sb("cmasks", [128, 128]); self.amask = sb("amasks", [128, 2, 256])
        self.bm = sb("bms", [8, 8, 4]); self.flag = sb("flags", [128, 1]); self.mg = sb("mgs", [64, 16])
        self.wg = sb("wg", [128, KC, 16], BF16)
        self.ropeT = sb("ropeT", [128, 2, NCMAX])
        self.nst = sb("nst", [128, H, 2]); self.mst = sb("mst", [8, 1])
        self.uhist = sb("uhist", [128, 2, NSL, 2])
        self.kTd = sb("kTd", [128, 8, 128 + NCMAX], BF16); self.vtok = sb("vtok", [128, 4, 512], BF16)
        self.sq = [sb(f"sq{i}", [128, NCMAX]) for i in range(2)]
        self.tmp = [sb(f"tmp{i}", [128, NCMAX]) for i in range(2)]
        self.rstd = sb("rstd", [128, NCMAX]); self.tmps = sb("tmps", [128, 4])
        self.cs32 = sb("cs32", [128, KC, 5]); self.csT = sb("csT", [128, KC, 5], BF16)
        self.epsT = sb("epsT", [128, 1])
        self.arena = es.enter_context(nc.sbuf_tensor("arena", [128, ARENA_BYTES // 2], BF16))[:]
        self.arena_live = []
        self.arena_prev = {}
        self.arena_off = 0
        self.ps = [T(es.enter_context(nc.psum_tensor(f"ps{i}", [128, 512], F32))[:]) for i in range(8)]
        self.pj = 0

    def pbank(self):
        self.pj ^= 1
        return self.ps[2 if self.pj else 7]

    def prologue(self):
        P, L, d = self.P, self.small_load, self.din
        for t, n in ((self.gT, "gT"), (self.gheadT, "gheadT"), (self.bif, "bif"), (self.bkv, "bkv"), (self.bqT, "bqT"),
                     (self.boT, "boT"), (self.nsinks, "nsinks_bc"), (self.sinks, "sinks_bc"), (self.sinksT, "sinksT"),
                     (self.wconvT, "wconvT"), (self.bconvT, "bconvT"), (self.ident, "ident"), (self.Rmat, "Rmat"),
                     (self.cmask, "cmask"), (self.bm, "bm"), (self.flag, "flag"), (self.cs32, "c5T"), (self.mg, "mg")):
            L(t, d[n])
        L(self.amask, d["amask"].rearrange("v p s -> p v s"))
        P.dma("pool", "d0", [], [self.wg.b], out=self.wg.ap, in_=d["w_gate"])
        P.op("dve", "tensor_copy", [self.ident.b], [self.identb.b], out=self.identb.ap, in_=self.ident.ap)
        P.op("dve", "memset", [], [self.ones.b], ap=self.ones.ap, constant=1.0)
        P.op("dve", "memset", [], [self.onesb.b], ap=self.onesb.ap, constant=1.0)
        P.op("dve", "memset", [], [self.epsT.b], ap=self.epsT.ap, constant=EPS)
        P.op("dve", "memset", [], [self.uhist.b], ap=self.uhist.ap, constant=0.0)
        P.op("dve", "memset", [], [self.kTd.b], ap=self.kTd.ap, constant=0.0)
        P.op("dve", "memset", [], [self.vtok.b], ap=self.vtok.ap, constant=0.0)
        P.op("dve", "memset", [], [self.nst.b], ap=self.nst.ap, constant=0.0)
        P.op("dve", "memset", [], [self.mst.b], ap=self.mst.ap, constant=0.0)
        P.op("dve", "tensor_scalar", [self.bif.b], [self.bif15.b], out=self.bif15.ap, in0=self.bif.ap,
             scalar1=1.0 / 15.0, scalar2=None, op0=ALU.mult)
        P.op("act", "activation", [self.cs32.b], [self.csT.b], out=self.csT.ap, in_=self.cs32.ap, func=AF.Silu)
        for src, dst in (("cconv_r1", "convs_old"), ("ck_old", "kwins_old"), ("cv_old", "vwins_old")):
            k = f"o{P.rr % 8}"
            P.rr += 1
            P.dma("sp", k, [], [], out=self.dout[dst], in_=d[src])

    def adaln(self):
        P = self.P
        src = self.din["w_ada_s"]
        for s in range(self.dbg.get("ada_slabs", 448)):
            slab = self.wslab(src[s], KC)
            bank = self.ps[(s // 64) % 2]
            col = (s % 64) * 5
            P.mm(bank.ap[:, col:col + 5], [(slab.ap[:, kc, :], self.csT.ap[:, kc, :]) for kc in range(KC)],
                 [slab.b, self.csT.b], [bank.b])
            if s % 64 == 63:
                s0 = s - 63
                P.op("act", "activation", [bank.b], [self.modT.b],
                     out=self.modT.ap[:, s0:s0 + 64, :].rearrange("p a b -> p (a b)"), in_=bank.ap[:, 0:320], func=AF.Copy)
        for n, (gi, sc, sh) in enumerate(NORMS):
            for q in range(5):
                P.op("dve", "scalar_tensor_tensor", [self.modT.b, self.gT.b], [self.Amod.b],
                     out=self.Amod.ap[:, n, :, q], in0=self.modT.ap[:, sc * 32:(sc + 1) * 32, q], scalar=1.0,
                     in1=self.gT.ap[:, gi, :], op0=ALU.add, op1=ALU.mult)

    def norm(self, n, ncols, npc, has_s, out_final=None):
        P, xT, hn = self.P, self.xT, self.hn
        pn = self.ps[6]
        for kc in range(KC):
            sq = self.sq[kc % 2]
            P.op("act", "activation", [xT.b], [sq.b], out=sq.ap[:, :ncols], in_=xT.ap[:, kc, :ncols], func=AF.Square)
            P.op("pe", "matmul", [sq.b, self.ones.b], [pn.b], out=pn.ap[:, :ncols], lhsT=self.ones.ap, rhs=sq.ap[:, :ncols],
                 start=(kc == 0), stop=(kc == KC - 1))
        P.op("act", "activation", [pn.b, self.epsT.b], [self.rstd.b], out=self.rstd.ap[:, :ncols], in_=pn.ap[:, :ncols],
             func=AF.Sqrt, scale=1.0 / D, bias=self.epsT.ap[:, 0:1])
        P.op("dve", "reciprocal", [self.rstd.b], [self.rstd.b], out=self.rstd.ap[:, :ncols], in_=self.rstd.ap[:, :ncols])
        for kc in range(KC):
            tmp = self.tmp[kc % 2]
            P.op("dve", "tensor_tensor", [xT.b, self.rstd.b], [tmp.b], out=tmp.ap[:, :ncols], in0=xT.ap[:, kc, :ncols],
                 in1=self.rstd.ap[:, :ncols], op=ALU.mult)
            if n == 5:
                P.op("act", "activation", [tmp.b, self.gT.b], [out_final.b], out=out_final.ap[:, kc, :ncols], in_=tmp.ap[:, :ncols],
                     func=AF.Copy, scale=self.gT.ap[:, 5, kc:kc + 1])
                continue
            gi, sc, sh = NORMS[n]
            P.op("act", "activation", [tmp.b, self.Amod.b, self.modT.b], [hn.b], out=hn.ap[:, kc, :npc], in_=tmp.ap[:, :npc],
                 func=AF.Identity, scale=self.Amod.ap[:, n, kc, 0:1], bias=self.modT.ap[:, sh * 32 + kc, 0:1])
            if has_s:
                P.op("dve", "tensor_tensor", [tmp.b, self.Amod.b], [self.tmps.b], out=self.tmps.ap, in0=tmp.ap[:, npc:npc + 4],
                     in1=self.Amod.ap[:, n, kc, 1:5], op=ALU.mult)
                P.op("dve", "tensor_tensor", [self.tmps.b, self.modT.b], [hn.b], out=hn.ap[:, kc, npc:npc + 4], in0=self.tmps.ap,
                     in1=self.modT.ap[:, sh * 32 + kc, 1:5], op=ALU.add)

    def proj(self, src, kcn, rhs_t, ncols, M=128):
        slab = self.wslab(src, kcn)
        bank = self.pbank()
        self.P.mm(bank.ap[0:M, :ncols], [(slab.ap[:, kc, 0:M], rhs_t.ap[:, kc, :ncols]) for kc in range(kcn)],
                  [slab.b, rhs_t.b], [bank.b])
        return bank

    def xupdate(self, bank, c, sec, ncols, npc, has_s, bias=None):
        P, xT, modT = self.P, self.xT, self.modT
        src = bank.ap
        rd = [bank.b]
        if bias is not None:
            t = self.tmp[0]
            P.op("dve", "tensor_scalar", [bank.b, bias.b], [t.b], out=t.ap[:, :ncols], in0=bank.ap[:, :ncols],
                 scalar1=bias.ap[:, c:c + 1], scalar2=None, op0=ALU.add)
            src = t.ap
            rd = [t.b]
        P.op("dve", "scalar_tensor_tensor", rd + [modT.b, xT.b], [xT.b], out=xT.ap[:, c, :npc], in0=src[:, :npc],
             scalar=modT.ap[:, sec * 32 + c, 0:1], in1=xT.ap[:, c, :npc], op0=ALU.mult, op1=ALU.add)
        if has_s:
            P.op("dve", "tensor_tensor", rd + [modT.b], [self.tmps.b], out=self.tmps.ap, in0=src[:, npc:npc + 4],
                 in1=modT.ap[:, sec * 32 + c, 1:5], op=ALU.mult)
            P.op("dve", "tensor_tensor", [self.tmps.b, xT.b], [xT.b], out=xT.ap[:, c, npc:npc + 4], in0=self.tmps.ap,
                 in1=xT.ap[:, c, npc:npc + 4], op=ALU.add)

    def run_tile(self, kind, ti, nblk, col0, has_s):
        P, d = self.P, self.din
        npc = nblk * BLK
        ncols = npc + (4 if has_s else 0)
        self.arena_reset()
        src = d["xpre"] if kind == "pre" else d["xfull"]
        P.dma("sp", "d0", [], [self.xT.b], out=self.xT.ap[:, :, :npc], in_=src[:, :, col0:col0 + npc])
        if has_s:
            P.dma("sp", "d1", [], [self.xT.b], out=self.xT.ap[:, :, npc:npc + 4], in_=d["xs"])
            self.xT.b.w["d0"] = P.cnt["d0"]
        if not self.dbg.get("skip_mixer"):
            self.norm(0, ncols, npc, has_s)
            self.mixer(kind, ti, nblk, ncols, npc, has_s)
        if kind == "pre":
            return
        ms = self.dbg.get("main_stop", 99)
        if ms <= 1:
            return self.dbg_dump(f"dbg_x{ti}", self.xT)
        self.arena_reset()
        if not self.dbg.get("skip_ffn0"):
            self.norm(1, ncols, npc, has_s)
            self.ffn(0, ti, ncols, npc, has_s)
        if ms <= 2:
            return self.dbg_dump(f"dbg_x{ti}", self.xT)
        self.arena_reset()
        self.attention(ti, nblk, ncols, npc, has_s)
        if ms <= 3:
            return self.dbg_dump(f"dbg_x{ti}", self.xT)
        self.arena_reset()
        self.norm(4, ncols, npc, has_s)
        self.ffn(1, ti, ncols, npc, has_s)
        self.arena_reset()
        ybuf = self.carve([128, KC, NCMAX])
        self.norm(5, ncols, npc, has_s, out_final=ybuf)
        self.store(self.dout["yT"][:, :, col0:col0 + npc], ybuf, ybuf.ap[:, :, :npc])
        if has_s:
            self.store(self.dout["ysT"], ybuf, ybuf.ap[:, :, npc:npc + 4])
        if ti == 2:
            for l in range(2):
                self.store(self.dout["convp"][l], self.uhist, self.uhist.ap[:, l])

    def mixer(self, kind, ti, nblk, ncols, npc, has_s):
        P, d, ps, hn = self.P, self.din, self.ps, self.hn
        main = kind == "main"
        C = self.carve
        qT = C([128, 2, NCMAX], BF16); kT = C([128, 2, NCMAX], BF16)
        vT = C([128, 4, NCMAX], BF16); oT = C([128, 4, NCMAX], BF16)
        ktok = [C([128, 256], BF16) for _ in range(2)]
        vtk = [C([128, 512], BF16) for _ in range(2)]
        otk = [C([128, 512], BF16) for _ in range(2)]
        PT = [C([128, 128], BF16) for _ in range(2)]
        Cp = C([128, 2, 512], BF16); npb = C([128, 2], BF16)
        hsg = [C([128, 512], BF16) for _ in range(2)]
        junk = C([128, 512], BF16)
        Cst = [C([128, 2, 512]) for _ in range(2)]
        hsT = C([128, 16, NCMAX], BF16)
        ig = C([8, NCMAX]); lf = C([8, NCMAX]); bt = C([8, NCMAX]); at = C([8, NCMAX]); ut = C([8, NCMAX]); dnf = C([8, NCMAX])
        tokm = C([128, 3, 2, 8]); sdbc = C([128, 8, 4]); Rexp = C([8, 8, 4])
        P.op("dve", "memset", [], [Rexp.b], ap=Rexp.ap, constant=0.0)
        sm = C([8, 16]); cvec = C([8, 4]); negc = C([8, 4]); sdv = C([8, 4])
        col = C([128, 8])
        ones8 = self.ones.ap[0:8, 0:BLK]
        pg = ps[0]
        P.mm(pg.ap[0:8, :ncols], [(self.wg.ap[:, kc, 0:8], hn.ap[:, kc, :ncols]) for kc in range(KC)], [self.wg.b, hn.b], [pg.b])
        pf = ps[1]
        P.mm(pf.ap[0:8, :ncols], [(self.wg.ap[:, kc, 8:16], hn.ap[:, kc, :ncols]) for kc in range(KC)], [self.wg.b, hn.b], [pf.b])
        P.op("act", "activation", [pg.b, self.bif15.b], [ig.b], out=ig.ap[:, :ncols], in_=pg.ap[0:8, :ncols], func=AF.Tanh,
             scale=1.0 / 15.0, bias=self.bif15.ap[:, 0:1])
        P.op("dve", "tensor_scalar", [ig.b], [ig.b], out=ig.ap[:, :ncols], in0=ig.ap[:, :ncols], scalar1=15.0, scalar2=None, op0=ALU.mult)
        P.op("act", "activation", [pf.b, self.bif15.b], [lf.b], out=lf.ap[:, :ncols], in_=pf.ap[0:8, :ncols], func=AF.Tanh,
             scale=1.0 / 15.0, bias=self.bif15.ap[:, 1:2])
        P.op("act", "activation", [lf.b], [lf.b], out=lf.ap[:, :ncols], in_=lf.ap[:, :ncols], func=AF.Exp, scale=-15.0)
        P.op("dve", "tensor_scalar", [lf.b], [lf.b], out=lf.ap[:, :ncols], in0=lf.ap[:, :ncols], scalar1=1.0, scalar2=None, op0=ALU.add)
        P.op("act", "activation", [lf.b], [lf.b], out=lf.ap[:, :ncols], in_=lf.ap[:, :ncols], func=AF.Ln)
        P.op("dve", "tensor_scalar", [lf.b], [lf.b], out=lf.ap[:, :ncols], in0=lf.ap[:, :ncols], scalar1=-1.0, scalar2=None, op0=ALU.mult)
        if main and ti == 0:
            P.op("dve", "tensor_scalar", [self.nst.b, self.flag.b], [self.nst.b], out=self.nst.ap, in0=self.nst.ap,
                 scalar1=self.flag.ap[:, 0:1], scalar2=None, op0=ALU.mult)
            P.op("dve", "tensor_scalar", [self.mst.b, self.flag.b], [self.mst.b], out=self.mst.ap, in0=self.mst.ap,
                 scalar1=self.flag.ap[0:8, 0:1], scalar2=None, op0=ALU.mult)
        mst = self.mst
        for blk in range(nblk):
            cs = slice(blk * BLK, (blk + 1) * BLK)
            P.op("dve", "tensor_tensor_scan", [lf.b, self.ones.b], [bt.b], out=bt.ap[:, cs], data0=ones8, data1=lf.ap[:, cs],
                 initial=0.0, op0=ALU.mult, op1=ALU.add)
            P.op("dve", "tensor_tensor", [ig.b, bt.b], [at.b], out=at.ap[:, cs], in0=ig.ap[:, cs], in1=bt.ap[:, cs], op=ALU.subtract)
            P.op("dve", "tensor_reduce", [at.b], [sm.b], out=sm.ap[:, blk:blk + 1], in_=at.ap[:, cs], axis=AX.X, op=ALU.max)
            P.op("dve", "tensor_tensor", [sm.b, mst.b], [cvec.b], out=cvec.ap[:, blk:blk + 1], in0=sm.ap[:, blk:blk + 1],
                 in1=mst.ap[:, 0:1], op=ALU.max)
            P.op("dve", "tensor_scalar", [cvec.b], [negc.b], out=negc.ap[:, blk:blk + 1], in0=cvec.ap[:, blk:blk + 1],
                 scalar1=-1.0, scalar2=None, op0=ALU.mult)
            P.op("act", "activation", [mst.b, negc.b], [sdv.b], out=sdv.ap[:, blk:blk + 1], in_=mst.ap[:, 0:1], func=AF.Exp,
                 bias=negc.ap[:, blk:blk + 1])
            P.op("act", "activation", [at.b, negc.b], [ut.b], out=ut.ap[:, cs], in_=at.ap[:, cs], func=AF.Exp,
                 bias=negc.ap[:, blk:blk + 1])
            P.op("act", "activation", [bt.b, negc.b], [dnf.b], out=dnf.ap[:, cs], in_=bt.ap[:, cs], func=AF.Exp, scale=-1.0,
                 bias=negc.ap[:, blk:blk + 1])
            P.op("dve", "tensor_tensor", [bt.b, cvec.b], [mst.b], out=mst.ap[:, 0:1], in0=bt.ap[:, (blk + 1) * BLK - 1:(blk + 1) * BLK],
                 in1=cvec.ap[:, blk:blk + 1], op=ALU.add)
        pt = ps[0]
        for blk in range(nblk):
            cs = slice(blk * BLK, (blk + 1) * BLK)
            for j, src in enumerate((ut, dnf)):
                o0 = (blk * 2 + j) * 8
                P.op("pe", "matmul", [src.b, self.ident.b], [pt.b], out=pt.ap[:, o0:o0 + 8], lhsT=src.ap[0:8, cs],
                     rhs=self.ident.ap[0:8, 0:8], start=True, stop=True)
        P.op("dve", "tensor_copy", [pt.b], [tokm.b], out=tokm.ap[:, 0:nblk].rearrange("p a b c -> p (a b c)"),
             in_=pt.ap[:, 0:nblk * 16])
        for h in range(H):
            P.op("dve", "tensor_tensor", [sdv.b, self.bm.b], [Rexp.b], out=Rexp.ap[:, h, 0:nblk], in0=sdv.ap[:, 0:nblk],
                 in1=self.bm.ap[:, h, 0:nblk], op=ALU.mult)
        pb_ = ps[1]
        P.op("pe", "matmul", [Rexp.b, self.ones.b], [pb_.b], out=pb_.ap[:, 0:32], lhsT=self.ones.ap[0:8, :],
             rhs=Rexp.ap.rearrange("p a b -> p (a b)"), start=True, stop=True)
        P.op("dve", "tensor_copy", [pb_.b], [sdbc.b], out=sdbc.ap.rearrange("p a b -> p (a b)"), in_=pb_.ap[:, 0:32])
        if has_s:
            S = self.sample_gates(ig, lf, npc)
        wsrc = d["w_min_s"]
        for h in range(H):
            Cs = Cst[h % 2]
            if self.first_state:
                P.op("dve", "memset", [], [Cs.b], ap=Cs.ap, constant=0.0)
            else:
                self.small_load(Cs, self.C_scr[h], dram_buf=self.cscr_b[h])
                if main and ti == 0:
                    P.op("dve", "tensor_scalar", [Cs.b, self.flag.b], [Cs.b], out=Cs.ap, in0=Cs.ap, scalar1=self.flag.ap[:, 0:1],
                         scalar2=None, op0=ALU.mult)
            if main:
                for c in range(2):
                    bk = self.proj(wsrc[h * 2 + c], KC, hn, ncols)
                    P.op("act", "activation", [bk.b], [qT.b], out=qT.ap[:, c, :ncols], in_=bk.ap[:, :ncols], func=AF.Copy, scale=1.0 / 16.0)
                    if has_s:
                        P.op("dve", "tensor_scalar", [bk.b], [S["q32"].b], out=S["q32"].ap[:, c, :], in0=bk.ap[:, npc:npc + 4],
                             scalar1=1.0 / 16.0, scalar2=None, op0=ALU.mult)
            for c in range(2):
                bk = self.proj(wsrc[16 + h * 2 + c], KC, hn, ncols)
                P.op("act", "activation", [bk.b], [kT.b], out=kT.ap[:, c, :ncols], in_=bk.ap[:, :ncols], func=AF.Copy)
                if has_s:
                    P.op("dve", "tensor_copy", [bk.b], [S["k32"].b], out=S["k32"].ap[:, c, :], in_=bk.ap[:, npc:npc + 4])
            for c in range(4):
                bk = self.proj(wsrc[32 + h * 4 + c], KC, hn, ncols)
                P.op("act", "activation", [bk.b], [vT.b], out=vT.ap[:, c, :ncols], in_=bk.ap[:, :ncols], func=AF.Copy)
                if has_s:
                    P.op("dve", "tensor_copy", [bk.b], [S["v32"].b], out=S["v32"].ap[:, c, :], in_=bk.ap[:, npc:npc + 4])
            if main:
                for c in range(4):
                    bk = self.proj(wsrc[64 + h * 4 + c], KC, hn, ncols)
                    P.op("act", "activation", [bk.b], [oT.b], out=oT.ap[:, c, :ncols], in_=bk.ap[:, :ncols], func=AF.Sigmoid)
                    if has_s:
                        P.op("act", "activation", [bk.b], [S["o32"].b], out=S["o32"].ap[:, c, :], in_=bk.ap[:, npc:npc + 4], func=AF.Sigmoid)
            hh = h % 4
            for blk in range(nblk):
                cs = slice(blk * BLK, (blk + 1) * BLK)
                kt, vt, ot, pT, hg = ktok[blk % 2], vtk[blk % 2], otk[blk % 2], PT[blk % 2], hsg[blk % 2]
                uS = tokm.ap[:, blk, 0, h:h + 1]
                dS = tokm.ap[:, blk, 1, h:h + 1]
                sdS = sdbc.ap[:, h, blk:blk + 1]
                b0 = ps[0]
                for c in range(2):
                    P.op("pe", "matmul", [kT.b, self.identb.b], [b0.b], out=b0.ap[:, c * 128:(c + 1) * 128], lhsT=kT.ap[:, c, cs],
                         rhs=self.identb.ap, start=True, stop=True)
                P.op("dve", "tensor_scalar", [b0.b, tokm.b], [kt.b], out=kt.ap, in0=b0.ap[:, 0:256], scalar1=uS, scalar2=None, op0=ALU.mult)
                b1 = ps[1]
                for c in range(4):
                    P.op("pe", "matmul", [vT.b, self.identb.b], [b1.b], out=b1.ap[:, c * 128:(c + 1) * 128], lhsT=vT.ap[:, c, cs],
                         rhs=self.identb.ap, start=True, stop=True)
                P.op("act", "activation", [b1.b], [vt.b], out=vt.ap, in_=b1.ap, func=AF.Copy)
                P.op("dve", "tensor_scalar", [Cs.b, sdbc.b], [Cp.b], out=Cp.ap, in0=Cs.ap, scalar1=sdS, scalar2=None, op0=ALU.mult)
                P.op("dve", "tensor_scalar", [self.nst.b, sdbc.b], [npb.b], out=npb.ap, in0=self.nst.ap[:, h, :], scalar1=sdS, scalar2=None, op0=ALU.mult)
                if main:
                    b2 = ps[4]
                    for c in range(4):
                        P.op("pe", "matmul", [oT.b, self.identb.b], [b2.b], out=b2.ap[:, c * 128:(c + 1) * 128], lhsT=oT.ap[:, c, cs],
                             rhs=self.identb.ap, start=True, stop=True)
                    P.op("act", "activation", [b2.b], [ot.b], out=ot.ap, in_=b2.ap, func=AF.Copy)
                    b3 = ps[3]
                    P.mm(b3.ap[:, 0:128], [(kT.ap[:, c, cs], qT.ap[:, c, cs]) for c in range(2)], [kT.b, qT.b], [b3.b])
                    P.op("dve", "scalar_tensor_tensor", [b3.b, tokm.b, self.cmask.b], [pT.b], out=pT.ap, in0=b3.ap[:, 0:128], scalar=uS,
                         in1=self.cmask.ap, op0=ALU.mult, op1=ALU.mult)
                    acc = ps[5]
                    P.mm(acc.ap, [(pT.ap, vt.ap), (qT.ap[:, 0, cs], Cp.ap[:, 0, :]), (qT.ap[:, 1, cs], Cp.ap[:, 1, :])],
                         [pT.b, vt.b, qT.b, Cp.b], [acc.b])
                    dac = ps[3]
                    P.mm(dac.ap[:, 256:257], [(pT.ap, self.onesb.ap[:, 0:1]), (qT.ap[:, 0, cs], npb.ap[:, 0:1]), (qT.ap[:, 1, cs], npb.ap[:, 1:2])],
                         [pT.b, self.onesb.b, qT.b, npb.b], [dac.b])
                    P.op("act", "activation", [acc.b], [junk.b, col.b], out=junk.ap, in_=acc.ap, func=AF.Square, accum_out=col.ap[:, 0:1])
                    P.op("dve", "tensor_scalar", [dac.b], [col.b], out=col.ap[:, 6:7], in0=dac.ap[:, 256:257], scalar1=-1.0, scalar2=None,
                         op0=ALU.mult)
                    P.op("dve", "tensor_tensor", [dac.b, col.b], [col.b], out=col.ap[:, 7:8], in0=dac.ap[:, 256:257], in1=col.ap[:, 6:7], op=ALU.max)
                    P.op("dve", "tensor_tensor", [col.b, tokm.b], [col.b], out=col.ap[:, 1:2], in0=col.ap[:, 7:8], in1=dS, op=ALU.max)
                    P.op("dve", "tensor_scalar", [col.b], [col.b], out=col.ap[:, 2:3], in0=col.ap[:, 1:2], scalar1=col.ap[:, 1:2], scalar2=EPS,
                         op0=ALU.mult, op1=ALU.mult)
                    P.op("dve", "scalar_tensor_tensor", [col.b], [col.b], out=col.ap[:, 3:4], in0=col.ap[:, 0:1], scalar=1.0 / 512.0,
                         in1=col.ap[:, 2:3], op0=ALU.mult, op1=ALU.add)
                    P.op("act", "activation", [col.b], [col.b], out=col.ap[:, 4:5], in_=col.ap[:, 3:4], func=AF.Sqrt)
                    P.op("dve", "reciprocal", [col.b], [col.b], out=col.ap[:, 5:6], in_=col.ap[:, 4:5])
                    P.op("dve", "scalar_tensor_tensor", [acc.b, col.b, ot.b], [hg.b], out=hg.ap, in0=acc.ap, scalar=col.ap[:, 5:6], in1=ot.ap,
                         op0=ALU.mult, op1=ALU.mult)
                    b6 = ps[6]
                    for c in range(4):
                        P.op("pe", "matmul", [hg.b, self.identb.b], [b6.b], out=b6.ap[:, c * 128:(c + 1) * 128], lhsT=hg.ap[:, c * 128:(c + 1) * 128],
                             rhs=self.identb.ap, start=True, stop=True)
                    for c in range(4):
                        P.op("act", "activation", [b6.b, self.gheadT.b], [hsT.b], out=hsT.ap[:, hh * 4 + c, cs], in_=b6.ap[:, c * 128:(c + 1) * 128],
                             func=AF.Copy, scale=self.gheadT.ap[:, h * 4 + c:h * 4 + c + 1])
                for c in range(2):
                    dC = ps[c]
                    P.mm(dC.ap, [(kt.ap[:, c * 128:(c + 1) * 128], vt.ap)], [kt.b, vt.b], [dC.b])
                    P.op("dve", "scalar_tensor_tensor", [Cs.b, sdbc.b, dC.b], [Cs.b], out=Cs.ap[:, c, :], in0=Cs.ap[:, c, :], scalar=sdS,
                         in1=dC.ap, op0=ALU.mult, op1=ALU.add)
                dn_ = ps[3]
                for c in range(2):
                    P.mm(dn_.ap[:, 260 + c:261 + c], [(kt.ap[:, c * 128:(c + 1) * 128], self.onesb.ap[:, 0:1])], [kt.b, self.onesb.b], [dn_.b])
                P.op("dve", "scalar_tensor_tensor", [self.nst.b, sdbc.b, dn_.b], [self.nst.b], out=self.nst.ap[:, h, :], in0=self.nst.ap[:, h, :],
                     scalar=sdS, in1=dn_.ap[:, 260:262], op0=ALU.mult, op1=ALU.add)
            self.store(self.C_scr[h], Cs, dram_buf=self.cscr_b[h])
            if main and ti == 2:
                self.store(self.dout["C_out"][h], Cs)
            if main and has_s:
                self.sample_head(h, S, hsT, hh, npc, Cst)
            if main and hh == 3:
                g = h // 4
                for c in range(KC):
                    bk = self.proj(d["w_mout_s"][g, c], 16, hsT, ncols)
                    self.xupdate(bk, c, 2, ncols, npc, has_s)
        self.first_state = False
        if main and ti == 2:
            self.store(self.dout["n_out"], self.nst)
            self.store(self.dout["m_out"], self.mst)
        if main and has_s:
            self.store(self.dout["ns_out"], S["nsn"])
            self.store(self.dout["ms_out"], S["mt"])


    def bcast8(self, tab, out_bc, bank):
        P = self.P
        R = self.carve([8, 8, 4])
        for h in range(H):
            P.op("dve", "tensor_tensor", [tab.b, self.bm.b], [R.b], out=R.ap[:, h, :], in0=tab.ap, in1=self.bm.ap[:, h, :], op=ALU.mult)
        P.op("pe", "matmul", [R.b, self.ones.b], [bank.b], out=bank.ap[:, 0:32], lhsT=self.ones.ap[0:8, :],
             rhs=R.ap.rearrange("p a b -> p (a b)"), start=True, stop=True)
        P.op("dve", "tensor_copy", [bank.b], [out_bc.b], out=out_bc.ap.rearrange("p a b -> p (a b)"), in_=bank.ap[:, 0:32])

    def sample_gates(self, ig, lf, npc):
        P, C, d = self.P, self.carve, self.din
        S = {"q32": C([128, 2, 4]), "k32": C([128, 2, 4]), "v32": C([128, 4, 4]), "o32": C([128, 4, 4]),
             "ws": C([128, 8, 4]), "si": C([128, 8, 4]), "dnf": C([128, 8, 4]), "mt": C([8, 4]),
             "ns": C([128, H, 2, 4]), "nsn": C([128, H, 2, 4])}
        ms = C([8, 4]); t1 = C([8, 4]); d1 = C([8, 4]); d2 = C([8, 4]); nm = C([8, 4])
        self.small_load(ms, d["st_m"]); self.small_load(S["ns"], d["st_n"])
        igs, lfs, mt = ig.ap[:, npc:npc + 4], lf.ap[:, npc:npc + 4], S["mt"]
        P.op("dve", "tensor_tensor", [lf.b, ms.b], [t1.b], out=t1.ap, in0=lfs, in1=ms.ap, op=ALU.add)
        P.op("dve", "tensor_tensor", [t1.b, ig.b], [mt.b], out=mt.ap, in0=t1.ap, in1=igs, op=ALU.max)
        P.op("dve", "tensor_tensor", [ig.b, mt.b], [d1.b], out=d1.ap, in0=igs, in1=mt.ap, op=ALU.subtract)
        P.op("dve", "tensor_tensor", [t1.b, mt.b], [d2.b], out=d2.ap, in0=t1.ap, in1=mt.ap, op=ALU.subtract)
        P.op("dve", "tensor_scalar", [mt.b], [nm.b], out=nm.ap, in0=mt.ap, scalar1=-1.0, scalar2=None, op0=ALU.mult)
        for t in (d1, d2, nm):
            P.op("act", "activation", [t.b], [t.b], out=t.ap, in_=t.ap, func=AF.Exp)
        self.bcast8(d1, S["ws"], self.ps[0]); self.bcast8(d2, S["si"], self.ps[1]); self.bcast8(nm, S["dnf"], self.ps[0])
        S["prod"] = C([128, 2, 8]); S["sums"] = C([128, 32]); S["c4"] = C([128, 16, 4]); S["hT"] = C([128, 4, 4])
        S["Dg"] = C([128, 4, 128]); S["wk"] = C([128, 2]); S["t44"] = C([128, 4, 4])
        return S

    def sample_head(self, h, S, hsT, hh, npc, Cst):
        P, d, ps = self.P, self.din, self.ps
        q32, k32, v32, o32, c4 = S["q32"], S["k32"], S["v32"], S["o32"], S["c4"]
        wsb, sib, dnb = S["ws"].ap[:, h, :], S["si"].ap[:, h, :], S["dnf"].ap[:, h, :]
        prod, sums = S["prod"], S["sums"]
        P.op("dve", "tensor_tensor", [q32.b, k32.b], [prod.b], out=prod.ap[:, :, 0:4], in0=q32.ap, in1=k32.ap, op=ALU.mult)
        P.op("dve", "tensor_tensor", [q32.b, S["ns"].b], [prod.b], out=prod.ap[:, :, 4:8], in0=q32.ap, in1=S["ns"].ap[:, h], op=ALU.mult)
        pA = ps[3]
        P.op("pe", "matmul", [prod.b, self.ones.b], [pA.b], out=pA.ap[:, 0:16], lhsT=self.ones.ap, rhs=prod.ap.rearrange("p a b -> p (a b)"),
             start=True, stop=True)
        P.op("dve", "tensor_copy", [pA.b], [sums.b], out=sums.ap[:, 0:16], in_=pA.ap[:, 0:16])
        qkn = c4.ap[:, 0:2].rearrange("p a b -> p (a b)")
        P.op("dve", "tensor_tensor", [sums.b], [c4.b], out=qkn, in0=sums.ap[:, 0:8], in1=sums.ap[:, 8:16], op=ALU.add)
        a1, t, den, dn, rdn, a1n, sin_ = (c4.ap[:, i, :] for i in range(2, 9))
        P.op("dve", "tensor_tensor", [S["ws"].b, c4.b], [c4.b], out=a1, in0=wsb, in1=c4.ap[:, 0, :], op=ALU.mult)
        P.op("dve", "tensor_tensor", [S["si"].b, c4.b], [c4.b], out=t, in0=sib, in1=c4.ap[:, 1, :], op=ALU.mult)
        P.op("dve", "tensor_tensor", [c4.b], [c4.b], out=den, in0=t, in1=a1, op=ALU.add)
        P.op("dve", "tensor_scalar", [c4.b], [c4.b], out=dn, in0=den, scalar1=-1.0, scalar2=None, op0=ALU.mult)
        P.op("dve", "tensor_tensor", [c4.b], [c4.b], out=dn, in0=dn, in1=den, op=ALU.max)
        P.op("dve", "tensor_tensor", [c4.b, S["dnf"].b], [c4.b], out=dn, in0=dn, in1=dnb, op=ALU.max)
        P.op("dve", "reciprocal", [c4.b], [c4.b], out=rdn, in_=dn)
        P.op("dve", "tensor_tensor", [c4.b], [c4.b], out=a1n, in0=a1, in1=rdn, op=ALU.mult)
        P.op("dve", "tensor_tensor", [c4.b, S["si"].b], [c4.b], out=sin_, in0=sib, in1=rdn, op=ALU.mult)
        pB = ps[5]
        for j in range(NSMP):
            Cs = Cst[j % 2]
            self.small_load(Cs, d["st_C"][j, h])
            for vc in range(4):
                P.mm(pB.ap[:, vc * 4 + j:vc * 4 + j + 1], [(Cs.ap[:, c, vc * 128:(vc + 1) * 128], q32.ap[:, c, j:j + 1]) for c in range(2)],
                     [Cs.b, q32.b], [pB.b])
            pV = ps[4]
            for vc in range(4):
                P.op("dve", "tensor_scalar", [self.ident.b, v32.b], [S["Dg"].b], out=S["Dg"].ap[:, vc, :], in0=self.ident.ap,
                     scalar1=v32.ap[:, vc, j:j + 1], scalar2=None, op0=ALU.mult)
            for vc in range(4):
                P.op("pe", "matmul", [S["Dg"].b, self.ones.b], [pV.b], out=pV.ap[:, vc * 128:(vc + 1) * 128], lhsT=self.ones.ap,
                     rhs=S["Dg"].ap[:, vc, :], start=True, stop=True)
            P.op("dve", "tensor_scalar", [k32.b, S["ws"].b], [S["wk"].b], out=S["wk"].ap, in0=k32.ap[:, :, j], scalar1=wsb[:, j:j + 1],
                 scalar2=None, op0=ALU.mult)
            for c in range(2):
                P.op("dve", "tensor_scalar", [Cs.b, S["si"].b], [Cs.b], out=Cs.ap[:, c, :], in0=Cs.ap[:, c, :], scalar1=sib[:, j:j + 1],
                     scalar2=None, op0=ALU.mult)
                P.op("dve", "scalar_tensor_tensor", [pV.b, S["wk"].b, Cs.b], [Cs.b], out=Cs.ap[:, c, :], in0=pV.ap, scalar=S["wk"].ap[:, c:c + 1],
                     in1=Cs.ap[:, c, :], op0=ALU.mult, op1=ALU.add)
            P.op("dve", "scalar_tensor_tensor", [S["ns"].b, S["si"].b, S["wk"].b], [S["nsn"].b], out=S["nsn"].ap[:, h, :, j],
                 in0=S["ns"].ap[:, h, :, j], scalar=sib[:, j:j + 1], in1=S["wk"].ap, op0=ALU.mult, op1=ALU.add)
            self.store(self.dout["Cs_out"][j, h], Cs)
        hT, t44 = S["hT"], S["t44"]
        for vc in range(4):
            P.op("dve", "tensor_tensor", [v32.b, c4.b], [t44.b], out=t44.ap[:, vc, :], in0=v32.ap[:, vc, :], in1=a1n, op=ALU.mult)
            P.op("dve", "tensor_tensor", [pB.b, c4.b], [hT.b], out=hT.ap[:, vc, :], in0=pB.ap[:, vc * 4:vc * 4 + 4], in1=sin_, op=ALU.mult)
        P.op("dve", "tensor_tensor", [hT.b, t44.b], [hT.b], out=hT.ap, in0=hT.ap, in1=t44.ap, op=ALU.add)
        P.op("dve", "tensor_tensor", [hT.b], [t44.b], out=t44.ap, in0=hT.ap, in1=hT.ap, op=ALU.mult)
        P.op("pe", "matmul", [t44.b, self.ones.b], [pA.b], out=pA.ap[:, 16:32], lhsT=self.ones.ap, rhs=t44.ap.rearrange("p a b -> p (a b)"),
             start=True, stop=True)
        P.op("dve", "tensor_copy", [pA.b], [sums.b], out=sums.ap[:, 16:32], in_=pA.ap[:, 16:32])
        ssq, rs = c4.ap[:, 9, :], c4.ap[:, 10, :]
        P.op("dve", "tensor_tensor", [sums.b], [c4.b], out=ssq, in0=sums.ap[:, 16:20], in1=sums.ap[:, 20:24], op=ALU.add)
        P.op("dve", "tensor_tensor", [sums.b, c4.b], [c4.b], out=ssq, in0=ssq, in1=sums.ap[:, 24:28], op=ALU.add)
        P.op("dve", "tensor_tensor", [sums.b, c4.b], [c4.b], out=ssq, in0=ssq, in1=sums.ap[:, 28:32], op=ALU.add)
        P.op("act", "activation", [c4.b, self.epsT.b], [c4.b], out=rs, in_=ssq, func=AF.Sqrt, scale=1.0 / 512.0, bias=self.epsT.ap[:, 0:1])
        P.op("dve", "reciprocal", [c4.b], [c4.b], out=rs, in_=rs)
        for vc in range(4):
            P.op("dve", "tensor_tensor", [hT.b, c4.b], [t44.b], out=t44.ap[:, vc, :], in0=hT.ap[:, vc, :], in1=rs, op=ALU.mult)
            P.op("dve", "tensor_tensor", [t44.b, o32.b], [t44.b], out=t44.ap[:, vc, :], in0=t44.ap[:, vc, :], in1=o32.ap[:, vc, :], op=ALU.mult)
            P.op("dve", "tensor_scalar", [t44.b, self.gheadT.b], [hsT.b], out=hsT.ap[:, hh * 4 + vc, npc:npc + 4], in0=t44.ap[:, vc, :],
                 scalar1=self.gheadT.ap[:, h * 4 + vc:h * 4 + vc + 1], scalar2=None, op0=ALU.mult)

    def ffn(self, l, ti, ncols, npc, has_s):
        P, d, C, hn = self.P, self.din, self.carve, self.hn
        act = C([128, 22, NCMAX], BF16)
        ue = [[C([128, NCMAX + 2]) for _ in range(2)] for _ in range(2)]
        y = [[C([128, NCMAX]) for _ in range(2)] for _ in range(2)]
        sg = [C([128, NCMAX]) for _ in range(2)]
        wcv, bcv, uh = self.wconvT, self.bconvT, self.uhist
        if has_s:
            cc = C([128, NSL, 4, 2]); unew = C([128, NSL, 4])
            self.small_load(cc, d["cconv"][l])
        for g, (f0, nf) in enumerate(FGRP):
            for jj in range(nf):
                j = f0 + jj
                for which in range(2):
                    sl = 2 * j + which
                    bank = self.proj(d["w_ffi_s"][l, sl], KC, hn, ncols)
                    u = ue[which][j % 2]
                    yy = y[which][j % 2]
                    w0, w1, w2 = (wcv.ap[:, l, sl, i:i + 1] for i in range(3))
                    P.op("dve", "tensor_copy", [uh.b], [u.b], out=u.ap[:, 0:2], in_=uh.ap[:, l, sl, :])
                    P.op("act", "activation", [bank.b], [u.b], out=u.ap[:, 2:2 + ncols], in_=bank.ap[:, :ncols], func=AF.Copy)
                    P.op("dve", "tensor_copy", [u.b], [uh.b], out=uh.ap[:, l, sl, :], in_=u.ap[:, npc:npc + 2])
                    P.op("dve", "tensor_scalar", [u.b, wcv.b, bcv.b], [yy.b], out=yy.ap[:, :npc], in0=u.ap[:, 0:npc], scalar1=w0,
                         scalar2=bcv.ap[:, l, sl:sl + 1], op0=ALU.mult, op1=ALU.add)
                    P.op("dve", "scalar_tensor_tensor", [u.b, wcv.b, yy.b], [yy.b], out=yy.ap[:, :npc], in0=u.ap[:, 1:npc + 1], scalar=w1,
                         in1=yy.ap[:, :npc], op0=ALU.mult, op1=ALU.add)
                    P.op("dve", "scalar_tensor_tensor", [u.b, wcv.b, yy.b], [yy.b], out=yy.ap[:, :npc], in0=u.ap[:, 2:npc + 2], scalar=w2,
                         in1=yy.ap[:, :npc], op0=ALU.mult, op1=ALU.add)
                    if has_s:
                        ys = yy.ap[:, npc:npc + 4]
                        P.op("dve", "tensor_scalar", [cc.b, wcv.b, bcv.b], [yy.b], out=ys, in0=cc.ap[:, sl, :, 0], scalar1=w0,
                             scalar2=bcv.ap[:, l, sl:sl + 1], op0=ALU.mult, op1=ALU.add)
                        P.op("dve", "scalar_tensor_tensor", [cc.b, wcv.b, yy.b], [yy.b], out=ys, in0=cc.ap[:, sl, :, 1], scalar=w1,
                             in1=ys, op0=ALU.mult, op1=ALU.add)
                        P.op("dve", "scalar_tensor_tensor", [u.b, wcv.b, yy.b], [yy.b], out=ys, in0=u.ap[:, 2 + npc:6 + npc], scalar=w2,
                             in1=ys, op0=ALU.mult, op1=ALU.add)
                        P.op("dve", "tensor_copy", [u.b], [unew.b], out=unew.ap[:, sl, :], in_=u.ap[:, 2 + npc:6 + npc])
                s_ = sg[j % 2]
                P.op("act", "activation", [y[0][j % 2].b], [s_.b], out=s_.ap[:, :ncols], in_=y[0][j % 2].ap[:, :ncols], func=AF.Silu)
                P.op("dve", "tensor_tensor", [s_.b, y[1][j % 2].b], [act.b], out=act.ap[:, jj, :ncols], in0=s_.ap[:, :ncols],
                     in1=y[1][j % 2].ap[:, :ncols], op=ALU.mult)
            for c in range(KC):
                bank = self.proj(d["w_ffo_s"][l, g, c][:, 0:nf, :], nf, act, ncols)
                self.xupdate(bank, c, 5 + 6 * l, ncols, npc, has_s)
        if has_s:
            self.store(self.dout["convs_new"][l], unew)

    def rope(self, kq, ncols, out_t, out_ap):
        P = self.P
        pr = self.ps[6]
        P.op("pe", "matmul", [kq.b, self.Rmat.b], [pr.b], out=pr.ap[:, :ncols], lhsT=self.Rmat.ap, rhs=kq.ap[:, :ncols], start=True, stop=True)
        t1, t2 = self.tmp
        P.op("dve", "tensor_tensor", [kq.b, self.ropeT.b], [t1.b], out=t1.ap[:, :ncols], in0=kq.ap[:, :ncols], in1=self.ropeT.ap[:, 0, :ncols], op=ALU.mult)
        P.op("dve", "tensor_tensor", [pr.b, self.ropeT.b], [t2.b], out=t2.ap[:, :ncols], in0=pr.ap[:, :ncols], in1=self.ropeT.ap[:, 1, :ncols], op=ALU.mult)
        P.op("dve", "tensor_tensor", [t1.b, t2.b], [out_t.b], out=out_ap, in0=t1.ap[:, :ncols], in1=t2.ap[:, :ncols], op=ALU.add)

    def attention(self, ti, nblk, ncols, npc, has_s):
        P, d, C, ps, hn = self.P, self.din, self.carve, self.ps, self.hn
        kTd, vtok = self.kTd, self.vtok
        kq32 = [C([128, NCMAX]) for _ in range(2)]
        kr32 = [C([128, NCMAX]) for _ in range(2)]
        qr = [C([128, NCMAX], BF16) for _ in range(2)]
        v32 = C([128, 4, NCMAX])
        oT = C([128, KC, NCMAX], BF16)
        Sm = [C([128, 256]) for _ in range(2)]
        Pf = [C([128, 256]) for _ in range(2)]
        Pn = [C([128, 256], BF16) for _ in range(2)]
        PTs = [C([128, 256], BF16) for _ in range(2)]
        colA = [C([128, 8]) for _ in range(2)]
        self.small_load(self.ropeT, d["rope"][ti])
        self.norm(2, ncols, npc, has_s)
        for g in range(8):
            bank = self.proj(d["w_kv_s"][g], KC, hn, ncols)
            kq, kr = kq32[g % 2], kr32[g % 2]
            P.op("dve", "tensor_scalar", [bank.b, self.bkv.b], [kq.b], out=kq.ap[:, :ncols], in0=bank.ap[:, :ncols], scalar1=self.bkv.ap[:, g:g + 1],
                 scalar2=None, op0=ALU.add)
            self.rope(kq, ncols, kr, kr.ap[:, :ncols])
            P.op("act", "activation", [kr.b], [kTd.b], out=kTd.ap[:, g, 128:128 + ncols], in_=kr.ap[:, :ncols], func=AF.Copy)
            if ti == 2:
                self.store(self.dout["kwinp"][:, g, :], kr, kr.ap[0:64, npc - 128:npc])
            if has_s:
                self.store(self.dout["knew"][:, g, :], kr, kr.ap[0:64, npc:npc + 4])
        for c in range(4):
            bank = self.proj(d["w_kv_s"][8 + c], KC, hn, ncols)
            P.op("dve", "tensor_scalar", [bank.b, self.bkv.b], [v32.b], out=v32.ap[:, c, :ncols], in0=bank.ap[:, :ncols],
                 scalar1=self.bkv.ap[:, 8 + c:9 + c], scalar2=None, op0=ALU.add)
        if ti == 2:
            self.store(self.dout["vwinp"], v32, v32.ap[:, :, npc - 128:npc])
        for blk in range(nblk):
            cs = slice(blk * BLK, (blk + 1) * BLK)
            pv = ps[0 + (blk % 2)]
            for c in range(4):
                P.op("pe", "matmul", [v32.b, self.ident.b], [pv.b], out=pv.ap[:, c * 128:(c + 1) * 128], lhsT=v32.ap[:, c, cs], rhs=self.ident.ap,
                     start=True, stop=True)
            P.op("act", "activation", [pv.b], [vtok.b], out=vtok.ap[:, blk + 1, :], in_=pv.ap, func=AF.Copy)
        if has_s:
            vts = C([4, 512])
            pvs = ps[3]
            for c in range(4):
                P.op("pe", "matmul", [v32.b, self.ident.b], [pvs.b], out=pvs.ap[0:4, c * 128:(c + 1) * 128], lhsT=v32.ap[:, c, npc:npc + 4],
                     rhs=self.ident.ap, start=True, stop=True)
            P.op("dve", "tensor_copy", [pvs.b], [vts.b], out=vts.ap, in_=pvs.ap[0:4, :])
            self.vnew_b = Buf()
            self.store(self.dout["vnew"], vts, dram_buf=self.vnew_b)
            qTs = C([128, KC, 4], BF16)
        astop = self.dbg.get("attn_stop", 99)
        if astop <= 1:
            return
        self.norm(3, ncols, npc, has_s)
        def q_start(c):
            return {"c": c, "slab": self.wslab(d["w_q_s"][c], KC), "bank": self.pbank()}

        def q_part(st, k):
            lo, hi = (0, 11, 22, 32)[k], (0, 11, 22, 32)[k + 1]
            slab, bank = st["slab"], st["bank"]
            P.mm(bank.ap[:, :ncols], [(slab.ap[:, kc, :], hn.ap[:, kc, :ncols]) for kc in range(lo, hi)], [slab.b, hn.b], [bank.b],
                 start=(lo == 0), stop=(hi == KC))

        def q_finish(st):
            c, bank = st["c"], st["bank"]
            kq, q = kq32[c % 2], qr[c % 2]
            P.op("dve", "tensor_scalar", [bank.b, self.bqT.b], [kq.b], out=kq.ap[:, :ncols], in0=bank.ap[:, :ncols], scalar1=self.bqT.ap[:, c:c + 1],
                 scalar2=None, op0=ALU.add)
            self.rope(kq, ncols, q, q.ap[:, :ncols])
            if has_s:
                P.op("dve", "tensor_copy", [q.b], [qTs.b], out=qTs.ap[:, c, :], in_=q.ap[:, npc:npc + 4])

        st0 = q_start(0)
        for k in range(3):
            q_part(st0, k)
        q_finish(st0)
        H2 = range(2)
        def alias(off, shape, dt):
            esz = 4 if dt == F32 else 2
            base = v32.ap.rearrange("p a b -> p (a b)")[:, off // 4:(off + shape[1] * esz) // 4]
            ap = base if dt == F32 else base.bitcast(BF16)
            t = T(ap)
            t.b.r = dict(v32.b.r)
            for k_, v_ in v32.b.w.items():
                if t.b.r.get(k_, 0) < v_:
                    t.b.r[k_] = v_
            return t
        Sm2 = [Sm, [alias(0, [128, 256], F32), alias(1024, [128, 256], F32)]]
        Pf2 = [Pf, [alias(2048, [128, 256], F32), alias(3072, [128, 256], F32)]]
        Pn2 = [Pn, [alias(4096, [128, 256], BF16), alias(4608, [128, 256], BF16)]]
        PT2 = [PTs, [alias(5120, [128, 256], BF16), alias(5632, [128, 256], BF16)]]
        cA2 = [colA, [C([128, 8]) for _ in range(2)]]

        def attn_A(c, q, blk, nxt):
            par = blk % 2
            g = c // 4
            cs = slice(blk * BLK, (blk + 1) * BLK)
            mk = self.amask.ap[:, 1 if (ti == 0 and blk == 0) else 0, :]
            rows = [slice(half * 64, half * 64 + 64) for half in H2]
            Sm_, Pf_, Pn_, cA_ = Sm2[par], Pf2[par], Pn2[par], cA2[par]
            for half in H2:
                pS = ps[half]
                P.mm(pS.ap[:, 0:256], [(q.ap[rows[half], cs], kTd.ap[rows[half], g, blk * BLK:blk * BLK + 256])], [q.b, kTd.b], [pS.b])
            if nxt is not None and blk < 3:
                q_part(nxt, blk)
            for half in H2:
                P.op("dve", "tensor_tensor", [ps[half].b, self.amask.b], [Sm_[half].b], out=Sm_[half].ap, in0=ps[half].ap[:, 0:256], in1=mk, op=ALU.add)
            for half in H2:
                P.op("dve", "tensor_reduce", [Sm_[half].b], [cA_[half].b], out=cA_[half].ap[:, 0:1], in_=Sm_[half].ap, axis=AX.X, op=ALU.max)
            for half in H2:
                h = 2 * c + half
                P.op("dve", "tensor_scalar", [cA_[half].b, self.nsinks.b], [cA_[half].b], out=cA_[half].ap[:, 1:2], in0=cA_[half].ap[:, 0:1],
                     scalar1=-0.125, scalar2=self.nsinks.ap[:, h:h + 1], op0=ALU.mult, op1=ALU.min)
            for half in H2:
                P.op("act", "activation", [Sm_[half].b, cA_[half].b], [Pf_[half].b, cA_[half].b], out=Pf_[half].ap, in_=Sm_[half].ap, func=AF.Exp,
                     scale=0.125, bias=cA_[half].ap[:, 1:2], accum_out=cA_[half].ap[:, 2:3])
            for half in H2:
                h = 2 * c + half
                P.op("act", "activation", [cA_[half].b, self.sinks.b], [cA_[half].b], out=cA_[half].ap[:, 3:4], in_=cA_[half].ap[:, 1:2],
                     func=AF.Exp, bias=self.sinks.ap[:, h:h + 1])
            for half in H2:
                cA = cA_[half]
                P.op("dve", "tensor_tensor", [cA.b], [cA.b], out=cA.ap[:, 4:5], in0=cA.ap[:, 2:3], in1=cA.ap[:, 3:4], op=ALU.add)
                P.op("dve", "reciprocal", [cA.b], [cA.b], out=cA.ap[:, 5:6], in_=cA.ap[:, 4:5])
                P.op("dve", "tensor_scalar", [Pf_[half].b, cA.b], [Pn_[half].b], out=Pn_[half].ap, in0=Pf_[half].ap, scalar1=cA.ap[:, 5:6], scalar2=None,
                     op0=ALU.mult)

        def attn_B(c, blk):
            par = blk % 2
            g = c // 4
            cs = slice(blk * BLK, (blk + 1) * BLK)
            rows = [slice(half * 64, half * 64 + 64) for half in H2]
            Pn_, PT_ = Pn2[par], PT2[par]
            pO = ps[5]
            for half in H2:
                pT = ps[3 + half]
                for j in range(2):
                    P.op("pe", "matmul", [Pn_[half].b, self.identb.b], [pT.b], out=pT.ap[:, j * 128:(j + 1) * 128],
                         lhsT=Pn_[half].ap[:, j * 128:(j + 1) * 128], rhs=self.identb.ap, start=True, stop=True)
            for half in H2:
                P.op("act", "activation", [ps[3 + half].b], [PT_[half].b], out=PT_[half].ap, in_=ps[3 + half].ap[:, 0:256], func=AF.Copy)
            for half in H2:
                pts = PT_[half]
                P.mm(pO.ap[rows[half], 0:128], [(vtok.ap[:, blk, g * 64:(g + 1) * 64], pts.ap[:, 0:128]),
                                                (vtok.ap[:, blk + 1, g * 64:(g + 1) * 64], pts.ap[:, 128:256])], [vtok.b, pts.b], [pO.b])
            P.op("act", "activation", [pO.b], [oT.b], out=oT.ap[:, c, cs], in_=pO.ap[:, 0:128], func=AF.Copy)

        for c in range(KC):
            q = qr[c % 2]
            nxt = q_start(c + 1) if c + 1 < KC else None
            attn_A(c, q, 0, nxt)
            for blk in range(nblk):
                if blk + 1 < nblk:
                    attn_A(c, q, blk + 1, nxt)
                attn_B(c, blk)
            if nxt is not None:
                q_finish(nxt)
        if astop <= 2:
            return
        if has_s:
            Ks = [C([128, 8, 128], BF16) for _ in range(2)]
            Vs = [C([128, 512], BF16) for _ in range(2)]
            STs = C([128, 64]); PnS = C([64, 128], BF16); PfS = C([64, 128]); PTS = C([128, 64], BF16); cS = C([64, 8])
            pO2 = ps[6]
            opad = C([64, 128])
            sa_stop = self.dbg.get("sa_stop", 99)
            for j in range(NSMP):
                K_, V_ = Ks[j % 2], Vs[j % 2]
                P.dma("pool", "d2", [], [K_.b], out=K_.ap[:, :, 0:127], in_=d["ckT"][j])
                P.op("dve", "tensor_copy", [kTd.b], [K_.b], out=K_.ap[:, :, 127], in_=kTd.ap[:, :, 128 + npc + j])
                P.dma("pool", "d3", [], [V_.b], out=V_.ap[0:127, :], in_=d["cv_old"][j])
                P.dma("pool", "d4", [self.vnew_b], [V_.b], out=V_.ap[127:128, :], in_=self.dout["vnew"][j:j + 1, :])
                V_.b.w["d3"] = P.cnt["d3"]
                if sa_stop <= 1:
                    continue
                for g in range(8):
                    for half in range(2):
                        rows = slice(half * 64, half * 64 + 64)
                        pSh = ps[half]
                        P.mm(pSh.ap[:, 4 * g:4 * g + 4], [(K_.ap[rows, g, :], qTs.ap[rows, 4 * g:4 * g + 4, j])], [K_.b, qTs.b], [pSh.b])
                ST3 = STs.ap.rearrange("p (g e) -> p g e", e=8)
                for half in range(2):
                    P.op("dve", "tensor_copy", [ps[half].b], [STs.b], out=ST3[:, :, 4 * half:4 * half + 4],
                         in_=ps[half].ap[:, 0:32].rearrange("p (g k) -> p g k", k=4))
                if sa_stop <= 2:
                    continue
                pS2 = ps[3]
                P.op("pe", "matmul", [STs.b, self.ident.b], [pS2.b], out=pS2.ap[0:64, 0:128], lhsT=STs.ap, rhs=self.ident.ap, start=True, stop=True)
                P.op("dve", "tensor_reduce", [pS2.b], [cS.b], out=cS.ap[:, 0:1], in_=pS2.ap[0:64, 0:128], axis=AX.X, op=ALU.max)
                P.op("dve", "tensor_scalar", [cS.b, self.sinksT.b], [cS.b], out=cS.ap[:, 1:2], in0=cS.ap[:, 0:1], scalar1=-0.125,
                     scalar2=self.sinksT.ap[:, 1:2], op0=ALU.mult, op1=ALU.min)
                P.op("act", "activation", [pS2.b, cS.b], [PfS.b, cS.b], out=PfS.ap, in_=pS2.ap[0:64, 0:128], func=AF.Exp, scale=0.125,
                     bias=cS.ap[:, 1:2], accum_out=cS.ap[:, 2:3])
                P.op("act", "activation", [cS.b, self.sinksT.b], [cS.b], out=cS.ap[:, 3:4], in_=cS.ap[:, 1:2], func=AF.Exp, bias=self.sinksT.ap[:, 0:1])
                P.op("dve", "tensor_tensor", [cS.b], [cS.b], out=cS.ap[:, 4:5], in0=cS.ap[:, 2:3], in1=cS.ap[:, 3:4], op=ALU.add)
                P.op("dve", "reciprocal", [cS.b], [cS.b], out=cS.ap[:, 5:6], in_=cS.ap[:, 4:5])
                P.op("dve", "tensor_scalar", [PfS.b, cS.b], [PnS.b], out=PnS.ap, in0=PfS.ap, scalar1=cS.ap[:, 5:6], scalar2=None, op0=ALU.mult)
                pP = ps[4]
                P.op("pe", "matmul", [PnS.b, self.identb.b], [pP.b], out=pP.ap[:, 0:64], lhsT=PnS.ap, rhs=self.identb.ap[0:64, 0:64], start=True, stop=True)
                P.op("act", "activation", [pP.b], [PTS.b], out=PTS.ap, in_=pP.ap[:, 0:64], func=AF.Copy)
                if sa_stop <= 3:
                    continue
                pO1 = ps[5]
                P.mm(pO1.ap[0:64, 0:512], [(PTS.ap, V_.ap)], [PTS.b, V_.b], [pO1.b])
                for hf in range(2):
                    osl = opad.ap[:, hf * 64:(hf + 1) * 64]
                    P.op("dve", "tensor_scalar", [pO1.b, self.mg.b], [opad.b], out=osl, in0=pO1.ap[0:64, 0:64], scalar1=self.mg.ap[:, hf * 8:hf * 8 + 1],
                         scalar2=None, op0=ALU.mult)
                    for g in range(1, 8):
                        P.op("dve", "scalar_tensor_tensor", [pO1.b, self.mg.b, opad.b], [opad.b], out=osl, in0=pO1.ap[0:64, g * 64:(g + 1) * 64],
                             scalar=self.mg.ap[:, hf * 8 + g:hf * 8 + g + 1], in1=osl, op0=ALU.mult, op1=ALU.add)
                if sa_stop <= 4:
                    continue
                P.op("pe", "matmul", [opad.b, self.ident.b], [pO2.b], out=pO2.ap[:, j * 64:(j + 1) * 64], lhsT=opad.ap, rhs=self.ident.ap[0:64, 0:64],
                     start=True, stop=True)
            if sa_stop > 5:
                tmpO = C([128, 256])
                P.op("act", "activation", [pO2.b], [tmpO.b], out=tmpO.ap, in_=pO2.ap[:, 0:256], func=AF.Copy)
                for j in range(NSMP):
                    for half in range(2):
                        rows = slice(half * 64, half * 64 + 64)
                        P.op("dve", "tensor_copy", [tmpO.b], [oT.b], out=oT.ap[rows, :, npc + j].rearrange("p (g k) -> p g k", k=4),
                             in_=tmpO.ap[rows, j * 64:(j + 1) * 64].rearrange("p (g e) -> p g e", e=8)[:, :, 4 * half:4 * half + 4])
        if astop <= 3:
            return
        for c in range(KC):
            bank = self.proj(d["w_o_s"][c], KC, oT, ncols)
            self.xupdate(bank, c, 8, ncols, npc, has_s, bias=self.boT)
        P.op("dve", "tensor_copy", [kTd.b], [kTd.b], out=kTd.ap[:, :, 0:128], in_=kTd.ap[:, :, npc:npc + 128])
        P.op("dve", "tensor_copy", [vtok.b], [vtok.b], out=vtok.ap[:, 0, :], in_=vtok.ap[:, nblk, :])


NORMS = [(0, 1, 0), (1, 4, 3), (2, 13, 12), (3, 7, 6), (4, 10, 9)]


def _slab(W):
    K, N = W.shape
    return np.ascontiguousarray(W.reshape(K // 128, 128, N // 128, 128).transpose(2, 1, 0, 3))


def _fm(x):
    T_ = x.shape[0]
    return np.ascontiguousarray(x.T.reshape(KC, 128, T_).transpose(1, 0, 2))


def _vecT(v):
    lead = v.shape[:-1]
    a = v.reshape(lead + (KC, 128))
    return np.ascontiguousarray(np.moveaxis(a, -1, 0))


_SL_COL = np.empty((NSL, 128), np.int64)
for _sl in range(NSL):
    _SL_COL[_sl] = (_sl % 2) * FF + (_sl // 2) * 128 + np.arange(128)


def _rope_tables(pos):
    half = 32
    freq = (np.float32(10000.0) ** (-np.arange(half, dtype=np.float32) / np.float32(half))).astype(np.float32)
    ang = (pos.astype(np.float32)[None, :] * freq[:, None]).astype(np.float32)
    cos, sin = np.cos(ang).astype(np.float32), np.sin(ang).astype(np.float32)
    p = np.arange(128)
    f = (p % 64) % 32
    sign = np.where((p % 64) < 32, -1.0, 1.0).astype(np.float32)
    return cos[f], sin[f] * sign[:, None]


def _shared_inputs(I, names=None):
    f = np.float32
    S = {}

    def want(n):
        return names is None or n in names
    if want("w_ada_s"):
        S["w_ada_s"] = np.concatenate([_slab(I["w_ada"][0]), _slab(I["w_ada"][1]), _slab(I["w_ada_kv"])], 0)
    if want("w_min_s") or want("w_gate"):
        wmi = I["w_m_in"][0]
        S["w_min_s"] = _slab(wmi[:, :12288])
        S["w_gate"] = np.ascontiguousarray(wmi[:, 12288:].reshape(KC, 128, 16).transpose(1, 0, 2))
    if want("w_mout_s"):
        S["w_mout_s"] = np.ascontiguousarray(I["w_m_out"][0].reshape(2, 16, 128, 32, 128).transpose(0, 3, 2, 1, 4))
    if want("w_ffi_s"):
        order = np.empty(NSL, np.int64)
        order[0::2] = np.arange(FC)
        order[1::2] = FC + np.arange(FC)
        S["w_ffi_s"] = np.stack([_slab(I["w_ffn_in"][l])[order] for l in range(2)])
    if want("w_ffo_s"):
        ffo = np.zeros((2, 4, 32, 128, 22, 128), f)
        for l in range(2):
            W = I["w_ffn_out"][l].reshape(FC, 128, 32, 128)
            for g, (f0, nf) in enumerate(FGRP):
                ffo[l, g, :, :, :nf, :] = W[f0:f0 + nf].transpose(2, 1, 0, 3)
        S["w_ffo_s"] = ffo
    if want("w_kv_s"):
        wk = I["w_kv"][:, :512].reshape(KC, 128, 8, 64).transpose(2, 1, 0, 3)
        S["w_kv_s"] = np.ascontiguousarray(np.concatenate([np.concatenate([wk, wk], -1), _slab(I["w_kv"][:, 512:])], 0))
    if want("w_q_s"):
        S["w_q_s"] = _slab(I["w_q"][0])
    if want("w_o_s"):
        S["w_o_s"] = _slab(I["w_o"][0])
    S["gT"] = _vecT(np.stack([I["g_norm1"][0], I["g_norm2"][0], I["g_kv"], I["g_norm1"][1], I["g_norm2"][1], I["g_final"]]))
    S["gheadT"] = _vecT(I["g_m_head"][0])
    S["bif"] = np.ascontiguousarray(np.stack([I["b_m_i"][0], I["b_m_f"][0]], 1))
    bk = I["b_kv"][:512].reshape(8, 64)
    S["bkv"] = np.ascontiguousarray(np.concatenate([np.concatenate([bk, bk], 1).T, I["b_kv"][512:].reshape(4, 128).T], 1))
    S["bqT"] = _vecT(I["b_q"][0]); S["boT"] = _vecT(I["b_o"][0])
    sk = I["sinks"][0]
    S["sinks_bc"] = np.ascontiguousarray(np.broadcast_to(sk[None, :], (128, 64)))
    S["nsinks_bc"] = np.ascontiguousarray(np.broadcast_to(np.negative(sk)[None, :], (128, 64)))
    hp = np.arange(64)
    hh = 8 * (hp // 8) + 2 * (hp % 4) + ((hp % 8) // 4)
    S["sinksT"] = np.ascontiguousarray(np.stack([sk[hh], np.negative(sk[hh])], 1))
    S["wconvT"] = np.ascontiguousarray(I["w_conv"][:, :, _SL_COL].transpose(3, 0, 2, 1))
    S["bconvT"] = np.ascontiguousarray(I["b_conv"][:, _SL_COL].transpose(2, 0, 1))
    S["ident"] = np.eye(128, dtype=f)
    m = np.arange(128)
    partner = np.where((m % 64) < 32, m + 32, m - 32)
    R = np.zeros((128, 128), f)
    R[partner, m] = 1.0
    S["Rmat"] = R
    S["cmask"] = (m[:, None] <= m[None, :]).astype(f)
    t = np.arange(128)[:, None]
    j = np.arange(256)[None, :]
    valid = np.where(j < 128, j > t, (j - 128) <= t)
    am = np.where(valid, 0.0, -30000.0).astype(f)
    am1 = am.copy()
    am1[:, :128] = -30000.0
    S["amask"] = np.stack([am, am1])
    bm = np.zeros((8, 8, 4), f)
    for k in range(8):
        bm[k, k, :] = 1.0
    S["bm"] = bm
    hp_ = np.arange(64)
    mg = np.zeros((64, 16), f)
    mg[hp_, ((hp_ % 8) // 4) * 8 + hp_ // 8] = 1.0
    S["mg"] = mg
    return {k: np.ascontiguousarray(v, dtype=f) for k, v in S.items()}


def _core_inputs(I, core):
    f = np.float32
    b, half = core // 2, core % 2
    xp = I["x_prompt"][b]
    s0 = 0 if half == 0 else 896
    sq = slice(4 * core, 4 * core + 4)
    M = {}
    M["xpre"] = _fm(xp[0:NPRE]); M["xfull"] = _fm(xp[s0:s0 + NFULL]); M["xs"] = _fm(I["x_sample"][sq, 0])
    M["c5T"] = _fm(np.concatenate([I["c_prompt"][b:b + 1], I["c_sample"][sq]], 0))
    M["flag"] = np.full((128, 1), float(half), f)
    rp = np.zeros((3, 128, 2, NCMAX), f)
    for ti in range(3):
        pos = np.concatenate([s0 + ti * 384 + np.arange(384), np.full(4, 16384)])
        c_, s_ = _rope_tables(pos)
        rp[ti, :, 0], rp[ti, :, 1] = c_, s_
    M["rope"] = rp
    M["st_C"] = I["state_mlstm_C"][0, sq].reshape(4, H, 2, 128, 512).transpose(0, 1, 3, 2, 4)
    M["st_n"] = I["state_mlstm_n"][0, sq].reshape(4, H, 2, 128).transpose(3, 1, 2, 0)
    M["st_m"] = I["state_mlstm_m"][0, sq].T
    cc = I["cache_conv"][:, sq]
    M["cconv"] = cc[:, :, :, _SL_COL].transpose(0, 4, 3, 1, 2)
    M["cconv_r1"] = cc[:, :, 1, :]
    ck = I["cache_k_win"][sq, 1:]
    ckt = ck.transpose(0, 3, 2, 1)
    M["ckT"] = np.concatenate([ckt, ckt], 1)
    M["ck_old"] = ck.reshape(4, 127, 512)
    M["cv_old"] = I["cache_v_win"][sq, 1:].reshape(4, 127, 512)
    return {k: np.ascontiguousarray(v, dtype=f) for k, v in M.items()}


_NC_CACHE = []


def kernel(**inputs):
    I = {k: np.asarray(v) for k, v in inputs.items()}
    if not _NC_CACHE:
        _NC_CACHE.append(Builder().build())
    nc = _NC_CACHE[0]
    shared = _shared_inputs(I)
    in_maps = []
    for core in range(8):
        m = dict(shared)
        m.update(_core_inputs(I, core))
        in_maps.append(m)
    res = run_bass_kernel_spmd(nc, in_maps, core_ids=list(range(8))).results
    f = np.float32
    y_prompt = np.zeros((4, 2048, D), f); y_sample = np.zeros((32, 1, D), f)
    C_p = np.zeros((1, 4, H, 256, 512), f); n_p = np.zeros((1, 4, H, 256), f); m_p = np.zeros((1, 4, H), f)
    conv_p = np.zeros((2, 4, 2, 2 * FF), f); kw_p = np.zeros((4, 128, 8, 64), f); vw_p = np.zeros((4, 128, 8, 64), f)
    C_s = np.zeros((1, 32, H, 256, 512), f); n_s = np.zeros((1, 32, H, 256), f); m_s = np.zeros((1, 32, H), f)
    conv_s = np.zeros((2, 32, 2, 2 * FF), f); kw_s = np.zeros((32, 128, 8, 64), f); vw_s = np.zeros((32, 128, 8, 64), f)
    for core in range(8):
        r = res[core]
        b, half = core // 2, core % 2
        sq = slice(4 * core, 4 * core + 4)
        yt = r["yT"].transpose(2, 1, 0).reshape(NFULL, D)
        if half == 0:
            y_prompt[b, 0:NFULL] = yt
        else:
            y_prompt[b, NFULL:2048] = yt[256:]
            C_p[0, b] = r["C_out"].transpose(0, 2, 1, 3).reshape(H, 256, 512)
            n_p[0, b] = r["n_out"].transpose(1, 2, 0).reshape(H, 256)
            m_p[0, b] = r["m_out"][:, 0]
            cp = r["convp"]
            for l in range(2):
                conv_p[l, b][:, _SL_COL] = cp[l].transpose(2, 1, 0)
            kw_p[b] = r["kwinp"].transpose(2, 1, 0)
            vw_p[b] = r["vwinp"].transpose(2, 1, 0).reshape(128, 4, 2, 64).reshape(128, 8, 64)
        y_sample[sq, 0] = r["ysT"].transpose(2, 1, 0).reshape(4, D)
        C_s[0, sq] = r["Cs_out"].transpose(0, 1, 3, 2, 4).reshape(4, H, 256, 512)
        n_s[0, sq] = r["ns_out"].transpose(3, 1, 2, 0).reshape(4, H, 256)
        m_s[0, sq] = r["ms_out"].T
        conv_s[:, sq, 0] = r["convs_old"]
        cn = r["convs_new"]
        for l in range(2):
            conv_s[l, sq, 1][:, _SL_COL] = cn[l].transpose(2, 1, 0)
        kw_s[sq, 0:127] = r["kwins_old"].reshape(4, 127, 8, 64)
        kw_s[sq, 127] = r["knew"].transpose(2, 1, 0)
        vw_s[sq, 0:127] = r["vwins_old"].reshape(4, 127, 8, 64)
        vw_s[sq, 127] = r["vnew"].reshape(4, 8, 64)
    return (y_prompt, y_sample, C_p, n_p, m_p, conv_p, kw_p, vw_p, C_s, n_s, m_s, conv_s, kw_s, vw_s)
```
